# Optimizing a Trainium2 kernel written in Bass

```python
import math
import jax, jax.numpy as jnp
from jax import lax
import numpy as np

D_MODEL = 2048
BATCH = 2
SEQ = 8192
DEPTH = 1

D_CONV = 1024
CONV_GROUPS = 16
CONV_WIDTH = 3
N_HEADS = 8
QK_NOPE = 128
QK_ROPE = 64
QK_HEAD = QK_NOPE + QK_ROPE
V_HEAD = 128
D_ATTN = N_HEADS * V_HEAD
Q_LORA = 512
KV_LORA = 256
ROPE_BASE = 10000.0
Q_BLOCK = 128
D_MIX = D_CONV + D_ATTN
IN_COLS = 4 * D_CONV + Q_LORA + KV_LORA + QK_ROPE + D_ATTN
EPS = 1e-6

kernel_name = "hymba_conv_mla_adaln_layer"


def _rmsnorm(x, g):
    x32 = x.astype(jnp.float32)
    y = x32 * lax.rsqrt(jnp.mean(x32 * x32, axis=-1, keepdims=True) + EPS)
    return (y * g.astype(jnp.float32)).astype(x.dtype)


def _rope_tables(positions):
    inv_freq = ROPE_BASE ** (-jnp.arange(0, QK_ROPE, 2, dtype=jnp.float32) / QK_ROPE)
    ang = positions.astype(jnp.float32)[..., None] * inv_freq
    return jnp.cos(ang), jnp.sin(ang)


def _apply_rope(x, cos, sin):
    half = QK_ROPE // 2
    x32 = x.astype(jnp.float32)
    x1, x2 = x32[..., :half], x32[..., half:]
    out = jnp.concatenate([x1 * cos - x2 * sin, x1 * sin + x2 * cos], axis=-1)
    return out.astype(x.dtype)


def _short_conv_branch(x_c, b_c, c_c, z_c, conv_w):
    u = c_c * x_c
    seq = u.shape[1]
    u_pad = jnp.pad(u, ((0, 0), (CONV_WIDTH - 1, 0), (0, 0)))
    conv = sum(conv_w[k] * u_pad[:, k:k + seq, :] for k in range(CONV_WIDTH))
    y = b_c * conv
    return y * jax.nn.silu(z_c)


def _causal_blocked_attention(q, k, v):
    bsz, seq = q.shape[0], q.shape[1]
    n_blk = seq // Q_BLOCK
    scale = 1.0 / math.sqrt(QK_HEAD)
    q_blocks = q.reshape(bsz, n_blk, Q_BLOCK, N_HEADS, QK_HEAD).transpose(1, 0, 2, 3, 4)
    key_idx = jnp.arange(seq, dtype=jnp.int32)

    def one_block(args):
        qb, blk = args
        q_idx = blk * Q_BLOCK + jnp.arange(Q_BLOCK, dtype=jnp.int32)
        s = jnp.einsum('bqhd,bkhd->bhqk', qb, k).astype(jnp.float32) * scale
        mask = key_idx[None, :] <= q_idx[:, None]
        s = jnp.where(mask[None, None], s, -jnp.inf)
        p = jax.nn.softmax(s, axis=-1).astype(v.dtype)
        return jnp.einsum('bhqk,bkhd->bqhd', p, v)

    out = lax.map(one_block, (q_blocks, jnp.arange(n_blk, dtype=jnp.int32)))
    return out.transpose(1, 0, 2, 3, 4).reshape(bsz, seq, N_HEADS, V_HEAD)


def _mla_branch(c_q, c_kv, k_rope, z_a, cos, sin, q_a_g, w_q_b, kv_a_g, w_kv_b, q_g, k_g):
    bsz, seq = c_q.shape[0], c_q.shape[1]
    q = (_rmsnorm(c_q, q_a_g) @ w_q_b).reshape(bsz, seq, N_HEADS, QK_HEAD)
    kv = (_rmsnorm(c_kv, kv_a_g) @ w_kv_b).reshape(bsz, seq, N_HEADS, QK_NOPE + V_HEAD)
    k_nope, v = kv[..., :QK_NOPE], kv[..., QK_NOPE:]
    k = jnp.concatenate([k_nope, jnp.broadcast_to(k_rope[:, :, None, :], (bsz, seq, N_HEADS, QK_ROPE))], axis=-1)
    q = _rmsnorm(q, q_g)
    k = _rmsnorm(k, k_g)
    cos_h, sin_h = cos[:, :, None, :], sin[:, :, None, :]
    q = jnp.concatenate([q[..., :QK_NOPE], _apply_rope(q[..., QK_NOPE:], cos_h, sin_h)], axis=-1)
    k = jnp.concatenate([k[..., :QK_NOPE], _apply_rope(k[..., QK_NOPE:], cos_h, sin_h)], axis=-1)
    o = _causal_blocked_attention(q, k, v).reshape(bsz, seq, D_ATTN)
    return o * jax.nn.silu(z_a)


def _layer(x, c, cos, sin, ada_w, ada_b, norm_g, w_in, conv_w, q_a_g, w_q_b, kv_a_g, w_kv_b, q_g, k_g, w_out):
    mod = jax.nn.silu(c) @ ada_w + ada_b
    shift, scale, gate = jnp.split(mod, 3, axis=-1)
    h = _rmsnorm(x, norm_g) * (1.0 + scale[:, None, :]) + shift[:, None, :]
    u = h @ w_in
    splits = np.cumsum([D_CONV, D_CONV, D_CONV, D_CONV, Q_LORA, KV_LORA, QK_ROPE])
    x_c, b_c, c_c, z_c, c_q, c_kv, k_rope, z_a = jnp.split(u, splits.tolist(), axis=-1)
    y_conv = _short_conv_branch(x_c, b_c, c_c, z_c, conv_w)
    y_attn = _mla_branch(c_q, c_kv, k_rope, z_a, cos, sin, q_a_g, w_q_b, kv_a_g, w_kv_b, q_g, k_g)
    y = jnp.concatenate([y_conv, y_attn], axis=-1) @ w_out
    return x + gate[:, None, :] * y


def setup_inputs(seed: int = 0) -> dict:
    key = jax.random.key(seed)
    ks = jax.random.split(key, 20)
    f32 = jnp.float32

    def nrm(k, shape, fan_in, mult=1.0):
        return jax.random.normal(k, shape, f32) * (mult * fan_in ** -0.5)

    def gain(k, shape):
        return 1.0 + 0.02 * jax.random.normal(k, shape, f32)

    x = jax.random.normal(ks[0], (BATCH, SEQ, D_MODEL), f32)
    c = jax.random.normal(ks[1], (BATCH, D_MODEL), f32)
    positions = jnp.broadcast_to(jnp.arange(SEQ, dtype=jnp.int32), (BATCH, SEQ))
    return {
        "x": x,
        "c": c,
        "positions": positions,
        "ada_w": nrm(ks[2], (DEPTH, D_MODEL, 3 * D_MODEL), D_MODEL, 0.5),
        "ada_b": 0.01 * jax.random.normal(ks[3], (DEPTH, 3 * D_MODEL), f32),
        "norm_g": gain(ks[4], (DEPTH, D_MODEL)),
        "w_in": nrm(ks[5], (DEPTH, D_MODEL, IN_COLS), D_MODEL),
        "conv_w": nrm(ks[6], (DEPTH, CONV_WIDTH, D_CONV), CONV_WIDTH),
        "q_a_g": gain(ks[7], (DEPTH, Q_LORA)),
        "w_q_b": nrm(ks[8], (DEPTH, Q_LORA, N_HEADS * QK_HEAD), Q_LORA),
        "kv_a_g": gain(ks[9], (DEPTH, KV_LORA)),
        "w_kv_b": nrm(ks[10], (DEPTH, KV_LORA, N_HEADS * (QK_NOPE + V_HEAD)), KV_LORA),
        "q_g": gain(ks[11], (DEPTH, QK_HEAD)),
        "k_g": gain(ks[12], (DEPTH, QK_HEAD)),
        "w_out": nrm(ks[13], (DEPTH, D_MIX, D_MODEL), D_MIX),
    }


def reference(x, c, positions, ada_w, ada_b, norm_g, w_in, conv_w, q_a_g, w_q_b, kv_a_g, w_kv_b, q_g, k_g, w_out):
    cos, sin = _rope_tables(positions)
    for l in range(DEPTH):
        x = _layer(x, c, cos, sin, ada_w[l], ada_b[l], norm_g[l], w_in[l], conv_w[l],
                   q_a_g[l], w_q_b[l], kv_a_g[l], w_kv_b[l], q_g[l], k_g[l], w_out[l])
    return x
```

```python
import math
from contextlib import ExitStack

import numpy as np
import concourse.bass as bass
import concourse.mybir as mybir
from concourse.bass_utils import run_bass_kernel_spmd

F32 = mybir.dt.float32
BF16 = mybir.dt.bfloat16
I32 = mybir.dt.int32
AF = mybir.ActivationFunctionType
ALU = mybir.AluOpType

D = 2048
KC = D // 128
NCORES = 8
EPS = 1e-6
TWO_PI = 2.0 * math.pi
CW1 = 6.28125
CW2 = TWO_PI - 6.28125
SHRINK = 1.0 - 2e-6
KSTOP = ""
DBG_HOOK = None


class Buf:
    __slots__ = ("name", "lw", "reads", "sem", "dcnt", "excl")

    def __init__(self, name, excl=False):
        self.name = name
        self.excl = excl
        self.lw = None
        self.reads = {}
        self.sem = None
        self.dcnt = 0


def bufs(name, *dims):
    if not dims:
        return Buf(name)
    return [bufs(f"{name}_{i}", *dims[1:]) for i in range(dims[0])]


class Sched:
    def __init__(self, nc):
        self.nc = nc
        self.eng = {"pe": nc.tensor, "act": nc.scalar, "dve": nc.vector, "pool": nc.gpsimd, "sp": nc.sync}
        self.sem = {k: nc.alloc_semaphore(name="es_" + k) for k in self.eng}
        self.cnt = {k: 0 for k in self.eng}
        self.seen = {k: {} for k in self.eng}
        self.dma_bufs = []
        self.free_sems = []
        self.off = False

    def _wait(self, e, deps):
        need = {}
        for d in deps:
            if d is None:
                continue
            k, v = d
            if k == e and e == "pe":
                continue
            if need.get(k, 0) < v:
                need[k] = v
        seen = self.seen[e]
        for k, v in need.items():
            if isinstance(k, Buf):
                v = k.dcnt
                so = k.sem
            else:
                so = self.sem[k]
            if seen.get(k, 0) >= v:
                continue
            seen[k] = v
            self.eng[e].wait_ge(so, v)

    @staticmethod
    def _deps(reads, writes):
        deps = []
        for b in reads:
            deps.append(b.lw)
        for b in writes:
            deps.append(b.lw)
            deps.extend(b.reads.items())
        return deps

    def op(self, e, fn, reads=(), writes=()):
        if self.off:
            return None
        if any(b.excl for b in reads):
            writes = list(writes) + [b for b in reads if b.excl]
            reads = [b for b in reads if not b.excl]
        self._wait(e, self._deps(reads, writes))
        ins = fn(self.eng[e])
        self.cnt[e] += 1
        ins.then_inc(self.sem[e], 1)
        v = self.cnt[e]
        for b in reads:
            if b.reads.get(e, 0) < v:
                b.reads[e] = v
        for b in writes:
            b.lw = (e, v)
            b.reads = {}
        return ins

    def dma(self, q, out, in_, reads=(), writes=(), sembuf=None):
        if self.off:
            return None
        self._wait(q, self._deps(reads, writes))
        if sembuf is None:
            sembuf = writes[0] if writes else reads[0]
        if sembuf.sem is None:
            sembuf.sem = self.nc.alloc_semaphore(name=f"ds{len(self.dma_bufs)}_" + sembuf.name)
            self.dma_bufs.append(sembuf)
        ins = self.eng[q].dma_start(out=out, in_=in_)
        sembuf.dcnt += 16
        ins.then_inc(sembuf.sem, 16)
        for b in reads:
            b.reads[sembuf] = sembuf.dcnt
        for b in writes:
            b.lw = (sembuf, sembuf.dcnt)
            b.reads = {}
        return ins

    def barrier(self):
        if self.off:
            return
        deps = [(k, c) for k, c in self.cnt.items() if k != "sp" and c > 0]
        deps += [(b, b.dcnt) for b in self.dma_bufs]
        self._wait("sp", deps)
        self.eng["sp"].sem_inc(self.sem["sp"], 1)
        self.cnt["sp"] += 1
        for e in self.eng:
            if e != "sp":
                self._wait(e, [("sp", self.cnt["sp"])])
                for k, v in self.seen["sp"].items():
                    if self.seen[e].get(k, 0) < v:
                        self.seen[e][k] = v


class _Stop(Exception):
    pass


class Ring:
    def __init__(self, items):
        self.items = list(items)
        self.i = 0

    def next(self):
        it = self.items[self.i % len(self.items)]
        self.i += 1
        return it


def build(SEQ):
    NT = SEQ // 512
    NSLOT = NT // 4
    NHALF = NSLOT // 2
    NBLK = SEQ // 128
    NOWN = NSLOT * 512
    NCC = 47

    nc = bass.Bass("TRN2", target_bir_lowering=False)

    def din(name, shape, dt=F32):
        return nc.dram_tensor(name, list(shape), dt, kind="ExternalInput").ap()

    x_all = din("x_all", [SEQ, D])
    x_own = din("x_own", [NOWN, D])
    x_halo = din("x_halo", [NSLOT * 2, D])
    halo_valid = din("halo_valid", [128, NSLOT * 2])
    pos_all = din("pos_all", [1, SEQ], I32)
    pos_own = din("pos_own", [1, NOWN], I32)
    qidx = din("qidx", [1, NOWN])
    kidx_d = din("kidx", [128, NBLK])
    c_col_d = din("c_col", [128, KC])
    ada_w_l = din("ada_w_l", [12, 128, KC, 512])
    ada_b_d = din("ada_b", [1, 6144])
    ng_col_d = din("ng_col", [128, KC])
    w_in_l = din("w_in_l", [NCC, 128, KC, 128])
    convw_d = din("convw_col", [128, 8, 3])
    qag_d = din("qag_col", [128, 4])
    kvag_d = din("kvag_col", [128, 2])
    wqb_l = din("wqb_l", [8, 128, 4, 256])
    wkvb_l = din("wkvb_l", [8, 128, 2, 256])
    qg_d = din("qg_col", [128, 3])
    kg_d = din("kg_col", [128, 3])
    w_out_l = din("w_out_l", [4, 128, KC, 512])
    rope_c_d = din("rope_c", [64, 4])
    out_d = nc.dram_tensor("out", [NOWN, D], F32, kind="ExternalOutput").ap()
    DBG = DBG_HOOK is not None
    if DBG:
        dbg_yc = nc.dram_tensor("dbg_yc", [NHALF, 128, 8, 1024], BF16, kind="ExternalOutput").ap()
        dbg_ya = nc.dram_tensor("dbg_ya", [NHALF, 128, 8, 1024], BF16, kind="ExternalOutput").ap()
        dbg_qn = nc.dram_tensor("dbg_qn", [128, 1024], BF16, kind="ExternalOutput").ap()
        dbg_qr = nc.dram_tensor("dbg_qr", [64, 1024], BF16, kind="ExternalOutput").ap()
        dbg_k = nc.dram_tensor("dbg_k", [128, 2048], BF16, kind="ExternalOutput").ap()
        dbg_kr = nc.dram_tensor("dbg_kr", [64, 2048], BF16, kind="ExternalOutput").ap()
        dbg_v = nc.dram_tensor("dbg_v", [128, 16, 128], BF16, kind="ExternalOutput").ap()
        dbg_rk = nc.dram_tensor("dbg_rk", [128, 16], F32, kind="ExternalOutput").ap()
        dbg_acc = nc.dram_tensor("dbg_acc", [128, 512], F32, kind="ExternalOutput").ap()
        dbg_ssb = nc.dram_tensor("dbg_ssb", [128, 512], F32, kind="ExternalOutput").ap()
        dbg_ssb2 = nc.dram_tensor("dbg_ssb2", [128, 512], F32, kind="ExternalOutput").ap()
        dbg_cs = nc.dram_tensor("dbg_cs", [64, 2048], F32, kind="ExternalOutput").ap()

    S = Sched(nc)
    op = S.op

    def mm(out, lhsT, rhs, start, stop, reads, writes):
        return op("pe", lambda e: e.matmul(out, lhsT=lhsT, rhs=rhs, start=start, stop=stop), reads, writes)

    def act(out, in_, func, reads, writes, scale=1.0, bias=0.0, accum_out=None):
        if accum_out is not None:
            return op("act", lambda e: e.activation(out=out, in_=in_, func=func, bias=bias, scale=scale,
                                                    accum_out=accum_out), reads, writes)
        return op("act", lambda e: e.activation(out=out, in_=in_, func=func, bias=bias, scale=scale), reads, writes)

    def ts(eng, out, in0, s1, s2, op0, op1, reads, writes):
        if s2 is None:
            return op(eng, lambda e: e.tensor_scalar(out=out, in0=in0, scalar1=s1, scalar2=None, op0=op0), reads, writes)
        return op(eng, lambda e: e.tensor_scalar(out=out, in0=in0, scalar1=s1, scalar2=s2, op0=op0, op1=op1), reads, writes)

    def stt(eng, out, in0, scalar, in1, op0, op1, reads, writes):
        return op(eng, lambda e: e.scalar_tensor_tensor(out=out, in0=in0, scalar=scalar, in1=in1, op0=op0, op1=op1),
                  reads, writes)

    def tt(eng, out, in0, in1, o, reads, writes):
        return op(eng, lambda e: e.tensor_tensor(out=out, in0=in0, in1=in1, op=o), reads, writes)

    def cp(eng, out, in_, reads, writes):
        if eng == "act":
            return op(eng, lambda e: e.activation(out=out, in_=in_, func=AF.Copy), reads, writes)
        return op(eng, lambda e: e.tensor_copy(out=out, in_=in_), reads, writes)

    top = ExitStack()
    kstop = KSTOP

    def ck(name):
        if kstop == name:
            S.off = True

    uid = [0]

    def sb(es, name, shape, dt):
        uid[0] += 1
        return es.enter_context(nc.sbuf_tensor(f"{name}_u{uid[0]}", list(shape), dt))

    banks = [top.enter_context(nc.psum_tensor(f"pb{i}", [128, 512], F32)) for i in range(8)]
    bank_bufs = [Buf(f"pb{i}", excl=True) for i in range(8)]
    tp_tiles = [banks[6][:, :].bitcast(BF16), banks[7][:, :].bitcast(BF16)]
    tp_bufs = [bank_bufs[6], bank_bufs[7]]

    ckvnT = sb(top, "ckvnT", [128, 2, SEQ], BF16)
    krT = sb(top, "krT", [64, SEQ], BF16)
    ssr = sb(top, "ssr", [128, NBLK], F32)
    b_ckvn = bufs("ckvn", NT)
    b_kr = bufs("kr", NT)
    b_ssr = bufs("ssr", NT)
    ident = sb(top, "ident", [128, 128], BF16)
    ones_bf = sb(top, "ones_bf", [128, 128], BF16)
    ones_f = sb(top, "ones_f", [128, 128], F32)
    cst = sb(top, "cst", [128, 2], F32)
    b_const = Buf("const")
    gs_col = sb(top, "gs_col", [128, KC], F32)
    shift_col = sb(top, "shift_col", [128, KC], F32)
    gate_bc = sb(top, "gate_bc", [128, D], F32)
    b_mod = Buf("mod")
    b_gate = Buf("gate")
    c_col = sb(top, "c_col", [128, KC], F32)
    ng_col = sb(top, "ng_col", [128, KC], F32)
    convw = sb(top, "convw", [128, 8, 3], F32)
    qag = sb(top, "qag", [128, 4], F32)
    kvag = sb(top, "kvag", [128, 2], F32)
    qg = sb(top, "qg", [128, 3], F32)
    kg = sb(top, "kg", [128, 3], F32)
    rope_c = sb(top, "rope_c", [64, 4], F32)
    kidx = sb(top, "kidx", [128, NBLK], F32)
    hvalid = sb(top, "hvalid", [128, NSLOT * 2], F32)
    b_par = Buf("par")

    for dst, src in ((c_col, c_col_d), (ng_col, ng_col_d), (convw, convw_d), (qag, qag_d), (kvag, kvag_d),
                     (qg, qg_d), (kg, kg_d), (rope_c, rope_c_d), (kidx, kidx_d), (hvalid, halo_valid)):
        S.dma("sp", dst[:], src, writes=[b_par])

    try:
        with ExitStack() as es:
            identf = sb(es, "identf", [128, 128], F32)
            b_idf = Buf("identf")
            op("pool", lambda e: e.memset(identf[:, :], 0.0), writes=[b_idf])
            op("pool", lambda e: e.affine_select(out=identf[:, :], in_=identf[:, :], compare_op=ALU.not_equal, fill=1.0,
                                                 base=0, pattern=[[-1, 128]], channel_multiplier=1), writes=[b_idf])
            cp("dve", ident[:, :], identf[:, :], [b_idf], [b_const])
            op("pool", lambda e: e.memset(ones_f[:, :], 1.0), writes=[b_const])
            op("pool", lambda e: e.memset(cst[:, 0:1], EPS), writes=[b_const])
            op("pool", lambda e: e.memset(cst[:, 1:2], 192.0 * EPS), writes=[b_const])
            cp("dve", ones_bf[:, :], ones_f[:, :], [b_const], [b_const])

            sc_bf = sb(es, "sc_bf", [128, KC], BF16)
            b_sc = Buf("sc")
            act(sc_bf[:, :], c_col[:, :], AF.Silu, [b_par], [b_sc])
            modrow = sb(es, "modrow", [1, 6144], F32)
            adab = sb(es, "adab", [1, 6144], F32)
            b_adab = Buf("adab")
            S.dma("sp", adab[:], ada_b_d, writes=[b_adab])
            b_modrow = Buf("modrow")
            adaw = [sb(es, f"adaw{i}", [128, KC, 512], BF16) for i in range(2)]
            b_adaw = bufs("adaw", 2)
            for ct in range(12):
                wt, wb = adaw[ct % 2], b_adaw[ct % 2]
                S.dma("pool", wt[:], ada_w_l[ct], writes=[wb])
                bk, bb = banks[ct % 2], bank_bufs[ct % 2]
                for kc in range(KC):
                    mm(bk[0:1, :], sc_bf[:, kc:kc + 1], wt[:, kc, :], kc == 0, kc == KC - 1, [b_sc, wb], [bb])
                tt("dve", modrow[0:1, ct * 512:(ct + 1) * 512], bk[0:1, :], adab[0:1, ct * 512:(ct + 1) * 512], ALU.add,
                   [bb, b_adab], [b_modrow])
            cb, cbb = banks[2], bank_bufs[2]
            for c in range(32):
                mm(cb[:, c:c + 1], modrow[0:1, c * 128:(c + 1) * 128], ones_f[0:1, 0:1], True, True, [b_modrow, b_const], [cbb])
            cp("dve", shift_col[:, :], cb[:, 0:KC], [cbb], [b_mod])
            stt("dve", gs_col[:, :], cb[:, KC:2 * KC], 1.0, ng_col[:, :], ALU.add, ALU.mult, [cbb, b_par], [b_mod])
            for ct in range(4):
                bk, bb = banks[3 + ct % 2], bank_bufs[3 + ct % 2]
                mm(bk[:, :], ones_f[0:1, :], modrow[0:1, 4096 + ct * 512:4096 + (ct + 1) * 512], True, True,
                   [b_modrow, b_const], [bb])
                cp("act", gate_bc[:, ct * 512:(ct + 1) * 512], bk[:, :], [bb], [b_gate])
            S.barrier()
            ck("prologue")

        bank_ring = Ring(list(zip(banks[:6], bank_bufs[:6])))
        ring8 = Ring(list(zip(banks, bank_bufs)))

        def make_hT_builder(es, nxt=3):
            xt = [sb(es, f"xt{i}", [128, D], F32) for i in range(nxt)]
            b_xt = bufs("xt", nxt)
            xn = [sb(es, f"xn{i}", [128, D], BF16) for i in range(8)]
            b_xn = bufs("xn", 8)
            junk = sb(es, "junk", [128, D], BF16)
            ssq = [sb(es, f"ssq{i}", [128, 4], F32) for i in range(2)]
            lnq = [sb(es, f"lnq{i}", [128, 4], F32) for i in range(2)]
            rsq = [sb(es, f"rsq{i}", [128, 4], F32) for i in range(2)]
            b_ssq = bufs("ssq", 2)
            b_lnq = bufs("lnq", 2)
            b_rsq = bufs("rsq", 2)
            state = {"n": 0, "xi": 0}

            def prep(src, row0):
                p2 = state["n"] % 2
                state["n"] += 1
                op("pool", lambda e: e.memset(ssq[p2][:, :], 0.0), writes=[b_ssq[p2]])
                xis = []
                for st in range(4):
                    xi = state["xi"] % nxt
                    state["xi"] += 1
                    xis.append(xi)
                    S.dma("sp", xt[xi][:], src[row0 + st * 128:row0 + (st + 1) * 128, :], writes=[b_xt[xi]])
                    act(junk[:, :], xt[xi][:, :], AF.Square, [b_xt[xi]], [b_ssq[p2]], accum_out=ssq[p2][:, st:st + 1])
                    if nxt < 4 or st == 3:
                        pass
                act(lnq[p2][:, :], ssq[p2][:, :], AF.Ln, [b_ssq[p2]], [b_lnq[p2]], scale=1.0 / D, bias=cst[:, 0:1])
                act(rsq[p2][:, :], lnq[p2][:, :], AF.Exp, [b_lnq[p2]], [b_rsq[p2]], scale=-0.5)
                return p2, xis

            def prep_full(src, row0):
                p2 = state["n"] % 2
                state["n"] += 1
                for st in range(4):
                    xi = state["xi"] % nxt
                    state["xi"] += 1
                    ni = p2 * 4 + st
                    S.dma("sp", xt[xi][:], src[row0 + st * 128:row0 + (st + 1) * 128, :], writes=[b_xt[xi]])
                    op("pool", lambda e: e.memset(ssq[p2][:, st:st + 1], 0.0), writes=[b_ssq[p2]])
                    act(junk[:, :], xt[xi][:, :], AF.Square, [b_xt[xi]], [b_ssq[p2]], accum_out=ssq[p2][:, st:st + 1])
                    act(lnq[p2][:, st:st + 1], ssq[p2][:, st:st + 1], AF.Ln, [b_ssq[p2]], [b_lnq[p2]], scale=1.0 / D, bias=cst[:, 0:1])
                    act(rsq[p2][:, st:st + 1], lnq[p2][:, st:st + 1], AF.Exp, [b_lnq[p2]], [b_rsq[p2]], scale=-0.5)
                    ts("dve", xn[ni][:, :], xt[xi][:, :], rsq[p2][:, st:st + 1], None, ALU.mult,
                       None, [b_xt[xi], b_rsq[p2]], [b_xn[ni]])
                return p2

            def finish(p2, hT, b_hT, col0):
                for kc in range(KC):
                    h = kc % 2
                    tph = tp_tiles[h][:, 0:512]
                    for st in range(4):
                        ni = p2 * 4 + st
                        op("pe", lambda e: e.transpose(out=tph[:, st * 128:(st + 1) * 128],
                                                       in_=xn[ni][:, kc * 128:(kc + 1) * 128], identity=ident[:, :]),
                           [b_xn[ni], b_const], [tp_bufs[h]])
                    dst = hT[:, kc, col0:col0 + 512]
                    if kc % 4 == 0:
                        act(dst, tph, AF.Identity, [tp_bufs[h], b_mod], [b_hT[kc]], scale=gs_col[:, kc:kc + 1],
                            bias=shift_col[:, kc:kc + 1])
                    else:
                        ts("dve", dst, tph, gs_col[:, kc:kc + 1], shift_col[:, kc:kc + 1], ALU.mult, ALU.add,
                           [tp_bufs[h], b_mod], [b_hT[kc]])

            return prep_full, finish, junk

        def make_tables(es, N, tag):
            posi = sb(es, f"posi{tag}", [64, N], I32)
            posf = sb(es, f"posf{tag}", [64, N], F32)
            ang = sb(es, f"ang{tag}", [64, N], F32)
            kf = [sb(es, f"kf{tag}{i}", [64, N], F32) for i in range(2)]
            ki = [sb(es, f"ki{tag}{i}", [64, N], I32) for i in range(2)]
            rr = [sb(es, f"rr{tag}{i}", [64, N], F32) for i in range(2)]
            b_posi, b_posf, b_ang = Buf("posi" + tag), Buf("posf" + tag), Buf("ang" + tag)
            b_kf, b_ki, b_rr = bufs("kf" + tag, 2), bufs("ki" + tag, 2), bufs("rr" + tag, 2)

            def tables(pos_src, cos2, sinS, b_cos, b_sin):
                S.dma("sp", posi[:], pos_src.broadcast_to([64, N]), writes=[b_posi])
                cp("dve", posf[:, :], posi[:, :], [b_posi], [b_posf])
                ts("dve", ang[:, :], posf[:, :], rope_c[:, 0:1], None, ALU.mult, None, [b_posf, b_par], [b_ang])
                for i, (eng, phase, outt, bo, ccol) in enumerate((("dve", 0.0, sinS, b_sin, 1), ("dve", 0.25, cos2, b_cos, 2))):
                    ts(eng, kf[i][:, :], ang[:, :], 1.0 / TWO_PI, phase, ALU.mult, ALU.add, [b_ang], [b_kf[i]])
                    cp(eng, ki[i][:, :], kf[i][:, :], [b_kf[i]], [b_ki[i]])
                    cp(eng, kf[i][:, :], ki[i][:, :], [b_ki[i]], [b_kf[i]])
                    stt(eng, rr[i][:, :], kf[i][:, :], -CW1, ang[:, :], ALU.mult, ALU.add, [b_kf[i], b_ang], [b_rr[i]])
                    stt(eng, rr[i][:, :], kf[i][:, :], -CW2, rr[i][:, :], ALU.mult, ALU.add, [b_kf[i], b_rr[i]], [b_rr[i]])
                    if phase != 0.0:
                        ts(eng, rr[i][:, :], rr[i][:, :], phase * TWO_PI, None, ALU.add, None, [b_rr[i]], [b_rr[i]])
                    act(outt, rr[i][:, :], AF.Sin, [b_rr[i], b_par], [bo], scale=rope_c[:, ccol:ccol + 1])

            return tables

        with ExitStack() as es:
            prep, finish, _ = make_hT_builder(es)
            tables = make_tables(es, 512, "L")
            hTt = [sb(es, f"hTt{i}", [128, KC, 512], BF16) for i in range(2)]
            b_hTt = bufs("hTt", 2, KC)
            wkv = sb(es, "wkv", [128, 3, KC, 128], BF16)
            b_wkv = Buf("wkv")
            for i in range(3):
                S.dma("pool", wkv[:, i, :, :], w_in_l[i], writes=[b_wkv])
            cosL = sb(es, "cosL", [64, 512], F32)
            sinL = sb(es, "sinL", [64, 512], F32)
            b_cosL, b_sinL = Buf("cosL"), Buf("sinL")
            sq = [sb(es, f"sqL{i}", [128, 512], BF16) for i in range(3)]
            b_sq = bufs("sqL", 3)
            lnc = sb(es, "lnc", [128, 512], F32)
            rstdc = sb(es, "rstdc", [128, 512], F32)
            b_lnc, b_rstdc = Buf("lnc"), Buf("rstdc")
            t1 = sb(es, "t1L", [64, 512], F32)
            t2 = sb(es, "t2L", [64, 512], F32)
            b_t1, b_t2 = Buf("t1L"), Buf("t2L")

            nxt_set = prep(x_all, 0)
            ck("L1")
            for t in range(NT):
                hT, bh = hTt[t % 2], b_hTt[t % 2]
                finish(nxt_set, hT, bh, 0)
                ck("L2")
                if t + 1 < NT:
                    nxt_set = prep(x_all, (t + 1) * 512)
                pb = [bank_ring.next() for _ in range(4)]
                lhs = [(wkv[:, 0, :, :], 128, 0), (wkv[:, 1, :, :], 128, 0), (wkv[:, 2, :, :], 64, 0), (wkv[:, 2, :, :], 64, 64)]
                for g in range(4):
                    w, m, c0 = lhs[g]
                    for kc in range(KC):
                        mm(pb[g][0][0:m, :], w[:, kc, c0:c0 + m], hT[:, kc, :], kc == 0, kc == KC - 1, [b_wkv, bh[kc]],
                           [pb[g][1]])
                ck("L3")
                tables(pos_all[0:1, t * 512:(t + 1) * 512], cosL[:, :], sinL[:, :], b_cosL, b_sinL)
                ck("L4")
                act(sq[0][:, :], pb[0][0][:, :], AF.Square, [pb[0][1]], [b_sq[0]])
                act(sq[1][:, :], pb[1][0][:, :], AF.Square, [pb[1][1]], [b_sq[1]])
                act(sq[2][0:64, :], pb[2][0][0:64, :], AF.Square, [pb[2][1]], [b_sq[2]])
                sb_, sbb = bank_ring.next()
                mm(sb_[:, :], ones_bf[:, :], sq[0][:, :], True, False, [b_const, b_sq[0]], [sbb])
                mm(sb_[:, :], ones_bf[:, :], sq[1][:, :], False, True, [b_const, b_sq[1]], [sbb])
                act(lnc[:, :], sb_[:, :], AF.Ln, [sbb], [b_lnc], scale=1.0 / 256, bias=cst[:, 0:1])
                act(rstdc[:, :], lnc[:, :], AF.Exp, [b_lnc], [b_rstdc], scale=-0.5)
                for kc in range(2):
                    stt("dve", ckvnT[:, kc, t * 512:(t + 1) * 512], pb[kc][0][:, :], kvag[:, kc:kc + 1], rstdc[:, :],
                        ALU.mult, ALU.mult, [pb[kc][1], b_par, b_rstdc], [b_ckvn[t]])
                rb, rbb = bank_ring.next()
                for st in range(4):
                    mm(rb[:, st:st + 1], sq[2][0:64, st * 128:(st + 1) * 128], ones_bf[0:64, 0:1], True, True,
                       [b_sq[2], b_const], [rbb])
                cp("dve", ssr[:, t * 4:(t + 1) * 4], rb[:, 0:4], [rbb], [b_ssr[t]])
                stt("dve", t1[:, :], pb[2][0][0:64, :], kg[0:64, 1:2], cosL[:, :], ALU.mult, ALU.mult,
                    [pb[2][1], b_par, b_cosL], [b_t1])
                stt("dve", t2[:, :], pb[3][0][0:64, :], kg[0:64, 2:3], sinL[:, :], ALU.mult, ALU.mult,
                    [pb[3][1], b_par, b_sinL], [b_t2])
                tt("pool", krT[:, t * 512:(t + 1) * 512], t1[:, :], t2[:, :], ALU.add, [b_t1, b_t2], [b_kr[t]])
                ck("L5")
            S.barrier()
            ck("L")

        for hf in range(NHALF):
            with ExitStack() as hs:
                yTc = sb(hs, "yTc", [128, 8, 1024], BF16)
                szaT = sb(hs, "szaT", [128, 8, 1024], BF16)
                cqnT = sb(hs, "cqnT", [128, 4, 1024], BF16)
                b_yTc = bufs(f"yTc{hf}", 8, 2)
                b_sza = bufs(f"sza{hf}", 8, 2)
                b_cqn = bufs(f"cqn{hf}", 4, 2)
                with ExitStack() as h2s:
                    hT2 = sb(h2s, "hT2", [128, KC, 1024], BF16)
                    b_hT2 = bufs(f"hT2{hf}", 2, KC)
                    hTh = sb(h2s, "hTh", [128, KC, 4], BF16)
                    b_hTh = Buf(f"hTh{hf}")

                    with ExitStack() as es:
                        prep, finish, junk = make_hT_builder(es, nxt=2)
                        set0 = prep(x_own, (hf * 2) * 512)
                        set1 = prep(x_own, (hf * 2 + 1) * 512)
                        finish(set0, hT2, b_hT2[0], 0)
                        finish(set1, hT2, b_hT2[1], 512)
                        xh = sb(es, "xh", [4, D], F32)
                        xnh = sb(es, "xnh", [4, D], BF16)
                        ssh = sb(es, "ssh", [4, 1], F32)
                        b_xh, b_xnh, b_ssh = Buf("xh"), Buf("xnh"), Buf("ssh")
                        S.dma("sp", xh[:], x_halo[hf * 4:(hf + 1) * 4, :], writes=[b_xh])
                        op("pool", lambda e: e.memset(ssh[:, :], 0.0), writes=[b_ssh])
                        act(junk[0:4, :], xh[:, :], AF.Square, [b_xh], [b_ssh], accum_out=ssh[:, 0:1])
                        act(ssh[:, :], ssh[:, :], AF.Ln, [b_ssh, b_const], [b_ssh], scale=1.0 / D, bias=cst[0:4, 0:1])
                        act(ssh[:, :], ssh[:, :], AF.Exp, [b_ssh], [b_ssh], scale=-0.5)
                        ts("dve", xnh[:, :], xh[:, :], ssh[:, 0:1], None, ALU.mult, None, [b_xh, b_ssh], [b_xnh])
                        for kc in range(KC):
                            op("pe", lambda e: e.transpose(out=tp_tiles[0][:, kc * 4:(kc + 1) * 4], in_=xnh[0:4, kc * 128:(kc + 1) * 128],
                                                           identity=ident[0:4, 0:4]), [b_xnh, b_const], [tp_bufs[0]])
                        for kc in range(KC):
                            act(hTh[:, kc, :], tp_tiles[0][:, kc * 4:(kc + 1) * 4], AF.Identity, [tp_bufs[0], b_mod], [b_hTh],
                                scale=gs_col[:, kc:kc + 1], bias=shift_col[:, kc:kc + 1])
                        S.barrier()
                        ck("2a")

                    with ExitStack() as es:
                        NW = 8
                        wr = [sb(es, f"wr{i}", [128, KC, 128], BF16) for i in range(NW)]
                        wring = Ring(list(zip(wr, bufs("wr", NW))))
                        order = list(range(3, 47))
                        loaded = {}
                        nload = [0]

                        def prefetch(upto):
                            while nload[0] < len(order) and nload[0] < upto:
                                cc = order[nload[0]]
                                w, wb = wring.next()
                                S.dma("pool", w[:], w_in_l[cc], writes=[wb])
                                loaded[cc] = (w, wb)
                                nload[0] += 1

                        def proj(w, wb, ot, bank, bb):
                            for kc in range(KC):
                                mm(bank[:, :], w[:, kc, :], hT2[:, kc, ot * 512:(ot + 1) * 512], kc == 0, kc == KC - 1,
                                   [wb, b_hT2[ot][kc]], [bb])

                        u_ext = [sb(es, f"uext{i}", [128, 514], F32) for i in range(2)]
                        b_uext = bufs("uext", 2)
                        cc_sb = [sb(es, f"ccsb{i}", [128, 512], F32) for i in range(2)]
                        b_ccsb = bufs("ccsb", 2)
                        acc = [sb(es, f"acc{i}", [128, 512], F32) for i in range(2)]
                        b_acc = bufs("acc", 2)
                        szt = [sb(es, f"szt{i}", [128, 512], F32) for i in range(2)]
                        b_szt = bufs("szt", 2)
                        tmp = [sb(es, f"tmpc{i}", [128, 512], F32) for i in range(2)]
                        b_tmp = bufs("tmpc", 2)
                        hc_sb = sb(es, "hc_sb", [128, 4], F32)
                        uh = sb(es, "uh", [128, 4], F32)
                        b_hc, b_uh = Buf("hc"), Buf("uh")
                        prefetch(4)
                        for i in range(8):
                            prefetch(4 * i + 8)
                            (wx, wxb), (wc, wcb), (wbm, wbb), (wz, wzb) = [loaded[3 + 4 * i + g] for g in range(4)]
                            hb, hbb = bank_ring.next()
                            for kc in range(KC):
                                mm(hb[:, 0:4], wx[:, kc, :], hTh[:, kc, :], kc == 0, kc == KC - 1, [wxb, b_hTh], [hbb])
                            for kc in range(KC):
                                mm(hb[:, 4:8], wc[:, kc, :], hTh[:, kc, :], kc == 0, kc == KC - 1, [wcb, b_hTh], [hbb])
                            cp("act", hc_sb[:, :], hb[:, 4:8], [hbb], [b_hc])
                            tt("dve", uh[:, :], hb[:, 0:4], hc_sb[:, :], ALU.mult, [hbb, b_hc], [b_uh])
                            tt("dve", uh[:, :], uh[:, :], hvalid[:, hf * 4:(hf + 1) * 4], ALU.mult, [b_uh, b_par], [b_uh])
                            for ot in range(2):
                                bx, bxb = bank_ring.next()
                                bc, bcb = bank_ring.next()
                                bbk, bbb = bank_ring.next()
                                bz, bzb = bank_ring.next()
                                proj(wx, wxb, ot, bx, bxb)
                                proj(wc, wcb, ot, bc, bcb)
                                proj(wbm, wbb, ot, bbk, bbb)
                                proj(wz, wzb, ot, bz, bzb)
                                cp("act", cc_sb[ot][:, :], bc[:, :], [bcb], [b_ccsb[ot]])
                                tt("dve", u_ext[ot][:, 2:514], bx[:, :], cc_sb[ot][:, :], ALU.mult, [bxb, b_ccsb[ot]], [b_uext[ot]])
                                cp("dve", u_ext[ot][:, 0:2], uh[:, ot * 2:(ot + 1) * 2], [b_uh], [b_uext[ot]])
                                ts("pool", acc[ot][:, :], u_ext[ot][:, 2:514], convw[:, i, 2:3], None, ALU.mult, None,
                                   [b_uext[ot], b_par], [b_acc[ot]])
                                stt("dve", acc[ot][:, :], u_ext[ot][:, 1:513], convw[:, i, 1:2], acc[ot][:, :], ALU.mult, ALU.add,
                                    [b_uext[ot], b_par, b_acc[ot]], [b_acc[ot]])
                                stt("dve", acc[ot][:, :], u_ext[ot][:, 0:512], convw[:, i, 0:1], acc[ot][:, :], ALU.mult, ALU.add,
                                    [b_uext[ot], b_par, b_acc[ot]], [b_acc[ot]])
                                act(szt[ot][:, :], bz[:, :], AF.Silu, [bzb], [b_szt[ot]])
                                tt("dve", tmp[ot][:, :], acc[ot][:, :], bbk[:, :], ALU.mult, [b_acc[ot], bbb], [b_tmp[ot]])
                                tt("pool", yTc[:, i, ot * 512:(ot + 1) * 512], tmp[ot][:, :], szt[ot][:, :], ALU.mult,
                                   [b_tmp[ot], b_szt[ot]], [b_yTc[i][ot]])
                        prefetch(32 + 4 + 2)
                        wq = [loaded[35 + c] for c in range(4)]
                        sqc = [sb(es, f"sqc{i}", [128, 512], BF16) for i in range(4)]
                        b_sqc = bufs("sqc", 4)
                        lnq2 = sb(es, "lnq2", [128, 512], F32)
                        rsq2 = sb(es, "rsq2", [128, 512], F32)
                        b_lnq2, b_rsq2 = Buf("lnq2"), Buf("rsq2")
                        for ot in range(2):
                            pbs = [bank_ring.next() for _ in range(4)]
                            for c in range(4):
                                proj(wq[c][0], wq[c][1], ot, pbs[c][0], pbs[c][1])
                                act(sqc[c][:, :], pbs[c][0][:, :], AF.Square, [pbs[c][1]], [b_sqc[c]])
                            sbk, sbb = bank_ring.next()
                            for c in range(4):
                                mm(sbk[:, :], ones_bf[:, :], sqc[c][:, :], c == 0, c == 3, [b_const, b_sqc[c]], [sbb])
                            act(lnq2[:, :], sbk[:, :], AF.Ln, [sbb], [b_lnq2], scale=1.0 / 512, bias=cst[:, 0:1])
                            act(rsq2[:, :], lnq2[:, :], AF.Exp, [b_lnq2], [b_rsq2], scale=-0.5)
                            for c in range(4):
                                stt("dve", cqnT[:, c, ot * 512:(ot + 1) * 512], pbs[c][0][:, :], qag[:, c:c + 1], rsq2[:, :],
                                    ALU.mult, ALU.mult, [pbs[c][1], b_par, b_rsq2], [b_cqn[c][ot]])
                        for i in range(8):
                            prefetch(36 + i + 3)
                            wz, wzb = loaded[39 + i]
                            for ot in range(2):
                                bz, bzb = bank_ring.next()
                                proj(wz, wzb, ot, bz, bzb)
                                act(szaT[:, i, ot * 512:(ot + 1) * 512], bz[:, :], AF.Silu, [bzb], [b_sza[i][ot]])
                        S.barrier()
                        ck("2b")

                s_lo = 2 * hf
                NG = s_lo + 2
                with ExitStack() as hs3:
                    yTa = sb(hs3, "yTa", [128, 8, 1024], BF16)
                    b_yTa = bufs(f"yTa{hf}", 8, 2)
                    with ExitStack() as es:
                        cosq = sb(es, "cosq", [64, 1024], F32)
                        sinq = sb(es, "sinq", [64, 1024], F32)
                        b_cosq, b_sinq = bufs("cosq", 2), bufs("sinq", 2)
                        qidx_bc = sb(es, "qidx_bc", [128, 1024], F32)
                        b_qidx = Buf("qidx")
                        with ExitStack() as ets:
                            tables = make_tables(ets, 512, "Q")
                            for ot in range(2):
                                c0 = hf * 1024 + ot * 512
                                tables(pos_own[0:1, c0:c0 + 512], cosq[:, ot * 512:(ot + 1) * 512], sinq[:, ot * 512:(ot + 1) * 512],
                                       b_cosq[ot], b_sinq[ot])
                            S.dma("sp", qidx_bc[:], qidx[0:1, hf * 1024:(hf + 1) * 1024].broadcast_to([128, 1024]),
                                  writes=[b_qidx])
                            S.barrier()
                            ck("3t")
                        wqb = [sb(es, f"wqb{i}", [128, 4, 256], BF16) for i in range(2)]
                        wkvb = [sb(es, f"wkvb{i}", [128, 2, 256], BF16) for i in range(2)]
                        b_wqb, b_wkvb = bufs("wqb", 2), bufs("wkvb", 2)
                        QTn = [sb(es, f"QTn{i}", [128, 1024], BF16) for i in range(2)]
                        QTr = [sb(es, f"QTr{i}", [64, 1024], BF16) for i in range(2)]
                        b_QT = bufs("QT", 2, 2)
                        KTg = [sb(es, f"KTg{i}", [128, 2048], BF16) for i in range(2)]
                        Vg = [sb(es, f"Vg{i}", [128, 16, 128], BF16) for i in range(2)]
                        b_KTg = bufs("KTg", 2, 4)
                        b_Vg = bufs("Vg", 2, 4)
                        rk = [sb(es, f"rk{i}", [128, 16], F32) for i in range(2)]
                        rkt = [sb(es, f"rkt{i}", [128, 16], F32) for i in range(2)]
                        b_rk, b_rkt = bufs("rk", 2), bufs("rkt", 2)
                        NPB = 6
                        Pb = [sb(es, f"Pb{i}", [128, 512], BF16) for i in range(NPB)]
                        pring = Ring(list(zip(Pb, bufs("Pb", NPB))))
                        sqn = sb(es, "sqn", [128, 512], BF16)
                        sqr = sb(es, "sqr", [64, 512], BF16)
                        sqk = [sb(es, f"sqk{i}", [128, 512], BF16) for i in range(2)]
                        b_sqn, b_sqr, b_sqk = Buf("sqn"), Buf("sqr"), bufs("sqk", 2)
                        lnq3 = sb(es, "lnq3", [128, 512], F32)
                        rsq3 = sb(es, "rsq3", [128, 512], F32)
                        b_lnq3, b_rsq3 = Buf("lnq3"), Buf("rsq3")
                        q1 = sb(es, "q1", [64, 512], F32)
                        q2 = sb(es, "q2", [64, 512], F32)
                        b_q1, b_q2 = Buf("q1"), Buf("q2")
                        rec = [sb(es, f"rec{i}", [128, 512], F32) for i in range(2)]
                        onrm = [sb(es, f"onrm{i}", [128, 512], F32) for i in range(2)]
                        b_rec, b_onrm = bufs("rec", 2), bufs("onrm", 2)
                        OT = [(banks[0], bank_bufs[0]), (banks[1], bank_bufs[1])]
                        SM = [(banks[2], bank_bufs[2]), (banks[3], bank_bufs[3])]
                        r3 = Ring([(banks[i], bank_bufs[i]) for i in (4, 5, 6, 7)])
                        accs = [[sb(es, f"accs{a}{b}", [128, 512], F32) for b in range(2)] for a in range(2)]
                        b_accs = bufs("accs", 2, 2)
                        junk3 = sb(es, "junk3", [128, 128], BF16)
                        osb = [sb(es, f"osb{i}", [128, 512], F32) for i in range(2)]
                        ssb = [sb(es, f"ssb{i}", [128, 512], F32) for i in range(2)]
                        b_osb, b_ssb = bufs("osb", 2), bufs("ssb", 2)
                        ATT_BIAS = -0.5 * math.log(192.0)

                        S.dma("pool", wqb[0][:], wqb_l[0], writes=[b_wqb[0]])
                        S.dma("pool", wkvb[0][:], wkvb_l[0], writes=[b_wkvb[0]])

                        def item_Q(h):
                            hp = h % 2
                            for ot in range(2):
                                op("pool", lambda e: e.memset(accs[hp][ot][:, :], 0.0), writes=[b_accs[hp][ot]])
                            if h + 1 < 8:
                                S.dma("pool", wqb[1 - hp][:], wqb_l[h + 1], writes=[b_wqb[1 - hp]])
                                S.dma("pool", wkvb[1 - hp][:], wkvb_l[h + 1], writes=[b_wkvb[1 - hp]])
                            for ot in range(2):
                                cs = slice(ot * 512, (ot + 1) * 512)
                                bn, bnb = r3.next()
                                for kc in range(4):
                                    mm(bn[:, :], wqb[hp][:, kc, 0:128], cqnT[:, kc, cs], kc == 0, kc == 3,
                                       [b_wqb[hp], b_cqn[kc][ot]], [bnb])
                                br, brb = r3.next()
                                for kc in range(4):
                                    mm(br[0:64, :], wqb[hp][:, kc, 128:192], cqnT[:, kc, cs], kc == 0, kc == 3,
                                       [b_wqb[hp], b_cqn[kc][ot]], [brb])
                                act(sqn[:, :], bn[:, :], AF.Square, [bnb], [b_sqn])
                                act(sqr[:, :], br[0:64, :], AF.Square, [brb], [b_sqr])
                                bs_, bsb = r3.next()
                                mm(bs_[:, :], ones_bf[:, :], sqn[:, :], True, False, [b_const, b_sqn], [bsb])
                                mm(bs_[:, :], ones_bf[0:64, :], sqr[:, :], False, True, [b_const, b_sqr], [bsb])
                                act(lnq3[:, :], bs_[:, :], AF.Ln, [bsb, b_const], [b_lnq3], scale=1.0 / 192, bias=cst[:, 0:1])
                                act(rsq3[:, :], lnq3[:, :], AF.Exp, [b_lnq3], [b_rsq3], scale=-0.5)
                                stt("dve", QTn[hp][:, cs], bn[:, :], qg[:, 0:1], rsq3[:, :], ALU.mult, ALU.mult,
                                    [bnb, b_par, b_rsq3], [b_QT[hp][ot]])
                                stt("dve", q1[:, :], br[0:64, :], qg[0:64, 1:2], cosq[:, cs], ALU.mult, ALU.mult,
                                    [brb, b_par, b_cosq[ot]], [b_q1])
                                bp, bpb = r3.next()
                                for kc in range(4):
                                    mm(bp[0:64, :], wqb[hp][:, kc, 192:256], cqnT[:, kc, cs], kc == 0, kc == 3,
                                       [b_wqb[hp], b_cqn[kc][ot]], [bpb])
                                stt("dve", q2[:, :], bp[0:64, :], qg[0:64, 2:3], sinq[:, cs], ALU.mult, ALU.mult,
                                    [bpb, b_par, b_sinq[ot]], [b_q2])
                                tt("pool", q1[:, :], q1[:, :], q2[:, :], ALU.add, [b_q1, b_q2], [b_q1])
                                tt("pool", QTr[hp][:, cs], q1[:, :], rsq3[0:64, :], ALU.mult, [b_q1, b_rsq3], [b_QT[hp][ot]])
                            if DBG and h == 0 and hf == 0:
                                S.dma("sp", dbg_qn, QTn[hp][:], reads=b_QT[hp])
                                S.dma("sp", dbg_qr, QTr[hp][:], reads=b_QT[hp])
                                S.dma("sp", dbg_kr, krT[:, 0:2048], reads=b_kr[0:4])
                                S.dma("sp", dbg_cs[:, 0:1024], cosq[:], reads=b_cosq)
                                S.dma("sp", dbg_cs[:, 1024:2048], sinq[:], reads=b_sinq)

                        def item_G(h, G):
                            hp = h % 2
                            gp = (h * NG + G) % 2
                            op("pool", lambda e: e.memset(rkt[gp][:, :], 0.0), writes=[b_rkt[gp]])
                            for t4 in range(4):
                                gt = 4 * G + t4
                                bk, bkb = r3.next()
                                for kc in range(2):
                                    mm(bk[:, :], wkvb[hp][:, kc, 0:128], ckvnT[:, kc, gt * 512:(gt + 1) * 512], kc == 0, kc == 1,
                                       [b_wkvb[hp], b_ckvn[gt]], [bkb])
                                ts("dve", KTg[gp][:, t4 * 512:(t4 + 1) * 512], bk[:, :], kg[:, 0:1], None, ALU.mult, None,
                                   [bkb, b_par], [b_KTg[gp][t4]])
                            for b2 in range(8):
                                bv, bvb = r3.next()
                                for q in range(2):
                                    blk = 16 * G + b2 * 2 + q
                                    for kc in range(2):
                                        mm(bv[:, q * 256:(q + 1) * 256], ckvnT[:, kc, blk * 128:(blk + 1) * 128],
                                           wkvb[hp][:, kc, 0:256], kc == 0, kc == 1, [b_wkvb[hp], b_ckvn[blk // 4]], [bvb])
                                for q in range(2):
                                    c = b2 * 2 + q
                                    act(junk3[:, :], bv[:, q * 256:q * 256 + 128], AF.Square, [bvb], [b_rkt[gp]],
                                        accum_out=rkt[gp][:, c:c + 1])
                                cp("dve", Vg[gp][:, b2 * 2:(b2 + 1) * 2, :],
                                   bv[:, :].rearrange("p (a b) -> p a b", a=2)[:, :, 128:256], [bvb], [b_Vg[gp][b2 // 2]])
                            tt("dve", rkt[gp][:, :], rkt[gp][:, :], ssr[:, G * 16:(G + 1) * 16], ALU.add,
                               [b_rkt[gp]] + [b_ssr[4 * G + i] for i in range(4)], [b_rkt[gp]])
                            act(rkt[gp][:, :], rkt[gp][:, :], AF.Ln, [b_rkt[gp], b_const], [b_rkt[gp]], scale=1.0, bias=cst[:, 1:2])
                            act(rk[gp][:, :], rkt[gp][:, :], AF.Exp, [b_rkt[gp]], [b_rk[gp]], scale=-0.5)
                            if DBG and h == 0 and hf == 0 and G == 0:
                                S.dma("sp", dbg_k, KTg[gp][:], reads=b_KTg[gp])
                                S.dma("sp", dbg_v, Vg[gp][:], reads=b_Vg[gp])
                                S.dma("sp", dbg_rk, rk[gp][:], reads=[b_rk[gp]])

                        def item_A(h, G):
                            hp = h % 2
                            gp = (h * NG + G) % 2
                            steps = []
                            for ot in range(2):
                                s = s_lo + ot
                                if s >= G:
                                    for kb in range(16):
                                        steps.append((ot, s, kb))
                            LAG = 2
                            pend = []
                            for i in range(len(steps) + LAG):
                                if i < len(steps):
                                    ot, s, kb = steps[i]
                                    cs = slice(ot * 512, (ot + 1) * 512)
                                    gblk = 16 * G + kb
                                    bs_, bsb = r3.next()
                                    mm(bs_[:, :], KTg[gp][:, kb * 128:(kb + 1) * 128], QTn[hp][:, cs], True, False,
                                       [b_KTg[gp][kb // 4], b_QT[hp][ot]], [bsb])
                                    mm(bs_[:, :], krT[0:64, gblk * 128:(gblk + 1) * 128], QTr[hp][:, cs], False, True,
                                       [b_kr[gblk // 4], b_QT[hp][ot]], [bsb])
                                    Pt, Ptb = pring.next()
                                    act(Pt[:, :], bs_[:, :], AF.Exp, [bsb, b_rk[gp]], [Ptb], scale=rk[gp][:, kb:kb + 1])
                                    if G == s:
                                        stt("dve", Pt[:, :], qidx_bc[:, cs], kidx[:, gblk:gblk + 1], Pt[:, :], ALU.is_ge,
                                            ALU.mult, [b_qidx, b_par, Ptb], [Ptb])
                                    pend.append((ot, s, kb, Pt, Ptb))
                                if i >= LAG:
                                    ot, s, kb, Pt, Ptb = pend[i - LAG]
                                    first = (G == 0 and kb == 0)
                                    last = (G == s and kb == 15)
                                    mm(OT[ot][0][:, :], Vg[gp][:, kb, :], Pt[:, :], first, last, [b_Vg[gp][kb // 4], Ptb],
                                       [OT[ot][1]])
                                    if G == s:
                                        mm(SM[ot][0][:, :], ones_bf[:, :], Pt[:, :], kb == 0, kb == 15 and s == 0, [b_const, Ptb],
                                           [SM[ot][1]])
                                        if kb == 15 and s > 0:
                                            mm(SM[ot][0][:, :], ones_f[:, :], accs[hp][ot][:, :], False, True,
                                               [b_const, b_accs[hp][ot]], [SM[ot][1]])
                                    else:
                                        stt("dve", accs[hp][ot][:, :], Pt[:, :], 1.0, accs[hp][ot][:, :], ALU.mult, ALU.add,
                                            [b_accs[hp][ot], Ptb], [b_accs[hp][ot]])

                        def item_F(h):
                            hp = h % 2
                            for ot in range(2):
                                cp("act", ssb[ot][:, :], SM[ot][0][:, :], [SM[ot][1]], [b_ssb[ot]])
                                cp("dve", osb[ot][:, :], OT[ot][0][:, :], [OT[ot][1]], [b_osb[ot]])
                            for ot in range(2):
                                cs = slice(ot * 512, (ot + 1) * 512)
                                op("dve", lambda e: e.reciprocal(out=rec[ot][:, :], in_=ssb[ot][:, :]), [b_ssb[ot]], [b_rec[ot]])
                                tt("pool", onrm[ot][:, :], osb[ot][:, :], rec[ot][:, :], ALU.mult, [b_osb[ot], b_rec[ot]], [b_onrm[ot]])
                                tt("pool", yTa[:, h, cs], onrm[ot][:, :], szaT[:, h, cs], ALU.mult, [b_onrm[ot], b_sza[h][ot]],
                                   [b_yTa[h][ot]])

                        item_Q(0)
                        item_G(0, 0)
                        for h in range(8):
                            for G in range(NG):
                                if G + 1 < NG:
                                    item_G(h, G + 1)
                                elif h + 1 < 8:
                                    item_Q(h + 1)
                                    item_G(h + 1, 0)
                                item_A(h, G)
                            item_F(h)

                        S.barrier()
                        ck("3")

                    if DBG:
                        S.dma("sp", dbg_yc[hf], yTc[:], reads=[b for l in b_yTc for b in l])
                        S.dma("sp", dbg_ya[hf], yTa[:], reads=[b for l in b_yTa for b in l])
                    with ExitStack() as es:
                        wo = [sb(es, f"wo{i}", [128, KC, 512], BF16) for i in range(2)]
                        b_wo = bufs("wo", 2)
                        NX = 4
                        xres = [sb(es, f"xres{i}", [128, 512], F32) for i in range(NX)]
                        b_xres = bufs("xres", NX)
                        o1 = [sb(es, f"o1{i}", [128, 512], F32) for i in range(2)]
                        b_o1 = bufs("o1", 2)
                        o2 = [sb(es, f"o2{i}", [128, 512], F32) for i in range(3)]
                        b_o2 = bufs("o2", 3)
                        items = [(ct, tb) for ct in range(4) for tb in range(8)]

                        def load_x(n):
                            ct, tb = items[n]
                            r0 = hf * 1024 + tb * 128
                            S.dma("sp", xres[n % NX][:], x_own[r0:r0 + 128, ct * 512:(ct + 1) * 512], writes=[b_xres[n % NX]])

                        S.dma("pool", wo[0][:], w_out_l[0], writes=[b_wo[0]])
                        load_x(0)
                        load_x(1)
                        for n, (ct, tb) in enumerate(items):
                            w, wb = wo[ct % 2], b_wo[ct % 2]
                            if tb == 0 and ct + 1 < 4:
                                S.dma("pool", wo[(ct + 1) % 2][:], w_out_l[ct + 1], writes=[b_wo[(ct + 1) % 2]])
                            if n + 2 < len(items):
                                load_x(n + 2)
                            r0 = hf * 1024 + tb * 128
                            ot = tb // 4
                            ts_ = slice(tb * 128, (tb + 1) * 128)
                            xi, oi, pi = n % NX, n % 3, n % 2
                            bo, bob = bank_ring.next()
                            for kc in range(KC):
                                if kc < 8:
                                    l, lb = yTc[:, kc, ts_], b_yTc[kc][ot]
                                else:
                                    l, lb = yTa[:, kc - 8, ts_], b_yTa[kc - 8][ot]
                                mm(bo[:, :], l, w[:, kc, :], kc == 0, kc == KC - 1, [lb, wb], [bob])
                            tt("dve", o1[pi][:, :], bo[:, :], gate_bc[:, ct * 512:(ct + 1) * 512], ALU.mult, [bob, b_gate],
                               [b_o1[pi]])
                            tt("pool", o2[oi][:, :], o1[pi][:, :], xres[xi][:, :], ALU.add, [b_o1[pi], b_xres[xi]],
                               [b_o2[oi]])
                            S.dma("sp", out_d[r0:r0 + 128, ct * 512:(ct + 1) * 512], o2[oi][:, :], reads=[b_o2[oi]])
                        S.barrier()
                        ck("4")

    except _Stop:
        pass
    S.off = False
    S.barrier()
    top.close()
    return nc


def _prep_inputs(x, c, positions, ada_w, ada_b, norm_g, w_in, conv_w, q_a_g, w_q_b, kv_a_g, w_kv_b, q_g, k_g, w_out):
    f = np.float32
    x = np.asarray(x, f)
    B, SEQ, _ = x.shape
    NT = SEQ // 512
    NSLOT = NT // 4
    NBLK = SEQ // 128
    positions = np.asarray(positions, np.int32)
    ada_w = np.asarray(ada_w, f)[0]
    w_in = np.asarray(w_in, f)[0]
    w_q_b = np.asarray(w_q_b, f)[0]
    w_kv_b = np.asarray(w_kv_b, f)[0]
    w_out = np.asarray(w_out, f)[0]

    def cols(v, n):
        return np.ascontiguousarray(np.asarray(v, f).reshape(n, 128).T)

    def wl(w):
        n = w.shape[1] // 128
        return w.reshape(KC, 128, n, 128).transpose(2, 1, 0, 3)

    perm = (np.arange(64) + 32) % 64
    wr = w_in[:, 4864:4928]
    chunks = [wl(w_in[:, 4608:4864]), wl(np.concatenate([wr, wr[:, perm]], axis=1))]
    conv = np.stack([w_in[:, 0:1024], w_in[:, 2048:3072], w_in[:, 1024:2048], w_in[:, 3072:4096]], 0)
    conv = conv.reshape(4, D, 8, 128).transpose(2, 0, 1, 3).reshape(32, D, 128)
    chunks.append(conv.reshape(32, KC, 128, 128).transpose(0, 2, 1, 3))
    chunks.append(wl(w_in[:, 4096:4608]))
    chunks.append(wl(w_in[:, 4928:5952]))
    w_in_l = np.ascontiguousarray(np.concatenate(chunks, 0))
    assert w_in_l.shape == (47, 128, KC, 128)
    ada_w_l = np.ascontiguousarray(ada_w.reshape(KC, 128, 12, 512).transpose(2, 1, 0, 3))
    w_out_l = np.ascontiguousarray(w_out.reshape(KC, 128, 4, 512).transpose(2, 1, 0, 3))
    wq = w_q_b.reshape(4, 128, 8, 192).transpose(2, 1, 0, 3)
    wqb_l = np.ascontiguousarray(np.concatenate([wq, wq[..., 128:192][..., perm]], -1))
    wkvb_l = np.ascontiguousarray(w_kv_b.reshape(2, 128, 8, 256).transpose(2, 1, 0, 3))

    def g3(g):
        g = np.asarray(g, f)[0]
        o = np.zeros((128, 3), f)
        o[:, 0] = g[0:128]
        o[0:64, 1] = g[128:192]
        o[0:64, 2] = g[128:192][perm]
        return o

    inv_freq = (10000.0 ** (-(np.arange(0, 64, 2, dtype=np.float64)) / 64.0)).astype(f)
    rope_c = np.zeros((64, 4), f)
    rope_c[:, 0] = np.concatenate([inv_freq, inv_freq])
    rope_c[:, 1] = np.concatenate([-np.ones(32), np.ones(32)]) * SHRINK
    rope_c[:, 2] = SHRINK
    kidx = (np.arange(NBLK)[None, :] * 128 + np.arange(128)[:, None]).astype(f)
    shared = dict(
        ada_w_l=ada_w_l, ada_b=np.asarray(ada_b, f).reshape(1, 6144), ng_col=cols(np.asarray(norm_g)[0], KC),
        w_in_l=w_in_l, convw_col=np.ascontiguousarray(np.asarray(conv_w, f)[0].reshape(3, 8, 128).transpose(2, 1, 0)),
        qag_col=cols(np.asarray(q_a_g)[0], 4), kvag_col=cols(np.asarray(kv_a_g)[0], 2), wqb_l=wqb_l, wkvb_l=wkvb_l,
        qg_col=g3(q_g), kg_col=g3(k_g), w_out_l=w_out_l, rope_c=rope_c, kidx=kidx)
    in_maps = []
    meta = []
    for core in range(NCORES):
        b, j = core // 4, core % 4
        tiles = [4 * s + j for s in range(NSLOT)]
        rows = np.concatenate([np.arange(T * 512, (T + 1) * 512) for T in tiles])
        halo = np.zeros((NSLOT * 2, D), f)
        hv = np.zeros((128, NSLOT * 2), f)
        for s, T in enumerate(tiles):
            if T > 0:
                halo[2 * s:2 * s + 2] = x[b, T * 512 - 2:T * 512]
                hv[:, 2 * s:2 * s + 2] = 1.0
        m = dict(shared)
        m.update(x_all=x[b], x_own=np.ascontiguousarray(x[b, rows]), x_halo=halo, halo_valid=hv,
                 pos_all=np.ascontiguousarray(positions[b][None, :]), pos_own=np.ascontiguousarray(positions[b, rows][None, :]),
                 qidx=rows.astype(f)[None, :], c_col=cols(np.asarray(c, f)[b], KC))
        in_maps.append(m)
        meta.append((b, rows))
    return in_maps, meta, (B, SEQ)


_NC_CACHE = {}


def kernel(**inputs):
    in_maps, meta, (B, SEQ) = _prep_inputs(**inputs)
    if SEQ not in _NC_CACHE:
        _NC_CACHE[SEQ] = build(SEQ)
    nc = _NC_CACHE[SEQ]
    res = run_bass_kernel_spmd(nc, in_maps, core_ids=list(range(NCORES)))
    out = np.empty((B, SEQ, D), np.float32)
    for core, (b, rows) in enumerate(meta):
        out[b, rows] = res.results[core]["out"]
    if DBG_HOOK is not None:
        DBG_HOOK(res)
    return out
```

```python
import math
from contextlib import ExitStack

import numpy as np
import concourse.bass as bass
import concourse.mybir as mybir
from concourse.bass_utils import run_bass_kernel_spmd

F32 = mybir.dt.float32
BF16 = mybir.dt.bfloat16
I32 = mybir.dt.int32
AF = mybir.ActivationFunctionType
ALU = mybir.AluOpType

D = 2048
KC = D // 128
NCORES = 8
EPS = 1e-6
TWO_PI = 2.0 * math.pi
CW1 = 6.28125
CW2 = TWO_PI - 6.28125
SHRINK = 1.0 - 2e-6
KSTOP = ""
DBG_HOOK = None


class Buf:
    __slots__ = ("name", "lw", "reads", "sem", "dcnt", "excl")

    def __init__(self, name, excl=False):
        self.name = name
        self.excl = excl
        self.lw = None
        self.reads = {}
        self.sem = None
        self.dcnt = 0


def bufs(name, *dims):
    if not dims:
        return Buf(name)
    return [bufs(f"{name}_{i}", *dims[1:]) for i in range(dims[0])]


class Sched:
    def __init__(self, nc):
        self.nc = nc
        self.eng = {"pe": nc.tensor, "act": nc.scalar, "dve": nc.vector, "pool": nc.gpsimd, "sp": nc.sync}
        self.sem = {k: nc.alloc_semaphore(name="es_" + k) for k in self.eng}
        self.cnt = {k: 0 for k in self.eng}
        self.seen = {k: {} for k in self.eng}
        self.dma_bufs = []
        self.free_sems = []
        self.off = False

    def _wait(self, e, deps):
        need = {}
        for d in deps:
            if d is None:
                continue
            k, v = d
            if k == e and e == "pe":
                continue
            if need.get(k, 0) < v:
                need[k] = v
        seen = self.seen[e]
        for k, v in need.items():
            if isinstance(k, Buf):
                v = k.dcnt
                so = k.sem
            else:
                so = self.sem[k]
            if seen.get(k, 0) >= v:
                continue
            seen[k] = v
            self.eng[e].wait_ge(so, v)

    @staticmethod
    def _deps(reads, writes):
        deps = []
        for b in reads:
            deps.append(b.lw)
        for b in writes:
            deps.append(b.lw)
            deps.extend(b.reads.items())
        return deps

    def op(self, e, fn, reads=(), writes=()):
        if self.off:
            return None
        if any(b.excl for b in reads):
            writes = list(writes) + [b for b in reads if b.excl]
            reads = [b for b in reads if not b.excl]
        self._wait(e, self._deps(reads, writes))
        ins = fn(self.eng[e])
        self.cnt[e] += 1
        ins.then_inc(self.sem[e], 1)
        v = self.cnt[e]
        for b in reads:
            if b.reads.get(e, 0) < v:
                b.reads[e] = v
        for b in writes:
            b.lw = (e, v)
            b.reads = {}
        return ins

    def dma(self, q, out, in_, reads=(), writes=(), sembuf=None):
        if self.off:
            return None
        self._wait(q, self._deps(reads, writes))
        if sembuf is None:
            sembuf = writes[0] if writes else reads[0]
        if sembuf.sem is None:
            sembuf.sem = self.nc.alloc_semaphore(name=f"ds{len(self.dma_bufs)}_" + sembuf.name)
            self.dma_bufs.append(sembuf)
        ins = self.eng[q].dma_start(out=out, in_=in_)
        sembuf.dcnt += 16
        ins.then_inc(sembuf.sem, 16)
        for b in reads:
            b.reads[sembuf] = sembuf.dcnt
        for b in writes:
            b.lw = (sembuf, sembuf.dcnt)
            b.reads = {}
        return ins

    def barrier(self):
        if self.off:
            return
        deps = [(k, c) for k, c in self.cnt.items() if k != "sp" and c > 0]
        deps += [(b, b.dcnt) for b in self.dma_bufs]
        self._wait("sp", deps)
        self.eng["sp"].sem_inc(self.sem["sp"], 1)
        self.cnt["sp"] += 1
        for e in self.eng:
            if e != "sp":
                self._wait(e, [("sp", self.cnt["sp"])])
                for k, v in self.seen["sp"].items():
                    if self.seen[e].get(k, 0) < v:
                        self.seen[e][k] = v


class _Stop(Exception):
    pass


class Ring:
    def __init__(self, items):
        self.items = list(items)
        self.i = 0

    def next(self):
        it = self.items[self.i % len(self.items)]
        self.i += 1
        return it


def build(SEQ):
    NT = SEQ // 512
    NSLOT = NT // 4
    NHALF = NSLOT // 2
    NBLK = SEQ // 128
    NOWN = NSLOT * 512
    NCC = 47

    nc = bass.Bass("TRN2", target_bir_lowering=False)

    def din(name, shape, dt=F32):
        return nc.dram_tensor(name, list(shape), dt, kind="ExternalInput").ap()

    x_all = din("x_all", [SEQ, D])
    x_own = din("x_own", [NOWN, D])
    x_halo = din("x_halo", [NSLOT * 2, D])
    halo_valid = din("halo_valid", [128, NSLOT * 2])
    pos_all = din("pos_all", [1, SEQ], I32)
    pos_own = din("pos_own", [1, NOWN], I32)
    qidx = din("qidx", [1, NOWN])
    kidx_d = din("kidx", [128, NBLK])
    c_col_d = din("c_col", [128, KC])
    ada_w_l = din("ada_w_l", [12, 128, KC, 512])
    ada_b_d = din("ada_b", [1, 6144])
    ng_col_d = din("ng_col", [128, KC])
    w_in_l = din("w_in_l", [NCC, 128, KC, 128])
    convw_d = din("convw_col", [128, 8, 3])
    qag_d = din("qag_col", [128, 4])
    kvag_d = din("kvag_col", [128, 2])
    wqb_l = din("wqb_l", [8, 128, 4, 256])
    wkvb_l = din("wkvb_l", [8, 128, 2, 256])
    qg_d = din("qg_col", [128, 3])
    kg_d = din("kg_col", [128, 3])
    w_out_l = din("w_out_l", [4, 128, KC, 512])
    rope_c_d = din("rope_c", [64, 4])
    out_d = nc.dram_tensor("out", [NOWN, D], F32, kind="ExternalOutput").ap()
    DBG = DBG_HOOK is not None
    if DBG:
        dbg_yc = nc.dram_tensor("dbg_yc", [NHALF, 128, 8, 1024], BF16, kind="ExternalOutput").ap()
        dbg_ya = nc.dram_tensor("dbg_ya", [NHALF, 128, 8, 1024], BF16, kind="ExternalOutput").ap()
        dbg_qn = nc.dram_tensor("dbg_qn", [128, 1024], BF16, kind="ExternalOutput").ap()
        dbg_qr = nc.dram_tensor("dbg_qr", [64, 1024], BF16, kind="ExternalOutput").ap()
        dbg_k = nc.dram_tensor("dbg_k", [128, 2048], BF16, kind="ExternalOutput").ap()
        dbg_kr = nc.dram_tensor("dbg_kr", [64, 2048], BF16, kind="ExternalOutput").ap()
        dbg_v = nc.dram_tensor("dbg_v", [128, 16, 128], BF16, kind="ExternalOutput").ap()
        dbg_rk = nc.dram_tensor("dbg_rk", [128, 16], F32, kind="ExternalOutput").ap()
        dbg_acc = nc.dram_tensor("dbg_acc", [128, 512], F32, kind="ExternalOutput").ap()
        dbg_ssb = nc.dram_tensor("dbg_ssb", [128, 512], F32, kind="ExternalOutput").ap()
        dbg_ssb2 = nc.dram_tensor("dbg_ssb2", [128, 512], F32, kind="ExternalOutput").ap()
        dbg_cs = nc.dram_tensor("dbg_cs", [64, 2048], F32, kind="ExternalOutput").ap()

    S = Sched(nc)
    op = S.op

    def mm(out, lhsT, rhs, start, stop, reads, writes):
        return op("pe", lambda e: e.matmul(out, lhsT=lhsT, rhs=rhs, start=start, stop=stop), reads, writes)

    def act(out, in_, func, reads, writes, scale=1.0, bias=0.0, accum_out=None):
        if accum_out is not None:
            return op("act", lambda e: e.activation(out=out, in_=in_, func=func, bias=bias, scale=scale,
                                                    accum_out=accum_out), reads, writes)
        return op("act", lambda e: e.activation(out=out, in_=in_, func=func, bias=bias, scale=scale), reads, writes)

    def ts(eng, out, in0, s1, s2, op0, op1, reads, writes):
        if s2 is None:
            return op(eng, lambda e: e.tensor_scalar(out=out, in0=in0, scalar1=s1, scalar2=None, op0=op0), reads, writes)
        return op(eng, lambda e: e.tensor_scalar(out=out, in0=in0, scalar1=s1, scalar2=s2, op0=op0, op1=op1), reads, writes)

    def stt(eng, out, in0, scalar, in1, op0, op1, reads, writes):
        return op(eng, lambda e: e.scalar_tensor_tensor(out=out, in0=in0, scalar=scalar, in1=in1, op0=op0, op1=op1),
                  reads, writes)

    def tt(eng, out, in0, in1, o, reads, writes):
        return op(eng, lambda e: e.tensor_tensor(out=out, in0=in0, in1=in1, op=o), reads, writes)

    def cp(eng, out, in_, reads, writes):
        if eng == "act":
            return op(eng, lambda e: e.activation(out=out, in_=in_, func=AF.Copy), reads, writes)
        return op(eng, lambda e: e.tensor_copy(out=out, in_=in_), reads, writes)

    top = ExitStack()
    kstop = KSTOP

    def ck(name):
        if kstop == name:
            S.off = True

    uid = [0]

    def sb(es, name, shape, dt):
        uid[0] += 1
        return es.enter_context(nc.sbuf_tensor(f"{name}_u{uid[0]}", list(shape), dt))

    banks = [top.enter_context(nc.psum_tensor(f"pb{i}", [128, 512], F32)) for i in range(8)]
    bank_bufs = [Buf(f"pb{i}", excl=True) for i in range(8)]
    tp_tiles = [banks[6][:, :].bitcast(BF16), banks[7][:, :].bitcast(BF16)]
    tp_bufs = [bank_bufs[6], bank_bufs[7]]

    ckvnT = sb(top, "ckvnT", [128, 2, SEQ], BF16)
    krT = sb(top, "krT", [64, SEQ], BF16)
    ssr = sb(top, "ssr", [128, NBLK], F32)
    b_ckvn = bufs("ckvn", NT)
    b_kr = bufs("kr", NT)
    b_ssr = bufs("ssr", NT)
    ident = sb(top, "ident", [128, 128], BF16)
    ones_bf = sb(top, "ones_bf", [128, 128], BF16)
    ones_f = sb(top, "ones_f", [128, 128], F32)
    cst = sb(top, "cst", [128, 2], F32)
    b_const = Buf("const")
    gs_col = sb(top, "gs_col", [128, KC], F32)
    shift_col = sb(top, "shift_col", [128, KC], F32)
    gate_bc = sb(top, "gate_bc", [128, D], F32)
    b_mod = Buf("mod")
    b_gate = Buf("gate")
    c_col = sb(top, "c_col", [128, KC], F32)
    ng_col = sb(top, "ng_col", [128, KC], F32)
    convw = sb(top, "convw", [128, 8, 3], F32)
    qag = sb(top, "qag", [128, 4], F32)
    kvag = sb(top, "kvag", [128, 2], F32)
    qg = sb(top, "qg", [128, 3], F32)
    kg = sb(top, "kg", [128, 3], F32)
    rope_c = sb(top, "rope_c", [64, 4], F32)
    kidx = sb(top, "kidx", [128, NBLK], F32)
    hvalid = sb(top, "hvalid", [128, NSLOT * 2], F32)
    b_par = Buf("par")

    for dst, src in ((c_col, c_col_d), (ng_col, ng_col_d), (convw, convw_d), (qag, qag_d), (kvag, kvag_d),
                     (qg, qg_d), (kg, kg_d), (rope_c, rope_c_d), (kidx, kidx_d), (hvalid, halo_valid)):
        S.dma("sp", dst[:], src, writes=[b_par])

    try:
        def make_tables(es, N, tag):
            posi = sb(es, f"posi{tag}", [64, N], I32)
            posf = sb(es, f"posf{tag}", [64, N], F32)
            ang = sb(es, f"ang{tag}", [64, N], F32)
            kf = [sb(es, f"kf{tag}{i}", [64, N], F32) for i in range(2)]
            ki = [sb(es, f"ki{tag}{i}", [64, N], I32) for i in range(2)]
            rr = [sb(es, f"rr{tag}{i}", [64, N], F32) for i in range(2)]
            b_posi, b_posf, b_ang = Buf("posi" + tag), Buf("posf" + tag), Buf("ang" + tag)
            b_kf, b_ki, b_rr = bufs("kf" + tag, 2), bufs("ki" + tag, 2), bufs("rr" + tag, 2)

            def tables(pos_src, cos2, sinS, b_cos, b_sin):
                S.dma("sp", posi[:], pos_src.broadcast_to([64, N]), writes=[b_posi])
                cp("dve", posf[:, :], posi[:, :], [b_posi], [b_posf])
                ts("dve", ang[:, :], posf[:, :], rope_c[:, 0:1], None, ALU.mult, None, [b_posf, b_par], [b_ang])
                for i, (eng, phase, outt, bo, ccol) in enumerate((("dve", 0.0, sinS, b_sin, 1), ("dve", 0.25, cos2, b_cos, 2))):
                    ts(eng, kf[i][:, :], ang[:, :], 1.0 / TWO_PI, phase, ALU.mult, ALU.add, [b_ang], [b_kf[i]])
                    cp(eng, ki[i][:, :], kf[i][:, :], [b_kf[i]], [b_ki[i]])
                    cp(eng, kf[i][:, :], ki[i][:, :], [b_ki[i]], [b_kf[i]])
                    stt(eng, rr[i][:, :], kf[i][:, :], -CW1, ang[:, :], ALU.mult, ALU.add, [b_kf[i], b_ang], [b_rr[i]])
                    stt(eng, rr[i][:, :], kf[i][:, :], -CW2, rr[i][:, :], ALU.mult, ALU.add, [b_kf[i], b_rr[i]], [b_rr[i]])
                    if phase != 0.0:
                        ts(eng, rr[i][:, :], rr[i][:, :], phase * TWO_PI, None, ALU.add, None, [b_rr[i]], [b_rr[i]])
                    act(outt, rr[i][:, :], AF.Sin, [b_rr[i], b_par], [bo], scale=rope_c[:, ccol:ccol + 1])

            return tables

        lscope = ExitStack()
        cosK = sb(lscope, "cosK", [64, SEQ], BF16)
        sinK = sb(lscope, "sinK", [64, SEQ], BF16)
        b_cosK, b_sinK = bufs("cosK", NT), bufs("sinK", NT)

        with ExitStack() as es:
            identf = sb(es, "identf", [128, 128], F32)
            b_idf = Buf("identf")
            op("pool", lambda e: e.memset(identf[:, :], 0.0), writes=[b_idf])
            op("pool", lambda e: e.affine_select(out=identf[:, :], in_=identf[:, :], compare_op=ALU.not_equal, fill=1.0,
                                                 base=0, pattern=[[-1, 128]], channel_multiplier=1), writes=[b_idf])
            cp("dve", ident[:, :], identf[:, :], [b_idf], [b_const])
            op("pool", lambda e: e.memset(ones_f[:, :], 1.0), writes=[b_const])
            op("pool", lambda e: e.memset(cst[:, 0:1], EPS), writes=[b_const])
            op("pool", lambda e: e.memset(cst[:, 1:2], 192.0 * EPS), writes=[b_const])
            cp("dve", ones_bf[:, :], ones_f[:, :], [b_const], [b_const])

            with ExitStack() as ets:
                tablesK = make_tables(ets, 512, "K")
                for t in range(NT):
                    tablesK(pos_all[0:1, t * 512:(t + 1) * 512], cosK[:, t * 512:(t + 1) * 512],
                            sinK[:, t * 512:(t + 1) * 512], b_cosK[t], b_sinK[t])
            sc_bf = sb(es, "sc_bf", [128, KC], BF16)
            b_sc = Buf("sc")
            act(sc_bf[:, :], c_col[:, :], AF.Silu, [b_par], [b_sc])
            modrow = sb(es, "modrow", [1, 6144], F32)
            adab = sb(es, "adab", [1, 6144], F32)
            b_adab = Buf("adab")
            S.dma("sp", adab[:], ada_b_d, writes=[b_adab])
            b_modrow = Buf("modrow")
            adaw = [sb(es, f"adaw{i}", [128, KC, 512], BF16) for i in range(2)]
            b_adaw = bufs("adaw", 2)
            for ct in range(12):
                wt, wb = adaw[ct % 2], b_adaw[ct % 2]
                S.dma("pool", wt[:], ada_w_l[ct], writes=[wb])
                bk, bb = banks[ct % 2], bank_bufs[ct % 2]
                for kc in range(KC):
                    mm(bk[0:1, :], sc_bf[:, kc:kc + 1], wt[:, kc, :], kc == 0, kc == KC - 1, [b_sc, wb], [bb])
                tt("dve", modrow[0:1, ct * 512:(ct + 1) * 512], bk[0:1, :], adab[0:1, ct * 512:(ct + 1) * 512], ALU.add,
                   [bb, b_adab], [b_modrow])
            cb, cbb = banks[2], bank_bufs[2]
            for c in range(32):
                mm(cb[:, c:c + 1], modrow[0:1, c * 128:(c + 1) * 128], ones_f[0:1, 0:1], True, True, [b_modrow, b_const], [cbb])
            cp("dve", shift_col[:, :], cb[:, 0:KC], [cbb], [b_mod])
            stt("dve", gs_col[:, :], cb[:, KC:2 * KC], 1.0, ng_col[:, :], ALU.add, ALU.mult, [cbb, b_par], [b_mod])
            for ct in range(4):
                bk, bb = banks[3 + ct % 2], bank_bufs[3 + ct % 2]
                mm(bk[:, :], ones_f[0:1, :], modrow[0:1, 4096 + ct * 512:4096 + (ct + 1) * 512], True, True,
                   [b_modrow, b_const], [bb])
                cp("act", gate_bc[:, ct * 512:(ct + 1) * 512], bk[:, :], [bb], [b_gate])
            S.barrier()
            ck("prologue")

        bank_ring = Ring(list(zip(banks[:6], bank_bufs[:6])))
        ring8 = Ring(list(zip(banks, bank_bufs)))

        def make_hT_builder(es, nxt=3):
            xt = [sb(es, f"xt{i}", [128, D], F32) for i in range(nxt)]
            b_xt = bufs("xt", nxt)
            xn = [sb(es, f"xn{i}", [128, D], BF16) for i in range(8)]
            b_xn = bufs("xn", 8)
            junk = sb(es, "junk", [128, D], BF16)
            ssq = [sb(es, f"ssq{i}", [128, 4], F32) for i in range(2)]
            lnq = [sb(es, f"lnq{i}", [128, 4], F32) for i in range(2)]
            rsq = [sb(es, f"rsq{i}", [128, 4], F32) for i in range(2)]
            b_ssq = bufs("ssq", 2)
            b_lnq = bufs("lnq", 2)
            b_rsq = bufs("rsq", 2)
            state = {"n": 0, "xi": 0}

            def prep(src, row0):
                p2 = state["n"] % 2
                state["n"] += 1
                op("pool", lambda e: e.memset(ssq[p2][:, :], 0.0), writes=[b_ssq[p2]])
                xis = []
                for st in range(4):
                    xi = state["xi"] % nxt
                    state["xi"] += 1
                    xis.append(xi)
                    S.dma("sp", xt[xi][:], src[row0 + st * 128:row0 + (st + 1) * 128, :], writes=[b_xt[xi]])
                    act(junk[:, :], xt[xi][:, :], AF.Square, [b_xt[xi]], [b_ssq[p2]], accum_out=ssq[p2][:, st:st + 1])
                    if nxt < 4 or st == 3:
                        pass
                act(lnq[p2][:, :], ssq[p2][:, :], AF.Ln, [b_ssq[p2]], [b_lnq[p2]], scale=1.0 / D, bias=cst[:, 0:1])
                act(rsq[p2][:, :], lnq[p2][:, :], AF.Exp, [b_lnq[p2]], [b_rsq[p2]], scale=-0.5)
                return p2, xis

            def prep_full(src, row0):
                p2 = state["n"] % 2
                state["n"] += 1
                for st in range(4):
                    xi = state["xi"] % nxt
                    state["xi"] += 1
                    ni = p2 * 4 + st
                    S.dma("sp", xt[xi][:], src[row0 + st * 128:row0 + (st + 1) * 128, :], writes=[b_xt[xi]])
                    op("pool", lambda e: e.memset(ssq[p2][:, st:st + 1], 0.0), writes=[b_ssq[p2]])
                    act(junk[:, :], xt[xi][:, :], AF.Square, [b_xt[xi]], [b_ssq[p2]], accum_out=ssq[p2][:, st:st + 1])
                    act(lnq[p2][:, st:st + 1], ssq[p2][:, st:st + 1], AF.Ln, [b_ssq[p2]], [b_lnq[p2]], scale=1.0 / D, bias=cst[:, 0:1])
                    act(rsq[p2][:, st:st + 1], lnq[p2][:, st:st + 1], AF.Exp, [b_lnq[p2]], [b_rsq[p2]], scale=-0.5)
                    ts("dve", xn[ni][:, :], xt[xi][:, :], rsq[p2][:, st:st + 1], None, ALU.mult,
                       None, [b_xt[xi], b_rsq[p2]], [b_xn[ni]])
                return p2

            def finish(p2, hT, b_hT, col0):
                for kc in range(KC):
                    h = kc % 2
                    tph = tp_tiles[h][:, 0:512]
                    for st in range(4):
                        ni = p2 * 4 + st
                        op("pe", lambda e: e.transpose(out=tph[:, st * 128:(st + 1) * 128],
                                                       in_=xn[ni][:, kc * 128:(kc + 1) * 128], identity=ident[:, :]),
                           [b_xn[ni], b_const], [tp_bufs[h]])
                    dst = hT[:, kc, col0:col0 + 512]
                    if kc % 2 == 0:
                        act(dst, tph, AF.Identity, [tp_bufs[h], b_mod], [b_hT[kc]], scale=gs_col[:, kc:kc + 1],
                            bias=shift_col[:, kc:kc + 1])
                    else:
                        ts("dve", dst, tph, gs_col[:, kc:kc + 1], shift_col[:, kc:kc + 1], ALU.mult, ALU.add,
                           [tp_bufs[h], b_mod], [b_hT[kc]])

            return prep_full, finish, junk

        with ExitStack() as es:
            prep, finish, _ = make_hT_builder(es)
            hTt = [sb(es, f"hTt{i}", [128, KC, 512], BF16) for i in range(2)]
            b_hTt = bufs("hTt", 2, KC)
            wkv = sb(es, "wkv", [128, 3, KC, 128], BF16)
            b_wkv = Buf("wkv")
            for i in range(3):
                S.dma("pool", wkv[:, i, :, :], w_in_l[i], writes=[b_wkv])
            sq = [sb(es, f"sqL{i}", [128, 512], BF16) for i in range(3)]
            b_sq = bufs("sqL", 3)
            lnc = sb(es, "lnc", [128, 512], F32)
            rstdc = sb(es, "rstdc", [128, 512], F32)
            b_lnc, b_rstdc = Buf("lnc"), Buf("rstdc")
            t1 = sb(es, "t1L", [64, 512], F32)
            t2 = sb(es, "t2L", [64, 512], F32)
            b_t1, b_t2 = Buf("t1L"), Buf("t2L")

            nxt_set = prep(x_all, 0)
            ck("L1")
            for t in range(NT):
                hT, bh = hTt[t % 2], b_hTt[t % 2]
                finish(nxt_set, hT, bh, 0)
                ck("L2")
                if t + 1 < NT:
                    nxt_set = prep(x_all, (t + 1) * 512)
                pb = [bank_ring.next() for _ in range(4)]
                lhs = [(wkv[:, 0, :, :], 128, 0), (wkv[:, 1, :, :], 128, 0), (wkv[:, 2, :, :], 64, 0), (wkv[:, 2, :, :], 64, 64)]
                for g in range(4):
                    w, m, c0 = lhs[g]
                    for kc in range(KC):
                        mm(pb[g][0][0:m, :], w[:, kc, c0:c0 + m], hT[:, kc, :], kc == 0, kc == KC - 1, [b_wkv, bh[kc]],
                           [pb[g][1]])
                ck("L3")
                ck("L4")
                act(sq[0][:, :], pb[0][0][:, :], AF.Square, [pb[0][1]], [b_sq[0]])
                act(sq[1][:, :], pb[1][0][:, :], AF.Square, [pb[1][1]], [b_sq[1]])
                act(sq[2][0:64, :], pb[2][0][0:64, :], AF.Square, [pb[2][1]], [b_sq[2]])
                sb_, sbb = bank_ring.next()
                mm(sb_[:, :], ones_bf[:, :], sq[0][:, :], True, False, [b_const, b_sq[0]], [sbb])
                mm(sb_[:, :], ones_bf[:, :], sq[1][:, :], False, True, [b_const, b_sq[1]], [sbb])
                act(lnc[:, :], sb_[:, :], AF.Ln, [sbb], [b_lnc], scale=1.0 / 256, bias=cst[:, 0:1])
                act(rstdc[:, :], lnc[:, :], AF.Exp, [b_lnc], [b_rstdc], scale=-0.5)
                for kc in range(2):
                    stt("dve", ckvnT[:, kc, t * 512:(t + 1) * 512], pb[kc][0][:, :], kvag[:, kc:kc + 1], rstdc[:, :],
                        ALU.mult, ALU.mult, [pb[kc][1], b_par, b_rstdc], [b_ckvn[t]])
                rb, rbb = bank_ring.next()
                for st in range(4):
                    mm(rb[:, st:st + 1], sq[2][0:64, st * 128:(st + 1) * 128], ones_bf[0:64, 0:1], True, True,
                       [b_sq[2], b_const], [rbb])
                cp("dve", ssr[:, t * 4:(t + 1) * 4], rb[:, 0:4], [rbb], [b_ssr[t]])
                stt("dve", t1[:, :], pb[2][0][0:64, :], kg[0:64, 1:2], cosK[:, t * 512:(t + 1) * 512], ALU.mult, ALU.mult,
                    [pb[2][1], b_par, b_cosK[t]], [b_t1])
                stt("dve", t2[:, :], pb[3][0][0:64, :], kg[0:64, 2:3], sinK[:, t * 512:(t + 1) * 512], ALU.mult, ALU.mult,
                    [pb[3][1], b_par, b_sinK[t]], [b_t2])
                tt("pool", krT[:, t * 512:(t + 1) * 512], t1[:, :], t2[:, :], ALU.add, [b_t1, b_t2], [b_kr[t]])
                ck("L5")
            S.barrier()
            ck("L")
        lscope.close()

        for hf in range(NHALF):
            with ExitStack() as hs:
                yTc = sb(hs, "yTc", [128, 8, 1024], BF16)
                szaT = sb(hs, "szaT", [128, 8, 1024], BF16)
                cqnT = sb(hs, "cqnT", [128, 4, 1024], BF16)
                b_yTc = bufs(f"yTc{hf}", 8, 2)
                b_sza = bufs(f"sza{hf}", 8, 2)
                b_cqn = bufs(f"cqn{hf}", 4, 2)
                with ExitStack() as h2s:
                    hT2 = sb(h2s, "hT2", [128, KC, 1024], BF16)
                    b_hT2 = bufs(f"hT2{hf}", 2, KC)
                    hTh = sb(h2s, "hTh", [128, KC, 4], BF16)
                    b_hTh = Buf(f"hTh{hf}")

                    with ExitStack() as es:
                        prep, finish, junk = make_hT_builder(es, nxt=2)
                        set0 = prep(x_own, (hf * 2) * 512)
                        set1 = prep(x_own, (hf * 2 + 1) * 512)
                        finish(set0, hT2, b_hT2[0], 0)
                        finish(set1, hT2, b_hT2[1], 512)
                        xh = sb(es, "xh", [4, D], F32)
                        xnh = sb(es, "xnh", [4, D], BF16)
                        ssh = sb(es, "ssh", [4, 1], F32)
                        b_xh, b_xnh, b_ssh = Buf("xh"), Buf("xnh"), Buf("ssh")
                        S.dma("sp", xh[:], x_halo[hf * 4:(hf + 1) * 4, :], writes=[b_xh])
                        op("pool", lambda e: e.memset(ssh[:, :], 0.0), writes=[b_ssh])
                        act(junk[0:4, :], xh[:, :], AF.Square, [b_xh], [b_ssh], accum_out=ssh[:, 0:1])
                        act(ssh[:, :], ssh[:, :], AF.Ln, [b_ssh, b_const], [b_ssh], scale=1.0 / D, bias=cst[0:4, 0:1])
                        act(ssh[:, :], ssh[:, :], AF.Exp, [b_ssh], [b_ssh], scale=-0.5)
                        ts("dve", xnh[:, :], xh[:, :], ssh[:, 0:1], None, ALU.mult, None, [b_xh, b_ssh], [b_xnh])
                        for kc in range(KC):
                            op("pe", lambda e: e.transpose(out=tp_tiles[0][:, kc * 4:(kc + 1) * 4], in_=xnh[0:4, kc * 128:(kc + 1) * 128],
                                                           identity=ident[0:4, 0:4]), [b_xnh, b_const], [tp_bufs[0]])
                        for kc in range(KC):
                            act(hTh[:, kc, :], tp_tiles[0][:, kc * 4:(kc + 1) * 4], AF.Identity, [tp_bufs[0], b_mod], [b_hTh],
                                scale=gs_col[:, kc:kc + 1], bias=shift_col[:, kc:kc + 1])
                        S.barrier()
                        ck("2a")

                    with ExitStack() as es:
                        NW = 8
                        wr = [sb(es, f"wr{i}", [128, KC, 128], BF16) for i in range(NW)]
                        wring = Ring(list(zip(wr, bufs("wr", NW))))
                        order = list(range(3, 47))
                        loaded = {}
                        nload = [0]

                        def prefetch(upto):
                            while nload[0] < len(order) and nload[0] < upto:
                                cc = order[nload[0]]
                                w, wb = wring.next()
                                S.dma("pool", w[:], w_in_l[cc], writes=[wb])
                                loaded[cc] = (w, wb)
                                nload[0] += 1

                        def proj(w, wb, ot, bank, bb):
                            for kc in range(KC):
                                mm(bank[:, :], w[:, kc, :], hT2[:, kc, ot * 512:(ot + 1) * 512], kc == 0, kc == KC - 1,
                                   [wb, b_hT2[ot][kc]], [bb])

                        u_ext = [sb(es, f"uext{i}", [128, 514], F32) for i in range(2)]
                        b_uext = bufs("uext", 2)
                        cc_sb = [sb(es, f"ccsb{i}", [128, 512], F32) for i in range(2)]
                        b_ccsb = bufs("ccsb", 2)
                        acc = [sb(es, f"acc{i}", [128, 512], F32) for i in range(2)]
                        b_acc = bufs("acc", 2)
                        szt = [sb(es, f"szt{i}", [128, 512], F32) for i in range(2)]
                        b_szt = bufs("szt", 2)
                        tmp = [sb(es, f"tmpc{i}", [128, 512], F32) for i in range(2)]
                        b_tmp = bufs("tmpc", 2)
                        hc_sb = sb(es, "hc_sb", [128, 4], F32)
                        uh = sb(es, "uh", [128, 4], F32)
                        b_hc, b_uh = Buf("hc"), Buf("uh")
                        prefetch(4)
                        for i in range(8):
                            prefetch(4 * i + 8)
                            (wx, wxb), (wc, wcb), (wbm, wbb), (wz, wzb) = [loaded[3 + 4 * i + g] for g in range(4)]
                            hb, hbb = bank_ring.next()
                            for kc in range(KC):
                                mm(hb[:, 0:4], wx[:, kc, :], hTh[:, kc, :], kc == 0, kc == KC - 1, [wxb, b_hTh], [hbb])
                            for kc in range(KC):
                                mm(hb[:, 4:8], wc[:, kc, :], hTh[:, kc, :], kc == 0, kc == KC - 1, [wcb, b_hTh], [hbb])
                            cp("act", hc_sb[:, :], hb[:, 4:8], [hbb], [b_hc])
                            tt("dve", uh[:, :], hb[:, 0:4], hc_sb[:, :], ALU.mult, [hbb, b_hc], [b_uh])
                            tt("dve", uh[:, :], uh[:, :], hvalid[:, hf * 4:(hf + 1) * 4], ALU.mult, [b_uh, b_par], [b_uh])
                            for ot in range(2):
                                bx, bxb = bank_ring.next()
                                bc, bcb = bank_ring.next()
                                bbk, bbb = bank_ring.next()
                                bz, bzb = bank_ring.next()
                                proj(wx, wxb, ot, bx, bxb)
                                proj(wc, wcb, ot, bc, bcb)
                                proj(wbm, wbb, ot, bbk, bbb)
                                proj(wz, wzb, ot, bz, bzb)
                                cp("act", cc_sb[ot][:, :], bc[:, :], [bcb], [b_ccsb[ot]])
                                tt("dve", u_ext[ot][:, 2:514], bx[:, :], cc_sb[ot][:, :], ALU.mult, [bxb, b_ccsb[ot]], [b_uext[ot]])
                                cp("dve", u_ext[ot][:, 0:2], uh[:, ot * 2:(ot + 1) * 2], [b_uh], [b_uext[ot]])
                                ts("pool", acc[ot][:, :], u_ext[ot][:, 2:514], convw[:, i, 2:3], None, ALU.mult, None,
                                   [b_uext[ot], b_par], [b_acc[ot]])
                                stt("dve", acc[ot][:, :], u_ext[ot][:, 1:513], convw[:, i, 1:2], acc[ot][:, :], ALU.mult, ALU.add,
                                    [b_uext[ot], b_par, b_acc[ot]], [b_acc[ot]])
                                stt("dve", acc[ot][:, :], u_ext[ot][:, 0:512], convw[:, i, 0:1], acc[ot][:, :], ALU.mult, ALU.add,
                                    [b_uext[ot], b_par, b_acc[ot]], [b_acc[ot]])
                                act(szt[ot][:, :], bz[:, :], AF.Silu, [bzb], [b_szt[ot]])
                                tt("dve", tmp[ot][:, :], acc[ot][:, :], bbk[:, :], ALU.mult, [b_acc[ot], bbb], [b_tmp[ot]])
                                tt("pool", yTc[:, i, ot * 512:(ot + 1) * 512], tmp[ot][:, :], szt[ot][:, :], ALU.mult,
                                   [b_tmp[ot], b_szt[ot]], [b_yTc[i][ot]])
                        prefetch(32 + 4 + 2)
                        wq = [loaded[35 + c] for c in range(4)]
                        sqc = [sb(es, f"sqc{i}", [128, 512], BF16) for i in range(4)]
                        b_sqc = bufs("sqc", 4)
                        lnq2 = sb(es, "lnq2", [128, 512], F32)
                        rsq2 = sb(es, "rsq2", [128, 512], F32)
                        b_lnq2, b_rsq2 = Buf("lnq2"), Buf("rsq2")
                        for ot in range(2):
                            pbs = [bank_ring.next() for _ in range(4)]
                            for c in range(4):
                                proj(wq[c][0], wq[c][1], ot, pbs[c][0], pbs[c][1])
                                act(sqc[c][:, :], pbs[c][0][:, :], AF.Square, [pbs[c][1]], [b_sqc[c]])
                            sbk, sbb = bank_ring.next()
                            for c in range(4):
                                mm(sbk[:, :], ones_bf[:, :], sqc[c][:, :], c == 0, c == 3, [b_const, b_sqc[c]], [sbb])
                            act(lnq2[:, :], sbk[:, :], AF.Ln, [sbb], [b_lnq2], scale=1.0 / 512, bias=cst[:, 0:1])
                            act(rsq2[:, :], lnq2[:, :], AF.Exp, [b_lnq2], [b_rsq2], scale=-0.5)
                            for c in range(4):
                                stt("dve", cqnT[:, c, ot * 512:(ot + 1) * 512], pbs[c][0][:, :], qag[:, c:c + 1], rsq2[:, :],
                                    ALU.mult, ALU.mult, [pbs[c][1], b_par, b_rsq2], [b_cqn[c][ot]])
                        for i in range(8):
                            prefetch(36 + i + 3)
                            wz, wzb = loaded[39 + i]
                            for ot in range(2):
                                bz, bzb = bank_ring.next()
                                proj(wz, wzb, ot, bz, bzb)
                                act(szaT[:, i, ot * 512:(ot + 1) * 512], bz[:, :], AF.Silu, [bzb], [b_sza[i][ot]])
                        S.barrier()
                        ck("2b")

                s_lo = 2 * hf
                NG = s_lo + 2
                with ExitStack() as hs3:
                    yTa = sb(hs3, "yTa", [128, 8, 1024], BF16)
                    b_yTa = bufs(f"yTa{hf}", 8, 2)
                    with ExitStack() as es:
                        cosq = sb(es, "cosq", [64, 1024], F32)
                        sinq = sb(es, "sinq", [64, 1024], F32)
                        b_cosq, b_sinq = bufs("cosq", 2), bufs("sinq", 2)
                        qidx_bc = sb(es, "qidx_bc", [128, 1024], F32)
                        b_qidx = Buf("qidx")
                        with ExitStack() as ets:
                            tables = make_tables(ets, 512, "Q")
                            for ot in range(2):
                                c0 = hf * 1024 + ot * 512
                                tables(pos_own[0:1, c0:c0 + 512], cosq[:, ot * 512:(ot + 1) * 512], sinq[:, ot * 512:(ot + 1) * 512],
                                       b_cosq[ot], b_sinq[ot])
                            S.dma("sp", qidx_bc[:], qidx[0:1, hf * 1024:(hf + 1) * 1024].broadcast_to([128, 1024]),
                                  writes=[b_qidx])
                            S.barrier()
                            ck("3t")
                        wqb = [sb(es, f"wqb{i}", [128, 4, 256], BF16) for i in range(2)]
                        wkvb = [sb(es, f"wkvb{i}", [128, 2, 256], BF16) for i in range(2)]
                        b_wqb, b_wkvb = bufs("wqb", 2), bufs("wkvb", 2)
                        QTn = [sb(es, f"QTn{i}", [128, 1024], BF16) for i in range(2)]
                        QTr = [sb(es, f"QTr{i}", [64, 1024], BF16) for i in range(2)]
                        b_QT = bufs("QT", 2, 2)
                        KTg = [sb(es, f"KTg{i}", [128, 2048], BF16) for i in range(2)]
                        Vg = [sb(es, f"Vg{i}", [128, 16, 128], BF16) for i in range(2)]
                        b_KTg = bufs("KTg", 2, 4)
                        b_Vg = bufs("Vg", 2, 4)
                        rk = [sb(es, f"rk{i}", [128, 16], F32) for i in range(2)]
                        rkt = [sb(es, f"rkt{i}", [128, 16], F32) for i in range(2)]
                        b_rk, b_rkt = bufs("rk", 2), bufs("rkt", 2)
                        NPB = 6
                        Pb = [sb(es, f"Pb{i}", [128, 512], BF16) for i in range(NPB)]
                        pring = Ring(list(zip(Pb, bufs("Pb", NPB))))
                        sqn = sb(es, "sqn", [128, 512], BF16)
                        sqr = sb(es, "sqr", [64, 512], BF16)
                        sqk = [sb(es, f"sqk{i}", [128, 512], BF16) for i in range(2)]
                        b_sqn, b_sqr, b_sqk = Buf("sqn"), Buf("sqr"), bufs("sqk", 2)
                        lnq3 = sb(es, "lnq3", [128, 512], F32)
                        rsq3 = sb(es, "rsq3", [128, 512], F32)
                        b_lnq3, b_rsq3 = Buf("lnq3"), Buf("rsq3")
                        q1 = sb(es, "q1", [64, 512], F32)
                        q2 = sb(es, "q2", [64, 512], F32)
                        b_q1, b_q2 = Buf("q1"), Buf("q2")
                        rec = [sb(es, f"rec{i}", [128, 512], F32) for i in range(2)]
                        onrm = [sb(es, f"onrm{i}", [128, 512], F32) for i in range(2)]
                        b_rec, b_onrm = bufs("rec", 2), bufs("onrm", 2)
                        OT = [(banks[0], bank_bufs[0]), (banks[1], bank_bufs[1])]
                        SM = [(banks[2], bank_bufs[2]), (banks[3], bank_bufs[3])]
                        r3 = Ring([(banks[i], bank_bufs[i]) for i in (4, 5, 6, 7)])
                        accs = [[sb(es, f"accs{a}{b}", [128, 512], F32) for b in range(2)] for a in range(2)]
                        b_accs = bufs("accs", 2, 2)
                        junk3 = sb(es, "junk3", [128, 128], BF16)
                        osb = [sb(es, f"osb{i}", [128, 512], F32) for i in range(2)]
                        ssb = [sb(es, f"ssb{i}", [128, 512], F32) for i in range(2)]
                        b_osb, b_ssb = bufs("osb", 2), bufs("ssb", 2)
                        ATT_BIAS = -0.5 * math.log(192.0)

                        S.dma("pool", wqb[0][:], wqb_l[0], writes=[b_wqb[0]])
                        S.dma("pool", wkvb[0][:], wkvb_l[0], writes=[b_wkvb[0]])

                        def item_Q(h):
                            hp = h % 2
                            for ot in range(2):
                                op("pool", lambda e: e.memset(accs[hp][ot][:, :], 0.0), writes=[b_accs[hp][ot]])
                            if h + 1 < 8:
                                S.dma("pool", wqb[1 - hp][:], wqb_l[h + 1], writes=[b_wqb[1 - hp]])
                                S.dma("pool", wkvb[1 - hp][:], wkvb_l[h + 1], writes=[b_wkvb[1 - hp]])
                            for ot in range(2):
                                cs = slice(ot * 512, (ot + 1) * 512)
                                bn, bnb = r3.next()
                                for kc in range(4):
                                    mm(bn[:, :], wqb[hp][:, kc, 0:128], cqnT[:, kc, cs], kc == 0, kc == 3,
                                       [b_wqb[hp], b_cqn[kc][ot]], [bnb])
                                br, brb = r3.next()
                                for kc in range(4):
                                    mm(br[0:64, :], wqb[hp][:, kc, 128:192], cqnT[:, kc, cs], kc == 0, kc == 3,
                                       [b_wqb[hp], b_cqn[kc][ot]], [brb])
                                yield
                                act(sqn[:, :], bn[:, :], AF.Square, [bnb], [b_sqn])
                                act(sqr[:, :], br[0:64, :], AF.Square, [brb], [b_sqr])
                                bs_, bsb = r3.next()
                                mm(bs_[:, :], ones_bf[:, :], sqn[:, :], True, False, [b_const, b_sqn], [bsb])
                                mm(bs_[:, :], ones_bf[0:64, :], sqr[:, :], False, True, [b_const, b_sqr], [bsb])
                                act(lnq3[:, :], bs_[:, :], AF.Ln, [bsb, b_const], [b_lnq3], scale=1.0 / 192, bias=cst[:, 0:1])
                                act(rsq3[:, :], lnq3[:, :], AF.Exp, [b_lnq3], [b_rsq3], scale=-0.5)
                                stt("dve", QTn[hp][:, cs], bn[:, :], qg[:, 0:1], rsq3[:, :], ALU.mult, ALU.mult,
                                    [bnb, b_par, b_rsq3], [b_QT[hp][ot]])
                                stt("dve", q1[:, :], br[0:64, :], qg[0:64, 1:2], cosq[:, cs], ALU.mult, ALU.mult,
                                    [brb, b_par, b_cosq[ot]], [b_q1])
                                yield
                                bp, bpb = r3.next()
                                for kc in range(4):
                                    mm(bp[0:64, :], wqb[hp][:, kc, 192:256], cqnT[:, kc, cs], kc == 0, kc == 3,
                                       [b_wqb[hp], b_cqn[kc][ot]], [bpb])
                                stt("dve", q2[:, :], bp[0:64, :], qg[0:64, 2:3], sinq[:, cs], ALU.mult, ALU.mult,
                                    [bpb, b_par, b_sinq[ot]], [b_q2])
                                tt("pool", q1[:, :], q1[:, :], q2[:, :], ALU.add, [b_q1, b_q2], [b_q1])
                                tt("pool", QTr[hp][:, cs], q1[:, :], rsq3[0:64, :], ALU.mult, [b_q1, b_rsq3], [b_QT[hp][ot]])
                                yield
                            if DBG and h == 0 and hf == 0:
                                S.dma("sp", dbg_qn, QTn[hp][:], reads=b_QT[hp])
                                S.dma("sp", dbg_qr, QTr[hp][:], reads=b_QT[hp])
                                S.dma("sp", dbg_kr, krT[:, 0:2048], reads=b_kr[0:4])
                                S.dma("sp", dbg_cs[:, 0:1024], cosq[:], reads=b_cosq)
                                S.dma("sp", dbg_cs[:, 1024:2048], sinq[:], reads=b_sinq)

                        def item_G(h, G):
                            hp = h % 2
                            gp = (h * NG + G) % 2
                            op("pool", lambda e: e.memset(rkt[gp][:, :], 0.0), writes=[b_rkt[gp]])
                            for t4 in range(4):
                                gt = 4 * G + t4
                                bk, bkb = r3.next()
                                for kc in range(2):
                                    mm(bk[:, :], wkvb[hp][:, kc, 0:128], ckvnT[:, kc, gt * 512:(gt + 1) * 512], kc == 0, kc == 1,
                                       [b_wkvb[hp], b_ckvn[gt]], [bkb])
                                ts("dve", KTg[gp][:, t4 * 512:(t4 + 1) * 512], bk[:, :], kg[:, 0:1], None, ALU.mult, None,
                                   [bkb, b_par], [b_KTg[gp][t4]])
                                yield
                            for b2 in range(8):
                                bv, bvb = r3.next()
                                for q in range(2):
                                    blk = 16 * G + b2 * 2 + q
                                    for kc in range(2):
                                        mm(bv[:, q * 256:(q + 1) * 256], ckvnT[:, kc, blk * 128:(blk + 1) * 128],
                                           wkvb[hp][:, kc, 0:256], kc == 0, kc == 1, [b_wkvb[hp], b_ckvn[blk // 4]], [bvb])
                                for q in range(2):
                                    c = b2 * 2 + q
                                    act(junk3[:, :], bv[:, q * 256:q * 256 + 128], AF.Square, [bvb], [b_rkt[gp]],
                                        accum_out=rkt[gp][:, c:c + 1])
                                cp("dve", Vg[gp][:, b2 * 2:(b2 + 1) * 2, :],
                                   bv[:, :].rearrange("p (a b) -> p a b", a=2)[:, :, 128:256], [bvb], [b_Vg[gp][b2 // 2]])
                                yield
                            tt("dve", rkt[gp][:, :], rkt[gp][:, :], ssr[:, G * 16:(G + 1) * 16], ALU.add,
                               [b_rkt[gp]] + [b_ssr[4 * G + i] for i in range(4)], [b_rkt[gp]])
                            act(rkt[gp][:, :], rkt[gp][:, :], AF.Ln, [b_rkt[gp], b_const], [b_rkt[gp]], scale=1.0, bias=cst[:, 1:2])
                            act(rk[gp][:, :], rkt[gp][:, :], AF.Exp, [b_rkt[gp]], [b_rk[gp]], scale=-0.5)
                            if DBG and h == 0 and hf == 0 and G == 0:
                                S.dma("sp", dbg_k, KTg[gp][:], reads=b_KTg[gp])
                                S.dma("sp", dbg_v, Vg[gp][:], reads=b_Vg[gp])
                                S.dma("sp", dbg_rk, rk[gp][:], reads=[b_rk[gp]])

                        def item_A(h, G, filler=None):
                            hp = h % 2
                            gp = (h * NG + G) % 2
                            steps = []
                            for ot in range(2):
                                s = s_lo + ot
                                if s >= G:
                                    for kb in range(16):
                                        steps.append((ot, s, kb))
                            LAG = 2
                            pend = []
                            for i in range(len(steps) + LAG):
                                if i < len(steps):
                                    ot, s, kb = steps[i]
                                    cs = slice(ot * 512, (ot + 1) * 512)
                                    gblk = 16 * G + kb
                                    bs_, bsb = r3.next()
                                    mm(bs_[:, :], KTg[gp][:, kb * 128:(kb + 1) * 128], QTn[hp][:, cs], True, False,
                                       [b_KTg[gp][kb // 4], b_QT[hp][ot]], [bsb])
                                    mm(bs_[:, :], krT[0:64, gblk * 128:(gblk + 1) * 128], QTr[hp][:, cs], False, True,
                                       [b_kr[gblk // 4], b_QT[hp][ot]], [bsb])
                                    Pt, Ptb = pring.next()
                                    act(Pt[:, :], bs_[:, :], AF.Exp, [bsb, b_rk[gp]], [Ptb], scale=rk[gp][:, kb:kb + 1])
                                    if G == s:
                                        stt("dve", Pt[:, :], qidx_bc[:, cs], kidx[:, gblk:gblk + 1], Pt[:, :], ALU.is_ge,
                                            ALU.mult, [b_qidx, b_par, Ptb], [Ptb])
                                    pend.append((ot, s, kb, Pt, Ptb))
                                if i >= LAG:
                                    ot, s, kb, Pt, Ptb = pend[i - LAG]
                                    first = (G == 0 and kb == 0)
                                    last = (G == s and kb == 15)
                                    mm(OT[ot][0][:, :], Vg[gp][:, kb, :], Pt[:, :], first, last, [b_Vg[gp][kb // 4], Ptb],
                                       [OT[ot][1]])
                                    if G == s:
                                        mm(SM[ot][0][:, :], ones_bf[:, :], Pt[:, :], kb == 0, kb == 15 and s == 0,
                                           [b_const, Ptb], [SM[ot][1]])
                                        if kb == 15 and s > 0:
                                            mm(SM[ot][0][:, :], ones_f[:, :], accs[hp][ot][:, :], False, True,
                                               [b_const, b_accs[hp][ot]], [SM[ot][1]])
                                    else:
                                        stt("dve", accs[hp][ot][:, :], Pt[:, :], 1.0, accs[hp][ot][:, :], ALU.mult, ALU.add,
                                            [b_accs[hp][ot], Ptb], [b_accs[hp][ot]])
                                if filler is not None:
                                    next(filler, None)
                            if filler is not None:
                                for _ in filler:
                                    pass

                        def item_F(h):
                            hp = h % 2
                            for ot in range(2):
                                cp("act", ssb[ot][:, :], SM[ot][0][:, :], [SM[ot][1]], [b_ssb[ot]])
                                cp("dve", osb[ot][:, :], OT[ot][0][:, :], [OT[ot][1]], [b_osb[ot]])
                            for ot in range(2):
                                cs = slice(ot * 512, (ot + 1) * 512)
                                op("dve", lambda e: e.reciprocal(out=rec[ot][:, :], in_=ssb[ot][:, :]), [b_ssb[ot]], [b_rec[ot]])
                                tt("pool", onrm[ot][:, :], osb[ot][:, :], rec[ot][:, :], ALU.mult, [b_osb[ot], b_rec[ot]], [b_onrm[ot]])
                                tt("pool", yTa[:, h, cs], onrm[ot][:, :], szaT[:, h, cs], ALU.mult, [b_onrm[ot], b_sza[h][ot]],
                                   [b_yTa[h][ot]])

                        import itertools
                        for _ in item_Q(0):
                            pass
                        for _ in item_G(0, 0):
                            pass
                        for h in range(8):
                            for G in range(NG):
                                if G + 1 < NG:
                                    filler = item_G(h, G + 1)
                                elif h + 1 < 8:
                                    filler = itertools.chain(item_Q(h + 1), item_G(h + 1, 0))
                                else:
                                    filler = None
                                item_A(h, G, filler)
                            item_F(h)

                        S.barrier()
                        ck("3")

                    if DBG:
                        S.dma("sp", dbg_yc[hf], yTc[:], reads=[b for l in b_yTc for b in l])
                        S.dma("sp", dbg_ya[hf], yTa[:], reads=[b for l in b_yTa for b in l])
                    with ExitStack() as es:
                        wo = [sb(es, f"wo{i}", [128, KC, 512], BF16) for i in range(2)]
                        b_wo = bufs("wo", 2)
                        NX = 4
                        xres = [sb(es, f"xres{i}", [128, 512], F32) for i in range(NX)]
                        b_xres = bufs("xres", NX)
                        o1 = [sb(es, f"o1{i}", [128, 512], F32) for i in range(2)]
                        b_o1 = bufs("o1", 2)
                        o2 = [sb(es, f"o2{i}", [128, 512], F32) for i in range(3)]
                        b_o2 = bufs("o2", 3)
                        items = [(ct, tb) for ct in range(4) for tb in range(8)]

                        def load_x(n):
                            ct, tb = items[n]
                            r0 = hf * 1024 + tb * 128
                            S.dma("sp", xres[n % NX][:], x_own[r0:r0 + 128, ct * 512:(ct + 1) * 512], writes=[b_xres[n % NX]])

                        S.dma("pool", wo[0][:], w_out_l[0], writes=[b_wo[0]])
                        load_x(0)
                        load_x(1)
                        for n, (ct, tb) in enumerate(items):
                            w, wb = wo[ct % 2], b_wo[ct % 2]
                            if tb == 0 and ct + 1 < 4:
                                S.dma("pool", wo[(ct + 1) % 2][:], w_out_l[ct + 1], writes=[b_wo[(ct + 1) % 2]])
                            if n + 2 < len(items):
                                load_x(n + 2)
                            r0 = hf * 1024 + tb * 128
                            ot = tb // 4
                            ts_ = slice(tb * 128, (tb + 1) * 128)
                            xi, oi, pi = n % NX, n % 3, n % 2
                            bo, bob = bank_ring.next()
                            for kc in range(KC):
                                if kc < 8:
                                    l, lb = yTc[:, kc, ts_], b_yTc[kc][ot]
                                else:
                                    l, lb = yTa[:, kc - 8, ts_], b_yTa[kc - 8][ot]
                                mm(bo[:, :], l, w[:, kc, :], kc == 0, kc == KC - 1, [lb, wb], [bob])
                            tt("dve", o1[pi][:, :], bo[:, :], gate_bc[:, ct * 512:(ct + 1) * 512], ALU.mult, [bob, b_gate],
                               [b_o1[pi]])
                            tt("pool", o2[oi][:, :], o1[pi][:, :], xres[xi][:, :], ALU.add, [b_o1[pi], b_xres[xi]],
                               [b_o2[oi]])
                            S.dma("sp", out_d[r0:r0 + 128, ct * 512:(ct + 1) * 512], o2[oi][:, :], reads=[b_o2[oi]])
                        S.barrier()
                        ck("4")

    except _Stop:
        pass
    S.off = False
    S.barrier()
    top.close()
    return nc


def _prep_inputs(x, c, positions, ada_w, ada_b, norm_g, w_in, conv_w, q_a_g, w_q_b, kv_a_g, w_kv_b, q_g, k_g, w_out):
    f = np.float32
    x = np.asarray(x, f)
    B, SEQ, _ = x.shape
    NT = SEQ // 512
    NSLOT = NT // 4
    NBLK = SEQ // 128
    positions = np.asarray(positions, np.int32)
    ada_w = np.asarray(ada_w, f)[0]
    w_in = np.asarray(w_in, f)[0]
    w_q_b = np.asarray(w_q_b, f)[0]
    w_kv_b = np.asarray(w_kv_b, f)[0]
    w_out = np.asarray(w_out, f)[0]

    def cols(v, n):
        return np.ascontiguousarray(np.asarray(v, f).reshape(n, 128).T)

    def wl(w):
        n = w.shape[1] // 128
        return w.reshape(KC, 128, n, 128).transpose(2, 1, 0, 3)

    perm = (np.arange(64) + 32) % 64
    wr = w_in[:, 4864:4928]
    chunks = [wl(w_in[:, 4608:4864]), wl(np.concatenate([wr, wr[:, perm]], axis=1))]
    conv = np.stack([w_in[:, 0:1024], w_in[:, 2048:3072], w_in[:, 1024:2048], w_in[:, 3072:4096]], 0)
    conv = conv.reshape(4, D, 8, 128).transpose(2, 0, 1, 3).reshape(32, D, 128)
    chunks.append(conv.reshape(32, KC, 128, 128).transpose(0, 2, 1, 3))
    chunks.append(wl(w_in[:, 4096:4608]))
    chunks.append(wl(w_in[:, 4928:5952]))
    w_in_l = np.ascontiguousarray(np.concatenate(chunks, 0))
    assert w_in_l.shape == (47, 128, KC, 128)
    ada_w_l = np.ascontiguousarray(ada_w.reshape(KC, 128, 12, 512).transpose(2, 1, 0, 3))
    w_out_l = np.ascontiguousarray(w_out.reshape(KC, 128, 4, 512).transpose(2, 1, 0, 3))
    wq = w_q_b.reshape(4, 128, 8, 192).transpose(2, 1, 0, 3)
    wqb_l = np.ascontiguousarray(np.concatenate([wq, wq[..., 128:192][..., perm]], -1))
    wkvb_l = np.ascontiguousarray(w_kv_b.reshape(2, 128, 8, 256).transpose(2, 1, 0, 3))

    def g3(g):
        g = np.asarray(g, f)[0]
        o = np.zeros((128, 3), f)
        o[:, 0] = g[0:128]
        o[0:64, 1] = g[128:192]
        o[0:64, 2] = g[128:192][perm]
        return o

    inv_freq = (10000.0 ** (-(np.arange(0, 64, 2, dtype=np.float64)) / 64.0)).astype(f)
    rope_c = np.zeros((64, 4), f)
    rope_c[:, 0] = np.concatenate([inv_freq, inv_freq])
    rope_c[:, 1] = np.concatenate([-np.ones(32), np.ones(32)]) * SHRINK
    rope_c[:, 2] = SHRINK
    kidx = (np.arange(NBLK)[None, :] * 128 + np.arange(128)[:, None]).astype(f)
    shared = dict(
        ada_w_l=ada_w_l, ada_b=np.asarray(ada_b, f).reshape(1, 6144), ng_col=cols(np.asarray(norm_g)[0], KC),
        w_in_l=w_in_l, convw_col=np.ascontiguousarray(np.asarray(conv_w, f)[0].reshape(3, 8, 128).transpose(2, 1, 0)),
        qag_col=cols(np.asarray(q_a_g)[0], 4), kvag_col=cols(np.asarray(kv_a_g)[0], 2), wqb_l=wqb_l, wkvb_l=wkvb_l,
        qg_col=g3(q_g), kg_col=g3(k_g), w_out_l=w_out_l, rope_c=rope_c, kidx=kidx)
    in_maps = []
    meta = []
    for core in range(NCORES):
        b, j = core // 4, core % 4
        tiles = [4 * s + j for s in range(NSLOT)]
        rows = np.concatenate([np.arange(T * 512, (T + 1) * 512) for T in tiles])
        halo = np.zeros((NSLOT * 2, D), f)
        hv = np.zeros((128, NSLOT * 2), f)
        for s, T in enumerate(tiles):
            if T > 0:
                halo[2 * s:2 * s + 2] = x[b, T * 512 - 2:T * 512]
                hv[:, 2 * s:2 * s + 2] = 1.0
        m = dict(shared)
        m.update(x_all=x[b], x_own=np.ascontiguousarray(x[b, rows]), x_halo=halo, halo_valid=hv,
                 pos_all=np.ascontiguousarray(positions[b][None, :]), pos_own=np.ascontiguousarray(positions[b, rows][None, :]),
                 qidx=rows.astype(f)[None, :], c_col=cols(np.asarray(c, f)[b], KC))
        in_maps.append(m)
        meta.append((b, rows))
    return in_maps, meta, (B, SEQ)


_NC_CACHE = {}


def kernel(**inputs):
    in_maps, meta, (B, SEQ) = _prep_inputs(**inputs)
    if SEQ not in _NC_CACHE:
        _NC_CACHE[SEQ] = build(SEQ)
    nc = _NC_CACHE[SEQ]
    res = run_bass_kernel_spmd(nc, in_maps, core_ids=list(range(NCORES)))
    out = np.empty((B, SEQ, D), np.float32)
    for core, (b, rows) in enumerate(meta):
        out[b, rows] = res.results[core]["out"]
    if DBG_HOOK is not None:
        DBG_HOOK(res)
    return out
```

```python
import math
from contextlib import ExitStack

import numpy as np
import concourse.bass as bass
import concourse.mybir as mybir
from concourse.bass_utils import run_bass_kernel_spmd

F32 = mybir.dt.float32
BF16 = mybir.dt.bfloat16
I32 = mybir.dt.int32
AF = mybir.ActivationFunctionType
ALU = mybir.AluOpType

D = 2048
KC = D // 128
NCORES = 8
EPS = 1e-6
TWO_PI = 2.0 * math.pi
CW1 = 6.28125
CW2 = TWO_PI - 6.28125
SHRINK = 1.0 - 2e-6
KSTOP = ""
DBG_HOOK = None


class Buf:
    __slots__ = ("name", "lw", "reads", "sem", "dcnt", "excl")

    def __init__(self, name, excl=False):
        self.name = name
        self.excl = excl
        self.lw = None
        self.reads = {}
        self.sem = None
        self.dcnt = 0


def bufs(name, *dims):
    if not dims:
        return Buf(name)
    return [bufs(f"{name}_{i}", *dims[1:]) for i in range(dims[0])]


class Sched:
    def __init__(self, nc):
        self.nc = nc
        self.eng = {"pe": nc.tensor, "act": nc.scalar, "dve": nc.vector, "pool": nc.gpsimd, "sp": nc.sync}
        self.sem = {k: nc.alloc_semaphore(name="es_" + k) for k in self.eng}
        self.cnt = {k: 0 for k in self.eng}
        self.seen = {k: {} for k in self.eng}
        self.dma_bufs = []
        self.free_sems = []
        self.off = False

    def _wait(self, e, deps):
        need = {}
        for d in deps:
            if d is None:
                continue
            k, v = d
            if k == e and (e == "pe" or v <= self.cnt[e] - 4):
                continue
            if need.get(k, 0) < v:
                need[k] = v
        seen = self.seen[e]
        for k, v in need.items():
            if isinstance(k, Buf):
                v = k.dcnt
                so = k.sem
            else:
                so = self.sem[k]
            if seen.get(k, 0) >= v:
                continue
            seen[k] = v
            self.eng[e].wait_ge(so, v)

    @staticmethod
    def _deps(reads, writes):
        deps = []
        for b in reads:
            deps.append(b.lw)
        for b in writes:
            deps.append(b.lw)
            deps.extend(b.reads.items())
        return deps

    def op(self, e, fn, reads=(), writes=()):
        if self.off:
            return None
        if any(b.excl for b in reads):
            writes = list(writes) + [b for b in reads if b.excl]
            reads = [b for b in reads if not b.excl]
        self._wait(e, self._deps(reads, writes))
        ins = fn(self.eng[e])
        self.cnt[e] += 1
        ins.then_inc(self.sem[e], 1)
        v = self.cnt[e]
        for b in reads:
            if b.reads.get(e, 0) < v:
                b.reads[e] = v
        for b in writes:
            b.lw = (e, v)
            b.reads = {}
        return ins

    def dma(self, q, out, in_, reads=(), writes=(), sembuf=None):
        if self.off:
            return None
        self._wait(q, self._deps(reads, writes))
        if sembuf is None:
            sembuf = writes[0] if writes else reads[0]
        if sembuf.sem is None:
            sembuf.sem = self.nc.alloc_semaphore(name=f"ds{len(self.dma_bufs)}_" + sembuf.name)
            self.dma_bufs.append(sembuf)
        ins = self.eng[q].dma_start(out=out, in_=in_)
        sembuf.dcnt += 16
        ins.then_inc(sembuf.sem, 16)
        for b in reads:
            b.reads[sembuf] = sembuf.dcnt
        for b in writes:
            b.lw = (sembuf, sembuf.dcnt)
            b.reads = {}
        return ins

    def barrier(self):
        if self.off:
            return
        deps = [(k, c) for k, c in self.cnt.items() if k != "sp" and c > 0]
        deps += [(b, b.dcnt) for b in self.dma_bufs]
        self._wait("sp", deps)
        self.eng["sp"].sem_inc(self.sem["sp"], 1)
        self.cnt["sp"] += 1
        for e in self.eng:
            if e != "sp":
                self._wait(e, [("sp", self.cnt["sp"])])
                for k, v in self.seen["sp"].items():
                    if self.seen[e].get(k, 0) < v:
                        self.seen[e][k] = v


class _Stop(Exception):
    pass


class Ring:
    def __init__(self, items):
        self.items = list(items)
        self.i = 0

    def next(self):
        it = self.items[self.i % len(self.items)]
        self.i += 1
        return it


def build(SEQ):
    NT = SEQ // 512
    NSLOT = NT // 4
    NHALF = NSLOT // 2
    NBLK = SEQ // 128
    NOWN = NSLOT * 512
    NCC = 47

    nc = bass.Bass("TRN2", target_bir_lowering=False)

    def din(name, shape, dt=F32):
        return nc.dram_tensor(name, list(shape), dt, kind="ExternalInput").ap()

    x_all = din("x_all", [SEQ, D])
    x_own = din("x_own", [NOWN, D])
    x_halo = din("x_halo", [NSLOT * 2, D])
    halo_valid = din("halo_valid", [128, NSLOT * 2])
    pos_all = din("pos_all", [1, SEQ], I32)
    pos_own = din("pos_own", [1, NOWN], I32)
    qidx = din("qidx", [1, NOWN])
    kidx_d = din("kidx", [128, NBLK])
    c_col_d = din("c_col", [128, KC])
    ada_w_l = din("ada_w_l", [12, 128, KC, 512])
    ada_b_d = din("ada_b", [1, 6144])
    ng_col_d = din("ng_col", [128, KC])
    w_in_l = din("w_in_l", [NCC, 128, KC, 128])
    convw_d = din("convw_col", [128, 8, 3])
    qag_d = din("qag_col", [128, 4])
    kvag_d = din("kvag_col", [128, 2])
    wqb_l = din("wqb_l", [8, 128, 4, 256])
    wkvb_l = din("wkvb_l", [8, 128, 2, 256])
    qg_d = din("qg_col", [128, 3])
    kg_d = din("kg_col", [128, 3])
    w_out_l = din("w_out_l", [4, 128, KC, 512])
    rope_c_d = din("rope_c", [64, 4])
    out_d = nc.dram_tensor("out", [NOWN, D], F32, kind="ExternalOutput").ap()
    DBG = DBG_HOOK is not None
    if DBG:
        dbg_yc = nc.dram_tensor("dbg_yc", [NHALF, 128, 8, 1024], BF16, kind="ExternalOutput").ap()
        dbg_ya = nc.dram_tensor("dbg_ya", [NHALF, 128, 8, 1024], BF16, kind="ExternalOutput").ap()
        dbg_qn = nc.dram_tensor("dbg_qn", [128, 1024], BF16, kind="ExternalOutput").ap()
        dbg_qr = nc.dram_tensor("dbg_qr", [64, 1024], BF16, kind="ExternalOutput").ap()
        dbg_k = nc.dram_tensor("dbg_k", [128, 2048], BF16, kind="ExternalOutput").ap()
        dbg_kr = nc.dram_tensor("dbg_kr", [64, 2048], BF16, kind="ExternalOutput").ap()
        dbg_v = nc.dram_tensor("dbg_v", [128, 16, 128], BF16, kind="ExternalOutput").ap()
        dbg_rk = nc.dram_tensor("dbg_rk", [128, 16], F32, kind="ExternalOutput").ap()
        dbg_acc = nc.dram_tensor("dbg_acc", [128, 512], F32, kind="ExternalOutput").ap()
        dbg_ssb = nc.dram_tensor("dbg_ssb", [128, 512], F32, kind="ExternalOutput").ap()
        dbg_ssb2 = nc.dram_tensor("dbg_ssb2", [128, 512], F32, kind="ExternalOutput").ap()
        dbg_cs = nc.dram_tensor("dbg_cs", [64, 2048], F32, kind="ExternalOutput").ap()

    S = Sched(nc)
    op = S.op

    def mm(out, lhsT, rhs, start, stop, reads, writes):
        return op("pe", lambda e: e.matmul(out, lhsT=lhsT, rhs=rhs, start=start, stop=stop), reads, writes)

    def act(out, in_, func, reads, writes, scale=1.0, bias=0.0, accum_out=None):
        if accum_out is not None:
            return op("act", lambda e: e.activation(out=out, in_=in_, func=func, bias=bias, scale=scale,
                                                    accum_out=accum_out), reads, writes)
        return op("act", lambda e: e.activation(out=out, in_=in_, func=func, bias=bias, scale=scale), reads, writes)

    def ts(eng, out, in0, s1, s2, op0, op1, reads, writes):
        if s2 is None:
            return op(eng, lambda e: e.tensor_scalar(out=out, in0=in0, scalar1=s1, scalar2=None, op0=op0), reads, writes)
        return op(eng, lambda e: e.tensor_scalar(out=out, in0=in0, scalar1=s1, scalar2=s2, op0=op0, op1=op1), reads, writes)

    def stt(eng, out, in0, scalar, in1, op0, op1, reads, writes):
        return op(eng, lambda e: e.scalar_tensor_tensor(out=out, in0=in0, scalar=scalar, in1=in1, op0=op0, op1=op1),
                  reads, writes)

    def tt(eng, out, in0, in1, o, reads, writes):
        return op(eng, lambda e: e.tensor_tensor(out=out, in0=in0, in1=in1, op=o), reads, writes)

    def cp(eng, out, in_, reads, writes):
        if eng == "act":
            return op(eng, lambda e: e.activation(out=out, in_=in_, func=AF.Copy), reads, writes)
        return op(eng, lambda e: e.tensor_copy(out=out, in_=in_), reads, writes)

    top = ExitStack()
    kstop = KSTOP

    def ck(name):
        if kstop == name:
            S.off = True

    uid = [0]

    def sb(es, name, shape, dt):
        uid[0] += 1
        return es.enter_context(nc.sbuf_tensor(f"{name}_u{uid[0]}", list(shape), dt))

    banks = [top.enter_context(nc.psum_tensor(f"pb{i}", [128, 512], F32)) for i in range(8)]
    bank_bufs = [Buf(f"pb{i}", excl=True) for i in range(8)]
    tp_tiles = [banks[6][:, :].bitcast(BF16), banks[7][:, :].bitcast(BF16)]
    tp_bufs = [bank_bufs[6], bank_bufs[7]]

    ckvnT = sb(top, "ckvnT", [128, 2, SEQ], BF16)
    krT = sb(top, "krT", [64, SEQ], BF16)
    ssr = sb(top, "ssr", [128, NBLK], F32)
    b_ckvn = bufs("ckvn", NT)
    b_kr = bufs("kr", NT)
    b_ssr = bufs("ssr", NT)
    ident = sb(top, "ident", [128, 128], BF16)
    ones_bf = sb(top, "ones_bf", [128, 128], BF16)
    ones_f = sb(top, "ones_f", [128, 128], F32)
    cst = sb(top, "cst", [128, 2], F32)
    rk_all = sb(top, "rk_all", [128, 8, NBLK], F32)
    b_rkall = bufs("rkall", 8, NBLK // 16)
    b_const = Buf("const")
    gs_col = sb(top, "gs_col", [128, KC], F32)
    shift_col = sb(top, "shift_col", [128, KC], F32)
    gate_bc = sb(top, "gate_bc", [128, D], F32)
    b_mod = Buf("mod")
    b_gate = Buf("gate")
    c_col = sb(top, "c_col", [128, KC], F32)
    ng_col = sb(top, "ng_col", [128, KC], F32)
    convw = sb(top, "convw", [128, 8, 3], F32)
    qag = sb(top, "qag", [128, 4], F32)
    kvag = sb(top, "kvag", [128, 2], F32)
    qg = sb(top, "qg", [128, 3], F32)
    kg = sb(top, "kg", [128, 3], F32)
    rope_c = sb(top, "rope_c", [64, 4], F32)
    kidx = sb(top, "kidx", [128, NBLK], F32)
    hvalid = sb(top, "hvalid", [128, NSLOT * 2], F32)
    b_par = Buf("par")

    for dst, src in ((c_col, c_col_d), (ng_col, ng_col_d), (convw, convw_d), (qag, qag_d), (kvag, kvag_d),
                     (qg, qg_d), (kg, kg_d), (rope_c, rope_c_d), (kidx, kidx_d), (hvalid, halo_valid)):
        S.dma("sp", dst[:], src, writes=[b_par])

    try:
        def make_tables(es, N, tag):
            posi = sb(es, f"posi{tag}", [64, N], I32)
            posf = sb(es, f"posf{tag}", [64, N], F32)
            ang = sb(es, f"ang{tag}", [64, N], F32)
            kf = [sb(es, f"kf{tag}{i}", [64, N], F32) for i in range(2)]
            ki = [sb(es, f"ki{tag}{i}", [64, N], I32) for i in range(2)]
            rr = [sb(es, f"rr{tag}{i}", [64, N], F32) for i in range(2)]
            b_posi, b_posf, b_ang = Buf("posi" + tag), Buf("posf" + tag), Buf("ang" + tag)
            b_kf, b_ki, b_rr = bufs("kf" + tag, 2), bufs("ki" + tag, 2), bufs("rr" + tag, 2)

            def tables(pos_src, cos2, sinS, b_cos, b_sin):
                S.dma("sp", posi[:], pos_src.broadcast_to([64, N]), writes=[b_posi])
                cp("dve", posf[:, :], posi[:, :], [b_posi], [b_posf])
                ts("dve", ang[:, :], posf[:, :], rope_c[:, 0:1], None, ALU.mult, None, [b_posf, b_par], [b_ang])
                for i, (eng, phase, outt, bo, ccol) in enumerate((("dve", 0.0, sinS, b_sin, 1), ("dve", 0.25, cos2, b_cos, 2))):
                    ts(eng, kf[i][:, :], ang[:, :], 1.0 / TWO_PI, phase, ALU.mult, ALU.add, [b_ang], [b_kf[i]])
                    cp(eng, ki[i][:, :], kf[i][:, :], [b_kf[i]], [b_ki[i]])
                    cp(eng, kf[i][:, :], ki[i][:, :], [b_ki[i]], [b_kf[i]])
                    stt(eng, rr[i][:, :], kf[i][:, :], -CW1, ang[:, :], ALU.mult, ALU.add, [b_kf[i], b_ang], [b_rr[i]])
                    stt(eng, rr[i][:, :], kf[i][:, :], -CW2, rr[i][:, :], ALU.mult, ALU.add, [b_kf[i], b_rr[i]], [b_rr[i]])
                    if phase != 0.0:
                        ts(eng, rr[i][:, :], rr[i][:, :], phase * TWO_PI, None, ALU.add, None, [b_rr[i]], [b_rr[i]])
                    act(outt, rr[i][:, :], AF.Sin, [b_rr[i], b_par], [bo], scale=rope_c[:, ccol:ccol + 1])

            return tables

        lscope = ExitStack()
        cosK = sb(lscope, "cosK", [64, SEQ], BF16)
        sinK = sb(lscope, "sinK", [64, SEQ], BF16)
        b_cosK, b_sinK = bufs("cosK", NT), bufs("sinK", NT)

        with ExitStack() as es:
            identf = sb(es, "identf", [128, 128], F32)
            b_idf = Buf("identf")
            op("pool", lambda e: e.memset(identf[:, :], 0.0), writes=[b_idf])
            op("pool", lambda e: e.affine_select(out=identf[:, :], in_=identf[:, :], compare_op=ALU.not_equal, fill=1.0,
                                                 base=0, pattern=[[-1, 128]], channel_multiplier=1), writes=[b_idf])
            cp("dve", ident[:, :], identf[:, :], [b_idf], [b_const])
            op("pool", lambda e: e.memset(ones_f[:, :], 1.0), writes=[b_const])
            op("pool", lambda e: e.memset(cst[:, 0:1], EPS), writes=[b_const])
            op("pool", lambda e: e.memset(cst[:, 1:2], 192.0 * EPS), writes=[b_const])
            cp("dve", ones_bf[:, :], ones_f[:, :], [b_const], [b_const])

            tablesK = make_tables(es, 512, "K")
            tdone = [0]

            def tables_upto(n):
                while tdone[0] < min(n, NT):
                    t = tdone[0]
                    tablesK(pos_all[0:1, t * 512:(t + 1) * 512], cosK[:, t * 512:(t + 1) * 512],
                            sinK[:, t * 512:(t + 1) * 512], b_cosK[t], b_sinK[t])
                    tdone[0] += 1

            sc_bf = sb(es, "sc_bf", [128, KC], BF16)
            b_sc = Buf("sc")
            act(sc_bf[:, :], c_col[:, :], AF.Silu, [b_par], [b_sc])
            modrow = sb(es, "modrow", [1, 6144], F32)
            adab = sb(es, "adab", [1, 6144], F32)
            b_adab = Buf("adab")
            S.dma("sp", adab[:], ada_b_d, writes=[b_adab])
            b_modrow = Buf("modrow")
            adaw = [sb(es, f"adaw{i}", [128, KC, 512], BF16) for i in range(2)]
            b_adaw = bufs("adaw", 2)
            for ct in range(12):
                wt, wb = adaw[ct % 2], b_adaw[ct % 2]
                S.dma("pool", wt[:], ada_w_l[ct], writes=[wb])
                bk, bb = banks[ct % 2], bank_bufs[ct % 2]
                for kc in range(KC):
                    mm(bk[0:1, :], sc_bf[:, kc:kc + 1], wt[:, kc, :], kc == 0, kc == KC - 1, [b_sc, wb], [bb])
                tt("dve", modrow[0:1, ct * 512:(ct + 1) * 512], bk[0:1, :], adab[0:1, ct * 512:(ct + 1) * 512], ALU.add,
                   [bb, b_adab], [b_modrow])
                tables_upto((ct + 1) * NT // 12 + 1)
            tables_upto(NT)
            cb, cbb = banks[2], bank_bufs[2]
            for c in range(32):
                mm(cb[:, c:c + 1], modrow[0:1, c * 128:(c + 1) * 128], ones_f[0:1, 0:1], True, True, [b_modrow, b_const], [cbb])
            cp("dve", shift_col[:, :], cb[:, 0:KC], [cbb], [b_mod])
            stt("dve", gs_col[:, :], cb[:, KC:2 * KC], 1.0, ng_col[:, :], ALU.add, ALU.mult, [cbb, b_par], [b_mod])
            for ct in range(4):
                bk, bb = banks[3 + ct % 2], bank_bufs[3 + ct % 2]
                mm(bk[:, :], ones_f[0:1, :], modrow[0:1, 4096 + ct * 512:4096 + (ct + 1) * 512], True, True,
                   [b_modrow, b_const], [bb])
                cp("act", gate_bc[:, ct * 512:(ct + 1) * 512], bk[:, :], [bb], [b_gate])
            S.barrier()
            ck("prologue")

        bank_ring = Ring(list(zip(banks[:6], bank_bufs[:6])))
        ring8 = Ring(list(zip(banks, bank_bufs)))

        def make_hT_builder(es, nxt=3):
            xt = [sb(es, f"xt{i}", [128, D], F32) for i in range(nxt)]
            b_xt = bufs("xt", nxt)
            xn = [sb(es, f"xn{i}", [128, D], BF16) for i in range(8)]
            b_xn = bufs("xn", 8)
            junk = sb(es, "junk", [128, D], BF16)
            ssq = [sb(es, f"ssq{i}", [128, 4], F32) for i in range(2)]
            lnq = [sb(es, f"lnq{i}", [128, 4], F32) for i in range(2)]
            rsq = [sb(es, f"rsq{i}", [128, 4], F32) for i in range(2)]
            b_ssq = bufs("ssq", 2)
            b_lnq = bufs("lnq", 2)
            b_rsq = bufs("rsq", 2)
            state = {"n": 0, "xi": 0}

            def prep(src, row0):
                p2 = state["n"] % 2
                state["n"] += 1
                op("pool", lambda e: e.memset(ssq[p2][:, :], 0.0), writes=[b_ssq[p2]])
                xis = []
                for st in range(4):
                    xi = state["xi"] % nxt
                    state["xi"] += 1
                    xis.append(xi)
                    S.dma("sp", xt[xi][:], src[row0 + st * 128:row0 + (st + 1) * 128, :], writes=[b_xt[xi]])
                    act(junk[:, :], xt[xi][:, :], AF.Square, [b_xt[xi]], [b_ssq[p2]], accum_out=ssq[p2][:, st:st + 1])
                    if nxt < 4 or st == 3:
                        pass
                act(lnq[p2][:, :], ssq[p2][:, :], AF.Ln, [b_ssq[p2]], [b_lnq[p2]], scale=1.0 / D, bias=cst[:, 0:1])
                act(rsq[p2][:, :], lnq[p2][:, :], AF.Exp, [b_lnq[p2]], [b_rsq[p2]], scale=-0.5)
                return p2, xis

            def prep_full(src, row0):
                p2 = state["n"] % 2
                state["n"] += 1
                for st in range(4):
                    xi = state["xi"] % nxt
                    state["xi"] += 1
                    ni = p2 * 4 + st
                    S.dma("sp", xt[xi][:], src[row0 + st * 128:row0 + (st + 1) * 128, :], writes=[b_xt[xi]])
                    op("pool", lambda e: e.memset(ssq[p2][:, st:st + 1], 0.0), writes=[b_ssq[p2]])
                    act(junk[:, :], xt[xi][:, :], AF.Square, [b_xt[xi]], [b_ssq[p2]], accum_out=ssq[p2][:, st:st + 1])
                    act(lnq[p2][:, st:st + 1], ssq[p2][:, st:st + 1], AF.Ln, [b_ssq[p2]], [b_lnq[p2]], scale=1.0 / D, bias=cst[:, 0:1])
                    act(rsq[p2][:, st:st + 1], lnq[p2][:, st:st + 1], AF.Exp, [b_lnq[p2]], [b_rsq[p2]], scale=-0.5)
                    ts("dve", xn[ni][:, :], xt[xi][:, :], rsq[p2][:, st:st + 1], None, ALU.mult,
                       None, [b_xt[xi], b_rsq[p2]], [b_xn[ni]])
                return p2

            def finish(p2, hT, b_hT, col0):
                for kc in range(KC):
                    h = kc % 2
                    tph = tp_tiles[h][:, 0:512]
                    for st in range(4):
                        ni = p2 * 4 + st
                        op("pe", lambda e: e.transpose(out=tph[:, st * 128:(st + 1) * 128],
                                                       in_=xn[ni][:, kc * 128:(kc + 1) * 128], identity=ident[:, :]),
                           [b_xn[ni], b_const], [tp_bufs[h]])
                    dst = hT[:, kc, col0:col0 + 512]
                    if kc % 2 == 0:
                        act(dst, tph, AF.Identity, [tp_bufs[h], b_mod], [b_hT[kc]], scale=gs_col[:, kc:kc + 1],
                            bias=shift_col[:, kc:kc + 1])
                    else:
                        ts("dve", dst, tph, gs_col[:, kc:kc + 1], shift_col[:, kc:kc + 1], ALU.mult, ALU.add,
                           [tp_bufs[h], b_mod], [b_hT[kc]])

            return prep_full, finish, junk

        with ExitStack() as es:
            prep, finish, _ = make_hT_builder(es)
            hTt = [sb(es, f"hTt{i}", [128, KC, 512], BF16) for i in range(2)]
            b_hTt = bufs("hTt", 2, KC)
            wkv = sb(es, "wkv", [128, 3, KC, 128], BF16)
            b_wkv = Buf("wkv")
            for i in range(3):
                S.dma("pool", wkv[:, i, :, :], w_in_l[i], writes=[b_wkv])
            sq = [sb(es, f"sqL{i}", [128, 512], BF16) for i in range(3)]
            b_sq = bufs("sqL", 3)
            lnc = sb(es, "lnc", [128, 512], F32)
            rstdc = sb(es, "rstdc", [128, 512], F32)
            b_lnc, b_rstdc = Buf("lnc"), Buf("rstdc")
            t1 = sb(es, "t1L", [64, 512], F32)
            t2 = sb(es, "t2L", [64, 512], F32)
            b_t1, b_t2 = Buf("t1L"), Buf("t2L")

            nxt_set = prep(x_all, 0)
            ck("L1")
            for t in range(NT):
                hT, bh = hTt[t % 2], b_hTt[t % 2]
                finish(nxt_set, hT, bh, 0)
                ck("L2")
                if t + 1 < NT:
                    nxt_set = prep(x_all, (t + 1) * 512)
                pb = [bank_ring.next() for _ in range(4)]
                lhs = [(wkv[:, 0, :, :], 128, 0), (wkv[:, 1, :, :], 128, 0), (wkv[:, 2, :, :], 64, 0), (wkv[:, 2, :, :], 64, 64)]
                for g in range(4):
                    w, m, c0 = lhs[g]
                    for kc in range(KC):
                        mm(pb[g][0][0:m, :], w[:, kc, c0:c0 + m], hT[:, kc, :], kc == 0, kc == KC - 1, [b_wkv, bh[kc]],
                           [pb[g][1]])
                ck("L3")
                ck("L4")
                act(sq[0][:, :], pb[0][0][:, :], AF.Square, [pb[0][1]], [b_sq[0]])
                act(sq[1][:, :], pb[1][0][:, :], AF.Square, [pb[1][1]], [b_sq[1]])
                act(sq[2][0:64, :], pb[2][0][0:64, :], AF.Square, [pb[2][1]], [b_sq[2]])
                sb_, sbb = bank_ring.next()
                mm(sb_[:, :], ones_bf[:, :], sq[0][:, :], True, False, [b_const, b_sq[0]], [sbb])
                mm(sb_[:, :], ones_bf[:, :], sq[1][:, :], False, True, [b_const, b_sq[1]], [sbb])
                act(lnc[:, :], sb_[:, :], AF.Ln, [sbb], [b_lnc], scale=1.0 / 256, bias=cst[:, 0:1])
                act(rstdc[:, :], lnc[:, :], AF.Exp, [b_lnc], [b_rstdc], scale=-0.5)
                for kc in range(2):
                    stt("dve", ckvnT[:, kc, t * 512:(t + 1) * 512], pb[kc][0][:, :], kvag[:, kc:kc + 1], rstdc[:, :],
                        ALU.mult, ALU.mult, [pb[kc][1], b_par, b_rstdc], [b_ckvn[t]])
                rb, rbb = bank_ring.next()
                for st in range(4):
                    mm(rb[:, st:st + 1], sq[2][0:64, st * 128:(st + 1) * 128], ones_bf[0:64, 0:1], True, True,
                       [b_sq[2], b_const], [rbb])
                cp("dve", ssr[:, t * 4:(t + 1) * 4], rb[:, 0:4], [rbb], [b_ssr[t]])
                stt("dve", t1[:, :], pb[2][0][0:64, :], kg[0:64, 1:2], cosK[:, t * 512:(t + 1) * 512], ALU.mult, ALU.mult,
                    [pb[2][1], b_par, b_cosK[t]], [b_t1])
                stt("dve", t2[:, :], pb[3][0][0:64, :], kg[0:64, 2:3], sinK[:, t * 512:(t + 1) * 512], ALU.mult, ALU.mult,
                    [pb[3][1], b_par, b_sinK[t]], [b_t2])
                tt("pool", krT[:, t * 512:(t + 1) * 512], t1[:, :], t2[:, :], ALU.add, [b_t1, b_t2], [b_kr[t]])
                ck("L5")
            S.barrier()
            ck("L")
        lscope.close()

        for hf in range(NHALF):
            with ExitStack() as hs:
                yTc = sb(hs, "yTc", [128, 8, 1024], BF16)
                szaT = sb(hs, "szaT", [128, 8, 1024], BF16)
                cqnT = sb(hs, "cqnT", [128, 4, 1024], BF16)
                b_yTc = bufs(f"yTc{hf}", 8, 2)
                b_sza = bufs(f"sza{hf}", 8, 2)
                b_cqn = bufs(f"cqn{hf}", 4, 2)
                with ExitStack() as h2s:
                    hT2 = sb(h2s, "hT2", [128, KC, 1024], BF16)
                    b_hT2 = bufs(f"hT2{hf}", 2, KC)
                    hTh = sb(h2s, "hTh", [128, KC, 4], BF16)
                    b_hTh = Buf(f"hTh{hf}")

                    with ExitStack() as es:
                        prep, finish, junk = make_hT_builder(es, nxt=2)
                        set0 = prep(x_own, (hf * 2) * 512)
                        set1 = prep(x_own, (hf * 2 + 1) * 512)
                        finish(set0, hT2, b_hT2[0], 0)
                        finish(set1, hT2, b_hT2[1], 512)
                        xh = sb(es, "xh", [4, D], F32)
                        xnh = sb(es, "xnh", [4, D], BF16)
                        ssh = sb(es, "ssh", [4, 1], F32)
                        b_xh, b_xnh, b_ssh = Buf("xh"), Buf("xnh"), Buf("ssh")
                        S.dma("sp", xh[:], x_halo[hf * 4:(hf + 1) * 4, :], writes=[b_xh])
                        op("pool", lambda e: e.memset(ssh[:, :], 0.0), writes=[b_ssh])
                        act(junk[0:4, :], xh[:, :], AF.Square, [b_xh], [b_ssh], accum_out=ssh[:, 0:1])
                        act(ssh[:, :], ssh[:, :], AF.Ln, [b_ssh, b_const], [b_ssh], scale=1.0 / D, bias=cst[0:4, 0:1])
                        act(ssh[:, :], ssh[:, :], AF.Exp, [b_ssh], [b_ssh], scale=-0.5)
                        ts("dve", xnh[:, :], xh[:, :], ssh[:, 0:1], None, ALU.mult, None, [b_xh, b_ssh], [b_xnh])
                        for kc in range(KC):
                            op("pe", lambda e: e.transpose(out=tp_tiles[0][:, kc * 4:(kc + 1) * 4], in_=xnh[0:4, kc * 128:(kc + 1) * 128],
                                                           identity=ident[0:4, 0:4]), [b_xnh, b_const], [tp_bufs[0]])
                        for kc in range(KC):
                            act(hTh[:, kc, :], tp_tiles[0][:, kc * 4:(kc + 1) * 4], AF.Identity, [tp_bufs[0], b_mod], [b_hTh],
                                scale=gs_col[:, kc:kc + 1], bias=shift_col[:, kc:kc + 1])
                        S.barrier()
                        ck("2a")

                    with ExitStack() as es:
                        NW = 8
                        wr = [sb(es, f"wr{i}", [128, KC, 128], BF16) for i in range(NW)]
                        wring = Ring(list(zip(wr, bufs("wr", NW))))
                        order = list(range(3, 47))
                        loaded = {}
                        nload = [0]

                        def prefetch(upto):
                            while nload[0] < len(order) and nload[0] < upto:
                                cc = order[nload[0]]
                                w, wb = wring.next()
                                S.dma("pool", w[:], w_in_l[cc], writes=[wb])
                                loaded[cc] = (w, wb)
                                nload[0] += 1

                        def proj(w, wb, ot, bank, bb):
                            for kc in range(KC):
                                mm(bank[:, :], w[:, kc, :], hT2[:, kc, ot * 512:(ot + 1) * 512], kc == 0, kc == KC - 1,
                                   [wb, b_hT2[ot][kc]], [bb])

                        u_ext = [sb(es, f"uext{i}", [128, 514], F32) for i in range(2)]
                        b_uext = bufs("uext", 2)
                        cc_sb = [sb(es, f"ccsb{i}", [128, 512], F32) for i in range(2)]
                        b_ccsb = bufs("ccsb", 2)
                        acc = [sb(es, f"acc{i}", [128, 512], F32) for i in range(2)]
                        b_acc = bufs("acc", 2)
                        szt = [sb(es, f"szt{i}", [128, 512], F32) for i in range(2)]
                        b_szt = bufs("szt", 2)
                        tmp = [sb(es, f"tmpc{i}", [128, 512], F32) for i in range(2)]
                        b_tmp = bufs("tmpc", 2)
                        hc_sb = sb(es, "hc_sb", [128, 4], F32)
                        uh = sb(es, "uh", [128, 4], F32)
                        b_hc, b_uh = Buf("hc"), Buf("uh")
                        prefetch(4)
                        for i in range(8):
                            prefetch(4 * i + 8)
                            (wx, wxb), (wc, wcb), (wbm, wbb), (wz, wzb) = [loaded[3 + 4 * i + g] for g in range(4)]
                            hb, hbb = bank_ring.next()
                            for kc in range(KC):
                                mm(hb[:, 0:4], wx[:, kc, :], hTh[:, kc, :], kc == 0, kc == KC - 1, [wxb, b_hTh], [hbb])
                            for kc in range(KC):
                                mm(hb[:, 4:8], wc[:, kc, :], hTh[:, kc, :], kc == 0, kc == KC - 1, [wcb, b_hTh], [hbb])
                            cp("act", hc_sb[:, :], hb[:, 4:8], [hbb], [b_hc])
                            tt("dve", uh[:, :], hb[:, 0:4], hc_sb[:, :], ALU.mult, [hbb, b_hc], [b_uh])
                            tt("dve", uh[:, :], uh[:, :], hvalid[:, hf * 4:(hf + 1) * 4], ALU.mult, [b_uh, b_par], [b_uh])
                            for ot in range(2):
                                bx, bxb = bank_ring.next()
                                bc, bcb = bank_ring.next()
                                bbk, bbb = bank_ring.next()
                                bz, bzb = bank_ring.next()
                                proj(wx, wxb, ot, bx, bxb)
                                proj(wc, wcb, ot, bc, bcb)
                                proj(wbm, wbb, ot, bbk, bbb)
                                proj(wz, wzb, ot, bz, bzb)
                                cp("act", cc_sb[ot][:, :], bc[:, :], [bcb], [b_ccsb[ot]])
                                tt("dve", u_ext[ot][:, 2:514], bx[:, :], cc_sb[ot][:, :], ALU.mult, [bxb, b_ccsb[ot]], [b_uext[ot]])
                                cp("dve", u_ext[ot][:, 0:2], uh[:, ot * 2:(ot + 1) * 2], [b_uh], [b_uext[ot]])
                                ts("pool", acc[ot][:, :], u_ext[ot][:, 2:514], convw[:, i, 2:3], None, ALU.mult, None,
                                   [b_uext[ot], b_par], [b_acc[ot]])
                                stt("dve", acc[ot][:, :], u_ext[ot][:, 1:513], convw[:, i, 1:2], acc[ot][:, :], ALU.mult, ALU.add,
                                    [b_uext[ot], b_par, b_acc[ot]], [b_acc[ot]])
                                stt("dve", acc[ot][:, :], u_ext[ot][:, 0:512], convw[:, i, 0:1], acc[ot][:, :], ALU.mult, ALU.add,
                                    [b_uext[ot], b_par, b_acc[ot]], [b_acc[ot]])
                                act(szt[ot][:, :], bz[:, :], AF.Silu, [bzb], [b_szt[ot]])
                                tt("dve", tmp[ot][:, :], acc[ot][:, :], bbk[:, :], ALU.mult, [b_acc[ot], bbb], [b_tmp[ot]])
                                tt("pool", yTc[:, i, ot * 512:(ot + 1) * 512], tmp[ot][:, :], szt[ot][:, :], ALU.mult,
                                   [b_tmp[ot], b_szt[ot]], [b_yTc[i][ot]])
                        prefetch(32 + 4 + 2)
                        wq = [loaded[35 + c] for c in range(4)]
                        sqc = [sb(es, f"sqc{i}", [128, 512], BF16) for i in range(4)]
                        b_sqc = bufs("sqc", 4)
                        lnq2 = sb(es, "lnq2", [128, 512], F32)
                        rsq2 = sb(es, "rsq2", [128, 512], F32)
                        b_lnq2, b_rsq2 = Buf("lnq2"), Buf("rsq2")
                        for ot in range(2):
                            pbs = [bank_ring.next() for _ in range(4)]
                            for c in range(4):
                                proj(wq[c][0], wq[c][1], ot, pbs[c][0], pbs[c][1])
                                act(sqc[c][:, :], pbs[c][0][:, :], AF.Square, [pbs[c][1]], [b_sqc[c]])
                            sbk, sbb = bank_ring.next()
                            for c in range(4):
                                mm(sbk[:, :], ones_bf[:, :], sqc[c][:, :], c == 0, c == 3, [b_const, b_sqc[c]], [sbb])
                            act(lnq2[:, :], sbk[:, :], AF.Ln, [sbb], [b_lnq2], scale=1.0 / 512, bias=cst[:, 0:1])
                            act(rsq2[:, :], lnq2[:, :], AF.Exp, [b_lnq2], [b_rsq2], scale=-0.5)
                            for c in range(4):
                                stt("dve", cqnT[:, c, ot * 512:(ot + 1) * 512], pbs[c][0][:, :], qag[:, c:c + 1], rsq2[:, :],
                                    ALU.mult, ALU.mult, [pbs[c][1], b_par, b_rsq2], [b_cqn[c][ot]])
                        for i in range(8):
                            prefetch(36 + i + 3)
                            wz, wzb = loaded[39 + i]
                            for ot in range(2):
                                bz, bzb = bank_ring.next()
                                proj(wz, wzb, ot, bz, bzb)
                                act(szaT[:, i, ot * 512:(ot + 1) * 512], bz[:, :], AF.Silu, [bzb], [b_sza[i][ot]])
                        S.barrier()
                        ck("2b")

                s_lo = 2 * hf
                NG = s_lo + 2
                with ExitStack() as hs3:
                    yTa = sb(hs3, "yTa", [128, 8, 1024], BF16)
                    b_yTa = bufs(f"yTa{hf}", 8, 2)
                    with ExitStack() as es:
                        cosq = sb(es, "cosq", [64, 1024], F32)
                        sinq = sb(es, "sinq", [64, 1024], F32)
                        b_cosq, b_sinq = bufs("cosq", 2), bufs("sinq", 2)
                        qidx_bc = sb(es, "qidx_bc", [128, 1024], F32)
                        b_qidx = Buf("qidx")
                        with ExitStack() as ets:
                            tables = make_tables(ets, 512, "Q")
                            for ot in range(2):
                                c0 = hf * 1024 + ot * 512
                                tables(pos_own[0:1, c0:c0 + 512], cosq[:, ot * 512:(ot + 1) * 512], sinq[:, ot * 512:(ot + 1) * 512],
                                       b_cosq[ot], b_sinq[ot])
                            S.dma("sp", qidx_bc[:], qidx[0:1, hf * 1024:(hf + 1) * 1024].broadcast_to([128, 1024]),
                                  writes=[b_qidx])
                            S.barrier()
                            ck("3t")
                        wqb = [sb(es, f"wqb{i}", [128, 4, 256], BF16) for i in range(2)]
                        wkvb = [sb(es, f"wkvb{i}", [128, 2, 256], BF16) for i in range(2)]
                        b_wqb, b_wkvb = bufs("wqb", 2), bufs("wkvb", 2)
                        QTn = [sb(es, f"QTn{i}", [128, 1024], BF16) for i in range(2)]
                        QTr = [sb(es, f"QTr{i}", [64, 1024], BF16) for i in range(2)]
                        b_QT = bufs("QT", 2, 2)
                        KTg = [sb(es, f"KTg{i}", [128, 2048], BF16) for i in range(2)]
                        Vg = [sb(es, f"Vg{i}", [128, 16, 128], BF16) for i in range(2)]
                        b_KTg = bufs("KTg", 2, 4)
                        b_Vg = bufs("Vg", 2, 4)
                        rk = [sb(es, f"rk{i}", [128, 16], F32) for i in range(2)]
                        rkt = [sb(es, f"rkt{i}", [128, 16], F32) for i in range(2)]
                        b_rk, b_rkt = bufs("rk", 2), bufs("rkt", 2)
                        NPB = 6
                        Pb = [sb(es, f"Pb{i}", [128, 512], BF16) for i in range(NPB)]
                        pring = Ring(list(zip(Pb, bufs("Pb", NPB))))
                        sqn = sb(es, "sqn", [128, 512], BF16)
                        sqr = sb(es, "sqr", [64, 512], BF16)
                        sqk = [sb(es, f"sqk{i}", [128, 512], BF16) for i in range(2)]
                        b_sqn, b_sqr, b_sqk = Buf("sqn"), Buf("sqr"), bufs("sqk", 2)
                        lnq3 = sb(es, "lnq3", [128, 512], F32)
                        rsq3 = sb(es, "rsq3", [128, 512], F32)
                        b_lnq3, b_rsq3 = Buf("lnq3"), Buf("rsq3")
                        q1 = sb(es, "q1", [64, 512], F32)
                        q2 = sb(es, "q2", [64, 512], F32)
                        b_q1, b_q2 = Buf("q1"), Buf("q2")
                        rec = [sb(es, f"rec{i}", [128, 512], F32) for i in range(2)]
                        onrm = [sb(es, f"onrm{i}", [128, 512], F32) for i in range(2)]
                        b_rec, b_onrm = bufs("rec", 2), bufs("onrm", 2)
                        OT = [(banks[0], bank_bufs[0]), (banks[1], bank_bufs[1])]
                        SM = [(banks[2], bank_bufs[2]), (banks[3], bank_bufs[3])]
                        r3 = Ring([(banks[i], bank_bufs[i]) for i in (4, 5, 6, 7)])
                        accs = [[sb(es, f"accs{a}{b}", [128, 512], F32) for b in range(2)] for a in range(2)]
                        b_accs = bufs("accs", 2, 2)
                        junk3 = sb(es, "junk3", [128, 128], BF16)
                        osb = [sb(es, f"osb{i}", [128, 512], F32) for i in range(2)]
                        ssb = [sb(es, f"ssb{i}", [128, 512], F32) for i in range(2)]
                        b_osb, b_ssb = bufs("osb", 2), bufs("ssb", 2)
                        ATT_BIAS = -0.5 * math.log(192.0)

                        S.dma("pool", wqb[0][:], wqb_l[0], writes=[b_wqb[0]])
                        S.dma("pool", wkvb[0][:], wkvb_l[0], writes=[b_wkvb[0]])

                        def item_Q(h):
                            hp = h % 2
                            for ot in range(2):
                                op("pool", lambda e: e.memset(accs[hp][ot][:, :], 0.0), writes=[b_accs[hp][ot]])
                            if h + 1 < 8:
                                S.dma("pool", wqb[1 - hp][:], wqb_l[h + 1], writes=[b_wqb[1 - hp]])
                                S.dma("pool", wkvb[1 - hp][:], wkvb_l[h + 1], writes=[b_wkvb[1 - hp]])
                            for ot in range(2):
                                cs = slice(ot * 512, (ot + 1) * 512)
                                bn, bnb = r3.next()
                                for kc in range(4):
                                    mm(bn[:, :], wqb[hp][:, kc, 0:128], cqnT[:, kc, cs], kc == 0, kc == 3,
                                       [b_wqb[hp], b_cqn[kc][ot]], [bnb])
                                br, brb = r3.next()
                                for kc in range(4):
                                    mm(br[0:64, :], wqb[hp][:, kc, 128:192], cqnT[:, kc, cs], kc == 0, kc == 3,
                                       [b_wqb[hp], b_cqn[kc][ot]], [brb])
                                yield
                                act(sqn[:, :], bn[:, :], AF.Square, [bnb], [b_sqn])
                                act(sqr[:, :], br[0:64, :], AF.Square, [brb], [b_sqr])
                                bs_, bsb = r3.next()
                                mm(bs_[:, :], ones_bf[:, :], sqn[:, :], True, False, [b_const, b_sqn], [bsb])
                                mm(bs_[:, :], ones_bf[0:64, :], sqr[:, :], False, True, [b_const, b_sqr], [bsb])
                                act(lnq3[:, :], bs_[:, :], AF.Ln, [bsb, b_const], [b_lnq3], scale=1.0 / 192, bias=cst[:, 0:1])
                                act(rsq3[:, :], lnq3[:, :], AF.Exp, [b_lnq3], [b_rsq3], scale=-0.5)
                                stt("dve", QTn[hp][:, cs], bn[:, :], qg[:, 0:1], rsq3[:, :], ALU.mult, ALU.mult,
                                    [bnb, b_par, b_rsq3], [b_QT[hp][ot]])
                                stt("dve", q1[:, :], br[0:64, :], qg[0:64, 1:2], cosq[:, cs], ALU.mult, ALU.mult,
                                    [brb, b_par, b_cosq[ot]], [b_q1])
                                yield
                                bp, bpb = r3.next()
                                for kc in range(4):
                                    mm(bp[0:64, :], wqb[hp][:, kc, 192:256], cqnT[:, kc, cs], kc == 0, kc == 3,
                                       [b_wqb[hp], b_cqn[kc][ot]], [bpb])
                                stt("dve", q2[:, :], bp[0:64, :], qg[0:64, 2:3], sinq[:, cs], ALU.mult, ALU.mult,
                                    [bpb, b_par, b_sinq[ot]], [b_q2])
                                tt("pool", q1[:, :], q1[:, :], q2[:, :], ALU.add, [b_q1, b_q2], [b_q1])
                                tt("pool", QTr[hp][:, cs], q1[:, :], rsq3[0:64, :], ALU.mult, [b_q1, b_rsq3], [b_QT[hp][ot]])
                                yield
                            if DBG and h == 0 and hf == 0:
                                S.dma("sp", dbg_qn, QTn[hp][:], reads=b_QT[hp])
                                S.dma("sp", dbg_qr, QTr[hp][:], reads=b_QT[hp])
                                S.dma("sp", dbg_kr, krT[:, 0:2048], reads=b_kr[0:4])
                                S.dma("sp", dbg_cs[:, 0:1024], cosq[:], reads=b_cosq)
                                S.dma("sp", dbg_cs[:, 1024:2048], sinq[:], reads=b_sinq)

                        def item_G(h, G):
                            hp = h % 2
                            gp = (h * NG + G) % 2
                            cached = G < s_lo
                            if not cached:
                                op("pool", lambda e: e.memset(rkt[gp][:, :], 0.0), writes=[b_rkt[gp]])
                            for t4 in range(4):
                                gt = 4 * G + t4
                                bk, bkb = r3.next()
                                for kc in range(2):
                                    mm(bk[:, :], wkvb[hp][:, kc, 0:128], ckvnT[:, kc, gt * 512:(gt + 1) * 512], kc == 0, kc == 1,
                                       [b_wkvb[hp], b_ckvn[gt]], [bkb])
                                ts("dve", KTg[gp][:, t4 * 512:(t4 + 1) * 512], bk[:, :], kg[:, 0:1], None, ALU.mult, None,
                                   [bkb, b_par], [b_KTg[gp][t4]])
                                yield
                            if cached:
                                for b4 in range(4):
                                    bv, bvb = r3.next()
                                    for q in range(4):
                                        blk = 16 * G + b4 * 4 + q
                                        for kc in range(2):
                                            mm(bv[:, q * 128:(q + 1) * 128], ckvnT[:, kc, blk * 128:(blk + 1) * 128],
                                               wkvb[hp][:, kc, 128:256], kc == 0, kc == 1, [b_wkvb[hp], b_ckvn[blk // 4]], [bvb])
                                    cp("dve", Vg[gp][:, b4 * 4:(b4 + 1) * 4, :], bv[:, :].rearrange("p (a b) -> p a b", a=4),
                                       [bvb], [b_Vg[gp][b4]])
                                    yield
                                return
                            for b2 in range(8):
                                bv, bvb = r3.next()
                                for q in range(2):
                                    blk = 16 * G + b2 * 2 + q
                                    for kc in range(2):
                                        mm(bv[:, q * 256:(q + 1) * 256], ckvnT[:, kc, blk * 128:(blk + 1) * 128],
                                           wkvb[hp][:, kc, 0:256], kc == 0, kc == 1, [b_wkvb[hp], b_ckvn[blk // 4]], [bvb])
                                for q in range(2):
                                    c = b2 * 2 + q
                                    act(junk3[:, :], bv[:, q * 256:q * 256 + 128], AF.Square, [bvb], [b_rkt[gp]],
                                        accum_out=rkt[gp][:, c:c + 1])
                                cp("dve", Vg[gp][:, b2 * 2:(b2 + 1) * 2, :],
                                   bv[:, :].rearrange("p (a b) -> p a b", a=2)[:, :, 128:256], [bvb], [b_Vg[gp][b2 // 2]])
                                yield
                            tt("dve", rkt[gp][:, :], rkt[gp][:, :], ssr[:, G * 16:(G + 1) * 16], ALU.add,
                               [b_rkt[gp]] + [b_ssr[4 * G + i] for i in range(4)], [b_rkt[gp]])
                            act(rkt[gp][:, :], rkt[gp][:, :], AF.Ln, [b_rkt[gp], b_const], [b_rkt[gp]], scale=1.0, bias=cst[:, 1:2])
                            act(rk_all[:, h, G * 16:(G + 1) * 16], rkt[gp][:, :], AF.Exp, [b_rkt[gp]], [b_rkall[h][G]], scale=-0.5)
                            if DBG and h == 0 and hf == 0 and G == 0:
                                S.dma("sp", dbg_k, KTg[gp][:], reads=b_KTg[gp])
                                S.dma("sp", dbg_v, Vg[gp][:], reads=b_Vg[gp])
                                S.dma("sp", dbg_rk, rk_all[:, h, 0:16], reads=[b_rkall[h][G]])

                        def item_A(h, G, filler=None):
                            hp = h % 2
                            gp = (h * NG + G) % 2
                            steps = []
                            for ot in range(2):
                                s = s_lo + ot
                                if s >= G:
                                    for kb in range(16):
                                        steps.append((ot, s, kb))
                            LAG = 2
                            pend = []
                            for i in range(len(steps) + LAG):
                                if i < len(steps):
                                    ot, s, kb = steps[i]
                                    cs = slice(ot * 512, (ot + 1) * 512)
                                    gblk = 16 * G + kb
                                    bs_, bsb = r3.next()
                                    mm(bs_[:, :], KTg[gp][:, kb * 128:(kb + 1) * 128], QTn[hp][:, cs], True, False,
                                       [b_KTg[gp][kb // 4], b_QT[hp][ot]], [bsb])
                                    mm(bs_[:, :], krT[0:64, gblk * 128:(gblk + 1) * 128], QTr[hp][:, cs], False, True,
                                       [b_kr[gblk // 4], b_QT[hp][ot]], [bsb])
                                    Pt, Ptb = pring.next()
                                    act(Pt[:, :], bs_[:, :], AF.Exp, [bsb, b_rkall[h][G]], [Ptb], scale=rk_all[:, h, gblk:gblk + 1])
                                    if G == s:
                                        stt("dve", Pt[:, :], qidx_bc[:, cs], kidx[:, gblk:gblk + 1], Pt[:, :], ALU.is_ge,
                                            ALU.mult, [b_qidx, b_par, Ptb], [Ptb])
                                    pend.append((ot, s, kb, Pt, Ptb))
                                if i >= LAG:
                                    ot, s, kb, Pt, Ptb = pend[i - LAG]
                                    first = (G == 0 and kb == 0)
                                    last = (G == s and kb == 15)
                                    mm(OT[ot][0][:, :], Vg[gp][:, kb, :], Pt[:, :], first, last, [b_Vg[gp][kb // 4], Ptb],
                                       [OT[ot][1]])
                                    if G == s:
                                        mm(SM[ot][0][:, :], ones_bf[:, :], Pt[:, :], kb == 0, kb == 15 and s == 0,
                                           [b_const, Ptb], [SM[ot][1]])
                                        if kb == 15 and s > 0:
                                            mm(SM[ot][0][:, :], ones_f[:, :], accs[hp][ot][:, :], False, True,
                                               [b_const, b_accs[hp][ot]], [SM[ot][1]])
                                    else:
                                        stt("dve", accs[hp][ot][:, :], Pt[:, :], 1.0, accs[hp][ot][:, :], ALU.mult, ALU.add,
                                            [b_accs[hp][ot], Ptb], [b_accs[hp][ot]])
                                if filler is not None:
                                    next(filler, None)
                            if filler is not None:
                                for _ in filler:
                                    pass

                        def item_F(h):
                            hp = h % 2
                            for ot in range(2):
                                cp("act", ssb[ot][:, :], SM[ot][0][:, :], [SM[ot][1]], [b_ssb[ot]])
                                cp("dve", osb[ot][:, :], OT[ot][0][:, :], [OT[ot][1]], [b_osb[ot]])
                            for ot in range(2):
                                cs = slice(ot * 512, (ot + 1) * 512)
                                op("dve", lambda e: e.reciprocal(out=rec[ot][:, :], in_=ssb[ot][:, :]), [b_ssb[ot]], [b_rec[ot]])
                                tt("pool", onrm[ot][:, :], osb[ot][:, :], rec[ot][:, :], ALU.mult, [b_osb[ot], b_rec[ot]], [b_onrm[ot]])
                                tt("pool", yTa[:, h, cs], onrm[ot][:, :], szaT[:, h, cs], ALU.mult, [b_onrm[ot], b_sza[h][ot]],
                                   [b_yTa[h][ot]])

                        import itertools
                        for _ in item_Q(0):
                            pass
                        for _ in item_G(0, 0):
                            pass
                        for h in range(8):
                            for G in range(NG):
                                if G + 1 < NG:
                                    filler = item_G(h, G + 1)
                                elif h + 1 < 8:
                                    filler = itertools.chain(item_Q(h + 1), item_G(h + 1, 0))
                                else:
                                    filler = None
                                item_A(h, G, filler)
                            item_F(h)

                        S.barrier()
                        ck("3")

                    if DBG:
                        S.dma("sp", dbg_yc[hf], yTc[:], reads=[b for l in b_yTc for b in l])
                        S.dma("sp", dbg_ya[hf], yTa[:], reads=[b for l in b_yTa for b in l])
                    with ExitStack() as es:
                        wo = [sb(es, f"wo{i}", [128, KC, 512], BF16) for i in range(2)]
                        b_wo = bufs("wo", 2)
                        NX = 4
                        xres = [sb(es, f"xres{i}", [128, 512], F32) for i in range(NX)]
                        b_xres = bufs("xres", NX)
                        o1 = [sb(es, f"o1{i}", [128, 512], F32) for i in range(2)]
                        b_o1 = bufs("o1", 2)
                        o2 = [sb(es, f"o2{i}", [128, 512], F32) for i in range(3)]
                        b_o2 = bufs("o2", 3)
                        items = [(ct, tb) for ct in range(4) for tb in range(8)]

                        def load_x(n):
                            ct, tb = items[n]
                            r0 = hf * 1024 + tb * 128
                            S.dma("sp", xres[n % NX][:], x_own[r0:r0 + 128, ct * 512:(ct + 1) * 512], writes=[b_xres[n % NX]])

                        S.dma("pool", wo[0][:], w_out_l[0], writes=[b_wo[0]])
                        load_x(0)
                        load_x(1)
                        for n, (ct, tb) in enumerate(items):
                            w, wb = wo[ct % 2], b_wo[ct % 2]
                            if tb == 0 and ct + 1 < 4:
                                S.dma("pool", wo[(ct + 1) % 2][:], w_out_l[ct + 1], writes=[b_wo[(ct + 1) % 2]])
                            if n + 2 < len(items):
                                load_x(n + 2)
                            r0 = hf * 1024 + tb * 128
                            ot = tb // 4
                            ts_ = slice(tb * 128, (tb + 1) * 128)
                            xi, oi, pi = n % NX, n % 3, n % 2
                            bo, bob = bank_ring.next()
                            for kc in range(KC):
                                if kc < 8:
                                    l, lb = yTc[:, kc, ts_], b_yTc[kc][ot]
                                else:
                                    l, lb = yTa[:, kc - 8, ts_], b_yTa[kc - 8][ot]
                                mm(bo[:, :], l, w[:, kc, :], kc == 0, kc == KC - 1, [lb, wb], [bob])
                            tt("dve", o1[pi][:, :], bo[:, :], gate_bc[:, ct * 512:(ct + 1) * 512], ALU.mult, [bob, b_gate],
                               [b_o1[pi]])
                            tt("pool", o2[oi][:, :], o1[pi][:, :], xres[xi][:, :], ALU.add, [b_o1[pi], b_xres[xi]],
                               [b_o2[oi]])
                            S.dma("sp", out_d[r0:r0 + 128, ct * 512:(ct + 1) * 512], o2[oi][:, :], reads=[b_o2[oi]])
                        S.barrier()
                        ck("4")

    except _Stop:
        pass
    S.off = False
    S.barrier()
    top.close()
    return nc


def _prep_inputs(x, c, positions, ada_w, ada_b, norm_g, w_in, conv_w, q_a_g, w_q_b, kv_a_g, w_kv_b, q_g, k_g, w_out):
    f = np.float32
    x = np.asarray(x, f)
    B, SEQ, _ = x.shape
    NT = SEQ // 512
    NSLOT = NT // 4
    NBLK = SEQ // 128
    positions = np.asarray(positions, np.int32)
    ada_w = np.asarray(ada_w, f)[0]
    w_in = np.asarray(w_in, f)[0]
    w_q_b = np.asarray(w_q_b, f)[0]
    w_kv_b = np.asarray(w_kv_b, f)[0]
    w_out = np.asarray(w_out, f)[0]

    def cols(v, n):
        return np.ascontiguousarray(np.asarray(v, f).reshape(n, 128).T)

    def wl(w):
        n = w.shape[1] // 128
        return w.reshape(KC, 128, n, 128).transpose(2, 1, 0, 3)

    perm = (np.arange(64) + 32) % 64
    wr = w_in[:, 4864:4928]
    chunks = [wl(w_in[:, 4608:4864]), wl(np.concatenate([wr, wr[:, perm]], axis=1))]
    conv = np.stack([w_in[:, 0:1024], w_in[:, 2048:3072], w_in[:, 1024:2048], w_in[:, 3072:4096]], 0)
    conv = conv.reshape(4, D, 8, 128).transpose(2, 0, 1, 3).reshape(32, D, 128)
    chunks.append(conv.reshape(32, KC, 128, 128).transpose(0, 2, 1, 3))
    chunks.append(wl(w_in[:, 4096:4608]))
    chunks.append(wl(w_in[:, 4928:5952]))
    w_in_l = np.ascontiguousarray(np.concatenate(chunks, 0))
    assert w_in_l.shape == (47, 128, KC, 128)
    ada_w_l = np.ascontiguousarray(ada_w.reshape(KC, 128, 12, 512).transpose(2, 1, 0, 3))
    w_out_l = np.ascontiguousarray(w_out.reshape(KC, 128, 4, 512).transpose(2, 1, 0, 3))
    wq = w_q_b.reshape(4, 128, 8, 192).transpose(2, 1, 0, 3)
    wqb_l = np.ascontiguousarray(np.concatenate([wq, wq[..., 128:192][..., perm]], -1))
    wkvb_l = np.ascontiguousarray(w_kv_b.reshape(2, 128, 8, 256).transpose(2, 1, 0, 3))

    def g3(g):
        g = np.asarray(g, f)[0]
        o = np.zeros((128, 3), f)
        o[:, 0] = g[0:128]
        o[0:64, 1] = g[128:192]
        o[0:64, 2] = g[128:192][perm]
        return o

    inv_freq = (10000.0 ** (-(np.arange(0, 64, 2, dtype=np.float64)) / 64.0)).astype(f)
    rope_c = np.zeros((64, 4), f)
    rope_c[:, 0] = np.concatenate([inv_freq, inv_freq])
    rope_c[:, 1] = np.concatenate([-np.ones(32), np.ones(32)]) * SHRINK
    rope_c[:, 2] = SHRINK
    kidx = (np.arange(NBLK)[None, :] * 128 + np.arange(128)[:, None]).astype(f)
    shared = dict(
        ada_w_l=ada_w_l, ada_b=np.asarray(ada_b, f).reshape(1, 6144), ng_col=cols(np.asarray(norm_g)[0], KC),
        w_in_l=w_in_l, convw_col=np.ascontiguousarray(np.asarray(conv_w, f)[0].reshape(3, 8, 128).transpose(2, 1, 0)),
        qag_col=cols(np.asarray(q_a_g)[0], 4), kvag_col=cols(np.asarray(kv_a_g)[0], 2), wqb_l=wqb_l, wkvb_l=wkvb_l,
        qg_col=g3(q_g), kg_col=g3(k_g), w_out_l=w_out_l, rope_c=rope_c, kidx=kidx)
    in_maps = []
    meta = []
    for core in range(NCORES):
        b, j = core // 4, core % 4
        tiles = [4 * s + j for s in range(NSLOT)]
        rows = np.concatenate([np.arange(T * 512, (T + 1) * 512) for T in tiles])
        halo = np.zeros((NSLOT * 2, D), f)
        hv = np.zeros((128, NSLOT * 2), f)
        for s, T in enumerate(tiles):
            if T > 0:
                halo[2 * s:2 * s + 2] = x[b, T * 512 - 2:T * 512]
                hv[:, 2 * s:2 * s + 2] = 1.0
        m = dict(shared)
        m.update(x_all=x[b], x_own=np.ascontiguousarray(x[b, rows]), x_halo=halo, halo_valid=hv,
                 pos_all=np.ascontiguousarray(positions[b][None, :]), pos_own=np.ascontiguousarray(positions[b, rows][None, :]),
                 qidx=rows.astype(f)[None, :], c_col=cols(np.asarray(c, f)[b], KC))
        in_maps.append(m)
        meta.append((b, rows))
    return in_maps, meta, (B, SEQ)


_NC_CACHE = {}


def kernel(**inputs):
    in_maps, meta, (B, SEQ) = _prep_inputs(**inputs)
    if SEQ not in _NC_CACHE:
        _NC_CACHE[SEQ] = build(SEQ)
    nc = _NC_CACHE[SEQ]
    res = run_bass_kernel_spmd(nc, in_maps, core_ids=list(range(NCORES)))
    out = np.empty((B, SEQ, D), np.float32)
    for core, (b, rows) in enumerate(meta):
        out[b, rows] = res.results[core]["out"]
    if DBG_HOOK is not None:
        DBG_HOOK(res)
    return out
```

```python
import math
from contextlib import ExitStack

import numpy as np
import concourse.bass as bass
import concourse.mybir as mybir
from concourse.bass_utils import run_bass_kernel_spmd

F32 = mybir.dt.float32
BF16 = mybir.dt.bfloat16
I32 = mybir.dt.int32
AF = mybir.ActivationFunctionType
ALU = mybir.AluOpType

D = 2048
KC = D // 128
NCORES = 8
EPS = 1e-6
TWO_PI = 2.0 * math.pi
CW1 = 6.28125
CW2 = TWO_PI - 6.28125
SHRINK = 1.0 - 2e-6
PI_SAFE = 3.141592
KSTOP = ""
DBG_HOOK = None


class Buf:
    __slots__ = ("name", "lw", "reads", "sem", "dcnt", "excl")

    def __init__(self, name, excl=False):
        self.name = name
        self.excl = excl
        self.lw = None
        self.reads = {}
        self.sem = None
        self.dcnt = 0


def bufs(name, *dims):
    if not dims:
        return Buf(name)
    return [bufs(f"{name}_{i}", *dims[1:]) for i in range(dims[0])]


class Sched:
    def __init__(self, nc):
        self.nc = nc
        self.eng = {"pe": nc.tensor, "act": nc.scalar, "dve": nc.vector, "pool": nc.gpsimd, "sp": nc.sync}
        self.sem = {k: nc.alloc_semaphore(name="es_" + k) for k in self.eng}
        self.cnt = {k: 0 for k in self.eng}
        self.seen = {k: {} for k in self.eng}
        self.dma_bufs = []
        self.free_sems = []
        self.off = False

    def _wait(self, e, deps):
        need = {}
        for d in deps:
            if d is None:
                continue
            k, v = d
            if k == e and (e == "pe" or v <= self.cnt[e] - 4):
                continue
            if need.get(k, 0) < v:
                need[k] = v
        seen = self.seen[e]
        for k, v in need.items():
            if isinstance(k, Buf):
                v = k.dcnt
                so = k.sem
            else:
                so = self.sem[k]
            if seen.get(k, 0) >= v:
                continue
            seen[k] = v
            self.eng[e].wait_ge(so, v)

    @staticmethod
    def _deps(reads, writes):
        deps = []
        for b in reads:
            deps.append(b.lw)
        for b in writes:
            deps.append(b.lw)
            deps.extend(b.reads.items())
        return deps

    def op(self, e, fn, reads=(), writes=()):
        if self.off:
            return None
        if any(b.excl for b in reads):
            writes = list(writes) + [b for b in reads if b.excl]
            reads = [b for b in reads if not b.excl]
        self._wait(e, self._deps(reads, writes))
        ins = fn(self.eng[e])
        self.cnt[e] += 1
        ins.then_inc(self.sem[e], 1)
        v = self.cnt[e]
        for b in reads:
            if b.reads.get(e, 0) < v:
                b.reads[e] = v
        for b in writes:
            b.lw = (e, v)
            b.reads = {}
        return ins

    def dma(self, q, out, in_, reads=(), writes=(), sembuf=None):
        if self.off:
            return None
        self._wait(q, self._deps(reads, writes))
        if sembuf is None:
            sembuf = writes[0] if writes else reads[0]
        if sembuf.sem is None:
            sembuf.sem = self.nc.alloc_semaphore(name=f"ds{len(self.dma_bufs)}_" + sembuf.name)
            self.dma_bufs.append(sembuf)
        ins = self.eng[q].dma_start(out=out, in_=in_)
        sembuf.dcnt += 16
        ins.then_inc(sembuf.sem, 16)
        for b in reads:
            b.reads[sembuf] = sembuf.dcnt
        for b in writes:
            b.lw = (sembuf, sembuf.dcnt)
            b.reads = {}
        return ins

    def barrier(self):
        if self.off:
            return
        deps = [(k, c) for k, c in self.cnt.items() if k != "sp" and c > 0]
        deps += [(b, b.dcnt) for b in self.dma_bufs]
        self._wait("sp", deps)
        self.eng["sp"].sem_inc(self.sem["sp"], 1)
        self.cnt["sp"] += 1
        for e in self.eng:
            if e != "sp":
                self._wait(e, [("sp", self.cnt["sp"])])
                for k, v in self.seen["sp"].items():
                    if self.seen[e].get(k, 0) < v:
                        self.seen[e][k] = v


class _Stop(Exception):
    pass


class Ring:
    def __init__(self, items):
        self.items = list(items)
        self.i = 0

    def next(self):
        it = self.items[self.i % len(self.items)]
        self.i += 1
        return it


def build(SEQ):
    NT = SEQ // 512
    NSLOT = NT // 4
    NHALF = NSLOT // 2
    NBLK = SEQ // 128
    NOWN = NSLOT * 512
    NCC = 47

    nc = bass.Bass("TRN2", target_bir_lowering=False)

    def din(name, shape, dt=F32):
        return nc.dram_tensor(name, list(shape), dt, kind="ExternalInput").ap()

    x_all = din("x_all", [SEQ, D])
    x_own = din("x_own", [NOWN, D])
    x_halo = din("x_halo", [NSLOT * 2, D])
    halo_valid = din("halo_valid", [128, NSLOT * 2])
    pos_all = din("pos_all", [1, SEQ], I32)
    pos_own = din("pos_own", [1, NOWN], I32)
    qidx = din("qidx", [1, NOWN])
    kidx_d = din("kidx", [128, NBLK])
    c_col_d = din("c_col", [128, KC])
    ada_w_l = din("ada_w_l", [12, 128, KC, 512])
    ada_b_d = din("ada_b", [1, 6144])
    ng_col_d = din("ng_col", [128, KC])
    w_in_l = din("w_in_l", [NCC, 128, KC, 128])
    convw_d = din("convw_col", [128, 8, 3])
    qag_d = din("qag_col", [128, 4])
    kvag_d = din("kvag_col", [128, 2])
    wqb_l = din("wqb_l", [8, 128, 4, 256])
    wkvb_l = din("wkvb_l", [8, 128, 2, 256])
    qg_d = din("qg_col", [128, 3])
    kg_d = din("kg_col", [128, 3])
    w_out_l = din("w_out_l", [4, 128, KC, 512])
    rope_c_d = din("rope_c", [64, 4])
    out_d = nc.dram_tensor("out", [NOWN, D], F32, kind="ExternalOutput").ap()
    DBG = DBG_HOOK is not None
    if DBG:
        dbg_yc = nc.dram_tensor("dbg_yc", [NHALF, 128, 8, 1024], BF16, kind="ExternalOutput").ap()
        dbg_ya = nc.dram_tensor("dbg_ya", [NHALF, 128, 8, 1024], BF16, kind="ExternalOutput").ap()
        dbg_qn = nc.dram_tensor("dbg_qn", [128, 1024], BF16, kind="ExternalOutput").ap()
        dbg_qr = nc.dram_tensor("dbg_qr", [64, 1024], BF16, kind="ExternalOutput").ap()
        dbg_k = nc.dram_tensor("dbg_k", [128, 2048], BF16, kind="ExternalOutput").ap()
        dbg_kr = nc.dram_tensor("dbg_kr", [64, 2048], BF16, kind="ExternalOutput").ap()
        dbg_v = nc.dram_tensor("dbg_v", [128, 16, 128], BF16, kind="ExternalOutput").ap()
        dbg_rk = nc.dram_tensor("dbg_rk", [128, 16], F32, kind="ExternalOutput").ap()
        dbg_acc = nc.dram_tensor("dbg_acc", [128, 512], F32, kind="ExternalOutput").ap()
        dbg_ssb = nc.dram_tensor("dbg_ssb", [128, 512], F32, kind="ExternalOutput").ap()
        dbg_ssb2 = nc.dram_tensor("dbg_ssb2", [128, 512], F32, kind="ExternalOutput").ap()
        dbg_cs = nc.dram_tensor("dbg_cs", [64, 2048], F32, kind="ExternalOutput").ap()

    S = Sched(nc)
    op = S.op

    def mm(out, lhsT, rhs, start, stop, reads, writes):
        return op("pe", lambda e: e.matmul(out, lhsT=lhsT, rhs=rhs, start=start, stop=stop), reads, writes)

    def act(out, in_, func, reads, writes, scale=1.0, bias=0.0, accum_out=None):
        if accum_out is not None:
            return op("act", lambda e: e.activation(out=out, in_=in_, func=func, bias=bias, scale=scale,
                                                    accum_out=accum_out), reads, writes)
        return op("act", lambda e: e.activation(out=out, in_=in_, func=func, bias=bias, scale=scale), reads, writes)

    def ts(eng, out, in0, s1, s2, op0, op1, reads, writes):
        if s2 is None:
            return op(eng, lambda e: e.tensor_scalar(out=out, in0=in0, scalar1=s1, scalar2=None, op0=op0), reads, writes)
        return op(eng, lambda e: e.tensor_scalar(out=out, in0=in0, scalar1=s1, scalar2=s2, op0=op0, op1=op1), reads, writes)

    def stt(eng, out, in0, scalar, in1, op0, op1, reads, writes):
        return op(eng, lambda e: e.scalar_tensor_tensor(out=out, in0=in0, scalar=scalar, in1=in1, op0=op0, op1=op1),
                  reads, writes)

    def tt(eng, out, in0, in1, o, reads, writes):
        return op(eng, lambda e: e.tensor_tensor(out=out, in0=in0, in1=in1, op=o), reads, writes)

    def cp(eng, out, in_, reads, writes):
        if eng == "act":
            return op(eng, lambda e: e.activation(out=out, in_=in_, func=AF.Copy), reads, writes)
        return op(eng, lambda e: e.tensor_copy(out=out, in_=in_), reads, writes)

    top = ExitStack()
    kstop = KSTOP

    def ck(name):
        if kstop == name:
            S.off = True

    uid = [0]

    def sb(es, name, shape, dt):
        uid[0] += 1
        return es.enter_context(nc.sbuf_tensor(f"{name}_u{uid[0]}", list(shape), dt))

    banks = [top.enter_context(nc.psum_tensor(f"pb{i}", [128, 512], F32)) for i in range(8)]
    bank_bufs = [Buf(f"pb{i}", excl=True) for i in range(8)]
    tp_tiles = [banks[6][:, :].bitcast(BF16), banks[7][:, :].bitcast(BF16)]
    tp_bufs = [bank_bufs[6], bank_bufs[7]]

    ckvnT = sb(top, "ckvnT", [128, 2, SEQ], BF16)
    krT = sb(top, "krT", [64, SEQ], BF16)
    ssr = sb(top, "ssr", [128, NBLK], F32)
    b_ckvn = bufs("ckvn", NT)
    b_kr = bufs("kr", NT)
    b_ssr = bufs("ssr", NT)
    ident = sb(top, "ident", [128, 128], BF16)
    ones_bf = sb(top, "ones_bf", [128, 128], BF16)
    ones_f = sb(top, "ones_f", [128, 128], F32)
    cst = sb(top, "cst", [128, 2], F32)
    rk_all = sb(top, "rk_all", [128, 8, NBLK], F32)
    b_rkall = bufs("rkall", 8, NBLK // 16)
    b_const = Buf("const")
    gs_col = sb(top, "gs_col", [128, KC], F32)
    shift_col = sb(top, "shift_col", [128, KC], F32)
    gate_bc = sb(top, "gate_bc", [128, D], F32)
    b_mod = Buf("mod")
    b_gate = Buf("gate")
    c_col = sb(top, "c_col", [128, KC], F32)
    ng_col = sb(top, "ng_col", [128, KC], F32)
    convw = sb(top, "convw", [128, 8, 3], F32)
    qag = sb(top, "qag", [128, 4], F32)
    kvag = sb(top, "kvag", [128, 2], F32)
    qg = sb(top, "qg", [128, 3], F32)
    kg = sb(top, "kg", [128, 3], F32)
    rope_c = sb(top, "rope_c", [64, 4], F32)
    kidx = sb(top, "kidx", [128, NBLK], F32)
    hvalid = sb(top, "hvalid", [128, NSLOT * 2], F32)
    b_par = Buf("par")

    for dst, src in ((c_col, c_col_d), (ng_col, ng_col_d), (convw, convw_d), (qag, qag_d), (kvag, kvag_d),
                     (qg, qg_d), (kg, kg_d), (rope_c, rope_c_d), (kidx, kidx_d), (hvalid, halo_valid)):
        S.dma("sp", dst[:], src, writes=[b_par])

    try:
        def make_tables(es, N, tag):
            posi = sb(es, f"posi{tag}", [64, N], I32)
            posf = sb(es, f"posf{tag}", [64, N], F32)
            ang = sb(es, f"ang{tag}", [64, N], F32)
            kf = [sb(es, f"kf{tag}{i}", [64, N], F32) for i in range(2)]
            ki = [sb(es, f"ki{tag}{i}", [64, N], I32) for i in range(2)]
            rr = [sb(es, f"rr{tag}{i}", [64, N], F32) for i in range(2)]
            b_posi, b_posf, b_ang = Buf("posi" + tag), Buf("posf" + tag), Buf("ang" + tag)
            b_kf, b_ki, b_rr = bufs("kf" + tag, 2), bufs("ki" + tag, 2), bufs("rr" + tag, 2)

            def tables(pos_src, cos2, sinS, b_cos, b_sin):
                S.dma("sp", posi[:], pos_src.broadcast_to([64, N]), writes=[b_posi])
                cp("dve", posf[:, :], posi[:, :], [b_posi], [b_posf])
                ts("dve", ang[:, :], posf[:, :], rope_c[:, 0:1], None, ALU.mult, None, [b_posf, b_par], [b_ang])
                for i, (eng, phase, outt, bo, ccol) in enumerate((("dve", 0.0, sinS, b_sin, 1), ("dve", 0.25, cos2, b_cos, 2))):
                    ts(eng, kf[i][:, :], ang[:, :], 1.0 / TWO_PI, phase, ALU.mult, ALU.add, [b_ang], [b_kf[i]])
                    cp(eng, ki[i][:, :], kf[i][:, :], [b_kf[i]], [b_ki[i]])
                    cp(eng, kf[i][:, :], ki[i][:, :], [b_ki[i]], [b_kf[i]])
                    stt(eng, rr[i][:, :], kf[i][:, :], -CW1, ang[:, :], ALU.mult, ALU.add, [b_kf[i], b_ang], [b_rr[i]])
                    stt(eng, rr[i][:, :], kf[i][:, :], -CW2, rr[i][:, :], ALU.mult, ALU.add, [b_kf[i], b_rr[i]], [b_rr[i]])
                    if phase != 0.0:
                        ts(eng, rr[i][:, :], rr[i][:, :], phase * TWO_PI, None, ALU.add, None, [b_rr[i]], [b_rr[i]])
                    ts(eng, rr[i][:, :], rr[i][:, :], -PI_SAFE, PI_SAFE, ALU.max, ALU.min, [b_rr[i]], [b_rr[i]])
                    act(outt, rr[i][:, :], AF.Sin, [b_rr[i], b_par], [bo], scale=rope_c[:, ccol:ccol + 1])

            return tables

        lscope = ExitStack()
        cosK = sb(lscope, "cosK", [64, SEQ], BF16)
        sinK = sb(lscope, "sinK", [64, SEQ], BF16)
        b_cosK, b_sinK = bufs("cosK", NT), bufs("sinK", NT)

        with ExitStack() as es:
            identf = sb(es, "identf", [128, 128], F32)
            b_idf = Buf("identf")
            op("pool", lambda e: e.memset(identf[:, :], 0.0), writes=[b_idf])
            op("pool", lambda e: e.affine_select(out=identf[:, :], in_=identf[:, :], compare_op=ALU.not_equal, fill=1.0,
                                                 base=0, pattern=[[-1, 128]], channel_multiplier=1), writes=[b_idf])
            cp("dve", ident[:, :], identf[:, :], [b_idf], [b_const])
            op("pool", lambda e: e.memset(ones_f[:, :], 1.0), writes=[b_const])
            op("pool", lambda e: e.memset(cst[:, 0:1], EPS), writes=[b_const])
            op("pool", lambda e: e.memset(cst[:, 1:2], 192.0 * EPS), writes=[b_const])
            cp("dve", ones_bf[:, :], ones_f[:, :], [b_const], [b_const])

            tablesK = make_tables(es, 512, "K")
            tdone = [0]

            def tables_upto(n):
                while tdone[0] < min(n, NT):
                    t = tdone[0]
                    tablesK(pos_all[0:1, t * 512:(t + 1) * 512], cosK[:, t * 512:(t + 1) * 512],
                            sinK[:, t * 512:(t + 1) * 512], b_cosK[t], b_sinK[t])
                    tdone[0] += 1

            sc_bf = sb(es, "sc_bf", [128, KC], BF16)
            b_sc = Buf("sc")
            act(sc_bf[:, :], c_col[:, :], AF.Silu, [b_par], [b_sc])
            modrow = sb(es, "modrow", [1, 6144], F32)
            adab = sb(es, "adab", [1, 6144], F32)
            b_adab = Buf("adab")
            S.dma("sp", adab[:], ada_b_d, writes=[b_adab])
            b_modrow = Buf("modrow")
            adaw = [sb(es, f"adaw{i}", [128, KC, 512], BF16) for i in range(2)]
            b_adaw = bufs("adaw", 2)
            for ct in range(12):
                wt, wb = adaw[ct % 2], b_adaw[ct % 2]
                S.dma("pool", wt[:], ada_w_l[ct], writes=[wb])
                bk, bb = banks[ct % 2], bank_bufs[ct % 2]
                for kc in range(KC):
                    mm(bk[0:1, :], sc_bf[:, kc:kc + 1], wt[:, kc, :], kc == 0, kc == KC - 1, [b_sc, wb], [bb])
                tt("dve", modrow[0:1, ct * 512:(ct + 1) * 512], bk[0:1, :], adab[0:1, ct * 512:(ct + 1) * 512], ALU.add,
                   [bb, b_adab], [b_modrow])
                tables_upto((ct + 1) * NT // 12 + 1)
            tables_upto(NT)
            cb, cbb = banks[2], bank_bufs[2]
            for c in range(32):
                mm(cb[:, c:c + 1], modrow[0:1, c * 128:(c + 1) * 128], ones_f[0:1, 0:1], True, True, [b_modrow, b_const], [cbb])
            cp("dve", shift_col[:, :], cb[:, 0:KC], [cbb], [b_mod])
            stt("dve", gs_col[:, :], cb[:, KC:2 * KC], 1.0, ng_col[:, :], ALU.add, ALU.mult, [cbb, b_par], [b_mod])
            for ct in range(4):
                bk, bb = banks[3 + ct % 2], bank_bufs[3 + ct % 2]
                mm(bk[:, :], ones_f[0:1, :], modrow[0:1, 4096 + ct * 512:4096 + (ct + 1) * 512], True, True,
                   [b_modrow, b_const], [bb])
                cp("act", gate_bc[:, ct * 512:(ct + 1) * 512], bk[:, :], [bb], [b_gate])
            S.barrier()
            ck("prologue")

        bank_ring = Ring(list(zip(banks[:6], bank_bufs[:6])))
        ring8 = Ring(list(zip(banks, bank_bufs)))

        def make_hT_builder(es, nxt=3):
            xt = [sb(es, f"xt{i}", [128, D], F32) for i in range(nxt)]
            b_xt = bufs("xt", nxt)
            xn = [sb(es, f"xn{i}", [128, D], BF16) for i in range(8)]
            b_xn = bufs("xn", 8)
            junk = sb(es, "junk", [128, D], BF16)
            ssq = [sb(es, f"ssq{i}", [128, 4], F32) for i in range(2)]
            lnq = [sb(es, f"lnq{i}", [128, 4], F32) for i in range(2)]
            rsq = [sb(es, f"rsq{i}", [128, 4], F32) for i in range(2)]
            b_ssq = bufs("ssq", 2)
            b_lnq = bufs("lnq", 2)
            b_rsq = bufs("rsq", 2)
            state = {"n": 0, "xi": 0}

            def prep(src, row0):
                p2 = state["n"] % 2
                state["n"] += 1
                op("pool", lambda e: e.memset(ssq[p2][:, :], 0.0), writes=[b_ssq[p2]])
                xis = []
                for st in range(4):
                    xi = state["xi"] % nxt
                    state["xi"] += 1
                    xis.append(xi)
                    S.dma("sp", xt[xi][:], src[row0 + st * 128:row0 + (st + 1) * 128, :], writes=[b_xt[xi]])
                    act(junk[:, :], xt[xi][:, :], AF.Square, [b_xt[xi]], [b_ssq[p2]], accum_out=ssq[p2][:, st:st + 1])
                    if nxt < 4 or st == 3:
                        pass
                act(lnq[p2][:, :], ssq[p2][:, :], AF.Ln, [b_ssq[p2]], [b_lnq[p2]], scale=1.0 / D, bias=cst[:, 0:1])
                act(rsq[p2][:, :], lnq[p2][:, :], AF.Exp, [b_lnq[p2]], [b_rsq[p2]], scale=-0.5)
                return p2, xis

            def prep_full(src, row0):
                p2 = state["n"] % 2
                state["n"] += 1
                for st in range(4):
                    xi = state["xi"] % nxt
                    state["xi"] += 1
                    ni = p2 * 4 + st
                    S.dma("sp", xt[xi][:], src[row0 + st * 128:row0 + (st + 1) * 128, :], writes=[b_xt[xi]])
                    op("pool", lambda e: e.memset(ssq[p2][:, st:st + 1], 0.0), writes=[b_ssq[p2]])
                    act(junk[:, :], xt[xi][:, :], AF.Square, [b_xt[xi]], [b_ssq[p2]], accum_out=ssq[p2][:, st:st + 1])
                    act(lnq[p2][:, st:st + 1], ssq[p2][:, st:st + 1], AF.Ln, [b_ssq[p2]], [b_lnq[p2]], scale=1.0 / D, bias=cst[:, 0:1])
                    act(rsq[p2][:, st:st + 1], lnq[p2][:, st:st + 1], AF.Exp, [b_lnq[p2]], [b_rsq[p2]], scale=-0.5)
                    ts("dve", xn[ni][:, :], xt[xi][:, :], rsq[p2][:, st:st + 1], None, ALU.mult,
                       None, [b_xt[xi], b_rsq[p2]], [b_xn[ni]])
                return p2

            def finish(p2, hT, b_hT, col0):
                for kc in range(KC):
                    h = kc % 2
                    tph = tp_tiles[h][:, 0:512]
                    for st in range(4):
                        ni = p2 * 4 + st
                        op("pe", lambda e: e.transpose(out=tph[:, st * 128:(st + 1) * 128],
                                                       in_=xn[ni][:, kc * 128:(kc + 1) * 128], identity=ident[:, :]),
                           [b_xn[ni], b_const], [tp_bufs[h]])
                    dst = hT[:, kc, col0:col0 + 512]
                    if kc % 2 == 0:
                        act(dst, tph, AF.Identity, [tp_bufs[h], b_mod], [b_hT[kc]], scale=gs_col[:, kc:kc + 1],
                            bias=shift_col[:, kc:kc + 1])
                    else:
                        ts("dve", dst, tph, gs_col[:, kc:kc + 1], shift_col[:, kc:kc + 1], ALU.mult, ALU.add,
                           [tp_bufs[h], b_mod], [b_hT[kc]])

            return prep_full, finish, junk

        with ExitStack() as es:
            prep, finish, _ = make_hT_builder(es)
            hTt = [sb(es, f"hTt{i}", [128, KC, 512], BF16) for i in range(2)]
            b_hTt = bufs("hTt", 2, KC)
            wkv = sb(es, "wkv", [128, 3, KC, 128], BF16)
            b_wkv = Buf("wkv")
            for i in range(3):
                S.dma("pool", wkv[:, i, :, :], w_in_l[i], writes=[b_wkv])
            sq = [sb(es, f"sqL{i}", [128, 512], BF16) for i in range(3)]
            b_sq = bufs("sqL", 3)
            lnc = sb(es, "lnc", [128, 512], F32)
            rstdc = sb(es, "rstdc", [128, 512], F32)
            b_lnc, b_rstdc = Buf("lnc"), Buf("rstdc")
            t1 = sb(es, "t1L", [64, 512], F32)
            t2 = sb(es, "t2L", [64, 512], F32)
            b_t1, b_t2 = Buf("t1L"), Buf("t2L")

            nxt_set = prep(x_all, 0)
            ck("L1")
            for t in range(NT):
                hT, bh = hTt[t % 2], b_hTt[t % 2]
                finish(nxt_set, hT, bh, 0)
                ck("L2")
                if t + 1 < NT:
                    nxt_set = prep(x_all, (t + 1) * 512)
                pb = [bank_ring.next() for _ in range(4)]
                lhs = [(wkv[:, 0, :, :], 128, 0), (wkv[:, 1, :, :], 128, 0), (wkv[:, 2, :, :], 64, 0), (wkv[:, 2, :, :], 64, 64)]
                for g in range(4):
                    w, m, c0 = lhs[g]
                    for kc in range(KC):
                        mm(pb[g][0][0:m, :], w[:, kc, c0:c0 + m], hT[:, kc, :], kc == 0, kc == KC - 1, [b_wkv, bh[kc]],
                           [pb[g][1]])
                ck("L3")
                ck("L4")
                act(sq[0][:, :], pb[0][0][:, :], AF.Square, [pb[0][1]], [b_sq[0]])
                act(sq[1][:, :], pb[1][0][:, :], AF.Square, [pb[1][1]], [b_sq[1]])
                act(sq[2][0:64, :], pb[2][0][0:64, :], AF.Square, [pb[2][1]], [b_sq[2]])
                sb_, sbb = bank_ring.next()
                mm(sb_[:, :], ones_bf[:, :], sq[0][:, :], True, False, [b_const, b_sq[0]], [sbb])
                mm(sb_[:, :], ones_bf[:, :], sq[1][:, :], False, True, [b_const, b_sq[1]], [sbb])
                act(lnc[:, :], sb_[:, :], AF.Ln, [sbb], [b_lnc], scale=1.0 / 256, bias=cst[:, 0:1])
                act(rstdc[:, :], lnc[:, :], AF.Exp, [b_lnc], [b_rstdc], scale=-0.5)
                for kc in range(2):
                    stt("dve", ckvnT[:, kc, t * 512:(t + 1) * 512], pb[kc][0][:, :], kvag[:, kc:kc + 1], rstdc[:, :],
                        ALU.mult, ALU.mult, [pb[kc][1], b_par, b_rstdc], [b_ckvn[t]])
                rb, rbb = bank_ring.next()
                for st in range(4):
                    mm(rb[:, st:st + 1], sq[2][0:64, st * 128:(st + 1) * 128], ones_bf[0:64, 0:1], True, True,
                       [b_sq[2], b_const], [rbb])
                cp("dve", ssr[:, t * 4:(t + 1) * 4], rb[:, 0:4], [rbb], [b_ssr[t]])
                stt("dve", t1[:, :], pb[2][0][0:64, :], kg[0:64, 1:2], cosK[:, t * 512:(t + 1) * 512], ALU.mult, ALU.mult,
                    [pb[2][1], b_par, b_cosK[t]], [b_t1])
                stt("dve", t2[:, :], pb[3][0][0:64, :], kg[0:64, 2:3], sinK[:, t * 512:(t + 1) * 512], ALU.mult, ALU.mult,
                    [pb[3][1], b_par, b_sinK[t]], [b_t2])
                tt("pool", krT[:, t * 512:(t + 1) * 512], t1[:, :], t2[:, :], ALU.add, [b_t1, b_t2], [b_kr[t]])
                ck("L5")
            S.barrier()
            ck("L")
        lscope.close()

        for hf in range(NHALF):
            with ExitStack() as hs:
                yTc = sb(hs, "yTc", [128, 8, 1024], BF16)
                szaT = sb(hs, "szaT", [128, 8, 1024], BF16)
                cqnT = sb(hs, "cqnT", [128, 4, 1024], BF16)
                b_yTc = bufs(f"yTc{hf}", 8, 2)
                b_sza = bufs(f"sza{hf}", 8, 2)
                b_cqn = bufs(f"cqn{hf}", 4, 2)
                with ExitStack() as h2s:
                    hT2 = sb(h2s, "hT2", [128, KC, 1024], BF16)
                    b_hT2 = bufs(f"hT2{hf}", 2, KC)
                    hTh = sb(h2s, "hTh", [128, KC, 4], BF16)
                    b_hTh = Buf(f"hTh{hf}")

                    with ExitStack() as es:
                        prep, finish, junk = make_hT_builder(es, nxt=2)
                        set0 = prep(x_own, (hf * 2) * 512)
                        set1 = prep(x_own, (hf * 2 + 1) * 512)
                        finish(set0, hT2, b_hT2[0], 0)
                        finish(set1, hT2, b_hT2[1], 512)
                        xh = sb(es, "xh", [4, D], F32)
                        xnh = sb(es, "xnh", [4, D], BF16)
                        ssh = sb(es, "ssh", [4, 1], F32)
                        b_xh, b_xnh, b_ssh = Buf("xh"), Buf("xnh"), Buf("ssh")
                        S.dma("sp", xh[:], x_halo[hf * 4:(hf + 1) * 4, :], writes=[b_xh])
                        op("pool", lambda e: e.memset(ssh[:, :], 0.0), writes=[b_ssh])
                        act(junk[0:4, :], xh[:, :], AF.Square, [b_xh], [b_ssh], accum_out=ssh[:, 0:1])
                        act(ssh[:, :], ssh[:, :], AF.Ln, [b_ssh, b_const], [b_ssh], scale=1.0 / D, bias=cst[0:4, 0:1])
                        act(ssh[:, :], ssh[:, :], AF.Exp, [b_ssh], [b_ssh], scale=-0.5)
                        ts("dve", xnh[:, :], xh[:, :], ssh[:, 0:1], None, ALU.mult, None, [b_xh, b_ssh], [b_xnh])
                        for kc in range(KC):
                            op("pe", lambda e: e.transpose(out=tp_tiles[0][:, kc * 4:(kc + 1) * 4], in_=xnh[0:4, kc * 128:(kc + 1) * 128],
                                                           identity=ident[0:4, 0:4]), [b_xnh, b_const], [tp_bufs[0]])
                        for kc in range(KC):
                            act(hTh[:, kc, :], tp_tiles[0][:, kc * 4:(kc + 1) * 4], AF.Identity, [tp_bufs[0], b_mod], [b_hTh],
                                scale=gs_col[:, kc:kc + 1], bias=shift_col[:, kc:kc + 1])
                        S.barrier()
                        ck("2a")

                    with ExitStack() as es:
                        NW = 8
                        wr = [sb(es, f"wr{i}", [128, KC, 128], BF16) for i in range(NW)]
                        wring = Ring(list(zip(wr, bufs("wr", NW))))
                        order = list(range(3, 47))
                        loaded = {}
                        nload = [0]

                        def prefetch(upto):
                            while nload[0] < len(order) and nload[0] < upto:
                                cc = order[nload[0]]
                                w, wb = wring.next()
                                S.dma("pool", w[:], w_in_l[cc], writes=[wb])
                                loaded[cc] = (w, wb)
                                nload[0] += 1

                        def proj(w, wb, ot, bank, bb):
                            for kc in range(KC):
                                mm(bank[:, :], w[:, kc, :], hT2[:, kc, ot * 512:(ot + 1) * 512], kc == 0, kc == KC - 1,
                                   [wb, b_hT2[ot][kc]], [bb])

                        u_ext = [sb(es, f"uext{i}", [128, 514], F32) for i in range(2)]
                        b_uext = bufs("uext", 2)
                        cc_sb = [sb(es, f"ccsb{i}", [128, 512], F32) for i in range(2)]
                        b_ccsb = bufs("ccsb", 2)
                        acc = [sb(es, f"acc{i}", [128, 512], F32) for i in range(2)]
                        b_acc = bufs("acc", 2)
                        szt = [sb(es, f"szt{i}", [128, 512], F32) for i in range(2)]
                        b_szt = bufs("szt", 2)
                        tmp = [sb(es, f"tmpc{i}", [128, 512], F32) for i in range(2)]
                        b_tmp = bufs("tmpc", 2)
                        hc_sb = sb(es, "hc_sb", [128, 4], F32)
                        uh = sb(es, "uh", [128, 4], F32)
                        b_hc, b_uh = Buf("hc"), Buf("uh")
                        prefetch(4)
                        for i in range(8):
                            prefetch(4 * i + 8)
                            (wx, wxb), (wc, wcb), (wbm, wbb), (wz, wzb) = [loaded[3 + 4 * i + g] for g in range(4)]
                            hb, hbb = bank_ring.next()
                            for kc in range(KC):
                                mm(hb[:, 0:4], wx[:, kc, :], hTh[:, kc, :], kc == 0, kc == KC - 1, [wxb, b_hTh], [hbb])
                            for kc in range(KC):
                                mm(hb[:, 4:8], wc[:, kc, :], hTh[:, kc, :], kc == 0, kc == KC - 1, [wcb, b_hTh], [hbb])
                            cp("act", hc_sb[:, :], hb[:, 4:8], [hbb], [b_hc])
                            tt("dve", uh[:, :], hb[:, 0:4], hc_sb[:, :], ALU.mult, [hbb, b_hc], [b_uh])
                            tt("dve", uh[:, :], uh[:, :], hvalid[:, hf * 4:(hf + 1) * 4], ALU.mult, [b_uh, b_par], [b_uh])
                            for ot in range(2):
                                bx, bxb = bank_ring.next()
                                bc, bcb = bank_ring.next()
                                bbk, bbb = bank_ring.next()
                                bz, bzb = bank_ring.next()
                                proj(wx, wxb, ot, bx, bxb)
                                proj(wc, wcb, ot, bc, bcb)
                                proj(wbm, wbb, ot, bbk, bbb)
                                proj(wz, wzb, ot, bz, bzb)
                                cp("act", cc_sb[ot][:, :], bc[:, :], [bcb], [b_ccsb[ot]])
                                tt("dve", u_ext[ot][:, 2:514], bx[:, :], cc_sb[ot][:, :], ALU.mult, [bxb, b_ccsb[ot]], [b_uext[ot]])
                                cp("dve", u_ext[ot][:, 0:2], uh[:, ot * 2:(ot + 1) * 2], [b_uh], [b_uext[ot]])
                                ts("pool", acc[ot][:, :], u_ext[ot][:, 2:514], convw[:, i, 2:3], None, ALU.mult, None,
                                   [b_uext[ot], b_par], [b_acc[ot]])
                                stt("dve", acc[ot][:, :], u_ext[ot][:, 1:513], convw[:, i, 1:2], acc[ot][:, :], ALU.mult, ALU.add,
                                    [b_uext[ot], b_par, b_acc[ot]], [b_acc[ot]])
                                stt("dve", acc[ot][:, :], u_ext[ot][:, 0:512], convw[:, i, 0:1], acc[ot][:, :], ALU.mult, ALU.add,
                                    [b_uext[ot], b_par, b_acc[ot]], [b_acc[ot]])
                                act(szt[ot][:, :], bz[:, :], AF.Silu, [bzb], [b_szt[ot]])
                                tt("dve", tmp[ot][:, :], acc[ot][:, :], bbk[:, :], ALU.mult, [b_acc[ot], bbb], [b_tmp[ot]])
                                tt("pool", yTc[:, i, ot * 512:(ot + 1) * 512], tmp[ot][:, :], szt[ot][:, :], ALU.mult,
                                   [b_tmp[ot], b_szt[ot]], [b_yTc[i][ot]])
                        prefetch(32 + 4 + 2)
                        wq = [loaded[35 + c] for c in range(4)]
                        sqc = [sb(es, f"sqc{i}", [128, 512], BF16) for i in range(4)]
                        b_sqc = bufs("sqc", 4)
                        lnq2 = sb(es, "lnq2", [128, 512], F32)
                        rsq2 = sb(es, "rsq2", [128, 512], F32)
                        b_lnq2, b_rsq2 = Buf("lnq2"), Buf("rsq2")
                        for ot in range(2):
                            pbs = [bank_ring.next() for _ in range(4)]
                            for c in range(4):
                                proj(wq[c][0], wq[c][1], ot, pbs[c][0], pbs[c][1])
                                act(sqc[c][:, :], pbs[c][0][:, :], AF.Square, [pbs[c][1]], [b_sqc[c]])
                            sbk, sbb = bank_ring.next()
                            for c in range(4):
                                mm(sbk[:, :], ones_bf[:, :], sqc[c][:, :], c == 0, c == 3, [b_const, b_sqc[c]], [sbb])
                            act(lnq2[:, :], sbk[:, :], AF.Ln, [sbb], [b_lnq2], scale=1.0 / 512, bias=cst[:, 0:1])
                            act(rsq2[:, :], lnq2[:, :], AF.Exp, [b_lnq2], [b_rsq2], scale=-0.5)
                            for c in range(4):
                                stt("dve", cqnT[:, c, ot * 512:(ot + 1) * 512], pbs[c][0][:, :], qag[:, c:c + 1], rsq2[:, :],
                                    ALU.mult, ALU.mult, [pbs[c][1], b_par, b_rsq2], [b_cqn[c][ot]])
                        for i in range(8):
                            prefetch(36 + i + 3)
                            wz, wzb = loaded[39 + i]
                            for ot in range(2):
                                bz, bzb = bank_ring.next()
                                proj(wz, wzb, ot, bz, bzb)
                                act(szaT[:, i, ot * 512:(ot + 1) * 512], bz[:, :], AF.Silu, [bzb], [b_sza[i][ot]])
                        S.barrier()
                        ck("2b")

                s_lo = 2 * hf
                NG = s_lo + 2
                with ExitStack() as hs3:
                    yTa = sb(hs3, "yTa", [128, 8, 1024], BF16)
                    b_yTa = bufs(f"yTa{hf}", 8, 2)
                    with ExitStack() as es:
                        cosq = sb(es, "cosq", [64, 1024], F32)
                        sinq = sb(es, "sinq", [64, 1024], F32)
                        b_cosq, b_sinq = bufs("cosq", 2), bufs("sinq", 2)
                        masks = sb(es, "masks", [128, 16, 512], BF16)
                        b_masks = Buf("masks")
                        with ExitStack() as ets:
                            qidx_bc = sb(ets, "qidx_bc", [128, 512], F32)
                            b_qidx = Buf("qidx")
                            tables = make_tables(ets, 512, "Q")
                            for ot in range(2):
                                c0 = hf * 1024 + ot * 512
                                tables(pos_own[0:1, c0:c0 + 512], cosq[:, ot * 512:(ot + 1) * 512], sinq[:, ot * 512:(ot + 1) * 512],
                                       b_cosq[ot], b_sinq[ot])
                            S.dma("sp", qidx_bc[:], qidx[0:1, hf * 1024:hf * 1024 + 512].broadcast_to([128, 512]),
                                  writes=[b_qidx])
                            for kb in range(16):
                                ts("dve", masks[:, kb, :], qidx_bc[:, :], kidx[:, 16 * s_lo + kb:16 * s_lo + kb + 1], None,
                                   ALU.is_ge, None, [b_qidx, b_par], [b_masks])
                            print("phase3 sbuf remaining", nc.sbuf_bytes_remaining)
                            S.barrier()
                            ck("3t")
                        wqb = [sb(es, f"wqb{i}", [128, 4, 256], BF16) for i in range(2)]
                        wkvb = [sb(es, f"wkvb{i}", [128, 2, 256], BF16) for i in range(2)]
                        b_wqb, b_wkvb = bufs("wqb", 2), bufs("wkvb", 2)
                        QTn = [sb(es, f"QTn{i}", [128, 1024], BF16) for i in range(2)]
                        QTr = [sb(es, f"QTr{i}", [64, 1024], BF16) for i in range(2)]
                        b_QT = bufs("QT", 2, 2)
                        KTg = [sb(es, f"KTg{i}", [128, 2048], BF16) for i in range(2)]
                        Vg = [sb(es, f"Vg{i}", [128, 16, 128], BF16) for i in range(2)]
                        b_KTg = bufs("KTg", 2, 4)
                        b_Vg = bufs("Vg", 2, 4)
                        rk = [sb(es, f"rk{i}", [128, 16], F32) for i in range(2)]
                        rkt = [sb(es, f"rkt{i}", [128, 16], F32) for i in range(2)]
                        b_rk, b_rkt = bufs("rk", 2), bufs("rkt", 2)
                        NPB = 6
                        Pb = [sb(es, f"Pb{i}", [128, 512], BF16) for i in range(NPB)]
                        pring = Ring(list(zip(Pb, bufs("Pb", NPB))))
                        sqn = sb(es, "sqn", [128, 512], BF16)
                        sqr = sb(es, "sqr", [64, 512], BF16)
                        b_sqn, b_sqr = Buf("sqn"), Buf("sqr")
                        lnq3 = sb(es, "lnq3", [128, 512], F32)
                        rsq3 = sb(es, "rsq3", [128, 512], F32)
                        b_lnq3, b_rsq3 = Buf("lnq3"), Buf("rsq3")
                        q1 = sb(es, "q1", [64, 512], F32)
                        q2 = sb(es, "q2", [64, 512], F32)
                        b_q1, b_q2 = Buf("q1"), Buf("q2")
                        OT = [(banks[0], bank_bufs[0]), (banks[1], bank_bufs[1])]
                        SM = [(banks[2], bank_bufs[2]), (banks[3], bank_bufs[3])]
                        r3 = Ring([(banks[i], bank_bufs[i]) for i in (4, 5, 6, 7)])
                        accs = [[sb(es, f"accs{a}{b}", [128, 512], F32) for b in range(2)] for a in range(2)]
                        b_accs = bufs("accs", 2, 2)
                        junk3 = sb(es, "junk3", [128, 128], BF16)
                        osb = [sb(es, f"osb{i}", [128, 512], F32) for i in range(2)]
                        ssb = [sb(es, f"ssb{i}", [128, 512], F32) for i in range(2)]
                        b_osb, b_ssb = bufs("osb", 2), bufs("ssb", 2)
                        ATT_BIAS = -0.5 * math.log(192.0)

                        S.dma("pool", wqb[0][:], wqb_l[0], writes=[b_wqb[0]])
                        S.dma("pool", wkvb[0][:], wkvb_l[0], writes=[b_wkvb[0]])

                        def item_Q(h):
                            hp = h % 2
                            for ot in range(2):
                                op("pool", lambda e: e.memset(accs[hp][ot][:, :], 0.0), writes=[b_accs[hp][ot]])
                            if h + 1 < 8:
                                S.dma("pool", wqb[1 - hp][:], wqb_l[h + 1], writes=[b_wqb[1 - hp]])
                                S.dma("pool", wkvb[1 - hp][:], wkvb_l[h + 1], writes=[b_wkvb[1 - hp]])
                            for ot in range(2):
                                cs = slice(ot * 512, (ot + 1) * 512)
                                bn, bnb = r3.next()
                                for kc in range(4):
                                    mm(bn[:, :], wqb[hp][:, kc, 0:128], cqnT[:, kc, cs], kc == 0, kc == 3,
                                       [b_wqb[hp], b_cqn[kc][ot]], [bnb])
                                br, brb = r3.next()
                                for kc in range(4):
                                    mm(br[0:64, :], wqb[hp][:, kc, 128:192], cqnT[:, kc, cs], kc == 0, kc == 3,
                                       [b_wqb[hp], b_cqn[kc][ot]], [brb])
                                yield
                                act(sqn[:, :], bn[:, :], AF.Square, [bnb], [b_sqn])
                                act(sqr[:, :], br[0:64, :], AF.Square, [brb], [b_sqr])
                                bs_, bsb = r3.next()
                                mm(bs_[:, :], ones_bf[:, :], sqn[:, :], True, False, [b_const, b_sqn], [bsb])
                                mm(bs_[:, :], ones_bf[0:64, :], sqr[:, :], False, True, [b_const, b_sqr], [bsb])
                                act(lnq3[:, :], bs_[:, :], AF.Ln, [bsb, b_const], [b_lnq3], scale=1.0 / 192, bias=cst[:, 0:1])
                                act(rsq3[:, :], lnq3[:, :], AF.Exp, [b_lnq3], [b_rsq3], scale=-0.5)
                                stt("dve", QTn[hp][:, cs], bn[:, :], qg[:, 0:1], rsq3[:, :], ALU.mult, ALU.mult,
                                    [bnb, b_par, b_rsq3], [b_QT[hp][ot]])
                                stt("dve", q1[:, :], br[0:64, :], qg[0:64, 1:2], cosq[:, cs], ALU.mult, ALU.mult,
                                    [brb, b_par, b_cosq[ot]], [b_q1])
                                yield
                                bp, bpb = r3.next()
                                for kc in range(4):
                                    mm(bp[0:64, :], wqb[hp][:, kc, 192:256], cqnT[:, kc, cs], kc == 0, kc == 3,
                                       [b_wqb[hp], b_cqn[kc][ot]], [bpb])
                                stt("dve", q2[:, :], bp[0:64, :], qg[0:64, 2:3], sinq[:, cs], ALU.mult, ALU.mult,
                                    [bpb, b_par, b_sinq[ot]], [b_q2])
                                tt("pool", q1[:, :], q1[:, :], q2[:, :], ALU.add, [b_q1, b_q2], [b_q1])
                                tt("pool", QTr[hp][:, cs], q1[:, :], rsq3[0:64, :], ALU.mult, [b_q1, b_rsq3], [b_QT[hp][ot]])
                                yield
                            if DBG and h == 0 and hf == 0:
                                S.dma("sp", dbg_qn, QTn[hp][:], reads=b_QT[hp])
                                S.dma("sp", dbg_qr, QTr[hp][:], reads=b_QT[hp])
                                S.dma("sp", dbg_kr, krT[:, 0:2048], reads=b_kr[0:4])
                                S.dma("sp", dbg_cs[:, 0:1024], cosq[:], reads=b_cosq)
                                S.dma("sp", dbg_cs[:, 1024:2048], sinq[:], reads=b_sinq)

                        def item_G(h, G):
                            hp = h % 2
                            gp = (h * NG + G) % 2
                            cached = G < s_lo
                            if not cached:
                                op("pool", lambda e: e.memset(rkt[gp][:, :], 0.0), writes=[b_rkt[gp]])
                            for t4 in range(4):
                                gt = 4 * G + t4
                                bk, bkb = r3.next()
                                for kc in range(2):
                                    mm(bk[:, :], wkvb[hp][:, kc, 0:128], ckvnT[:, kc, gt * 512:(gt + 1) * 512], kc == 0, kc == 1,
                                       [b_wkvb[hp], b_ckvn[gt]], [bkb])
                                ts("dve", KTg[gp][:, t4 * 512:(t4 + 1) * 512], bk[:, :], kg[:, 0:1], None, ALU.mult, None,
                                   [bkb, b_par], [b_KTg[gp][t4]])
                                yield
                            if cached:
                                for b4 in range(4):
                                    bv, bvb = r3.next()
                                    for q in range(4):
                                        blk = 16 * G + b4 * 4 + q
                                        for kc in range(2):
                                            mm(bv[:, q * 128:(q + 1) * 128], ckvnT[:, kc, blk * 128:(blk + 1) * 128],
                                               wkvb[hp][:, kc, 128:256], kc == 0, kc == 1, [b_wkvb[hp], b_ckvn[blk // 4]], [bvb])
                                    cp("dve", Vg[gp][:, b4 * 4:(b4 + 1) * 4, :], bv[:, :].rearrange("p (a b) -> p a b", a=4),
                                       [bvb], [b_Vg[gp][b4]])
                                    yield
                                return
                            for b2 in range(8):
                                bv, bvb = r3.next()
                                for q in range(2):
                                    blk = 16 * G + b2 * 2 + q
                                    for kc in range(2):
                                        mm(bv[:, q * 256:(q + 1) * 256], ckvnT[:, kc, blk * 128:(blk + 1) * 128],
                                           wkvb[hp][:, kc, 0:256], kc == 0, kc == 1, [b_wkvb[hp], b_ckvn[blk // 4]], [bvb])
                                for q in range(2):
                                    c = b2 * 2 + q
                                    act(junk3[:, :], bv[:, q * 256:q * 256 + 128], AF.Square, [bvb], [b_rkt[gp]],
                                        accum_out=rkt[gp][:, c:c + 1])
                                cp("dve", Vg[gp][:, b2 * 2:(b2 + 1) * 2, :],
                                   bv[:, :].rearrange("p (a b) -> p a b", a=2)[:, :, 128:256], [bvb], [b_Vg[gp][b2 // 2]])
                                yield
                            tt("dve", rkt[gp][:, :], rkt[gp][:, :], ssr[:, G * 16:(G + 1) * 16], ALU.add,
                               [b_rkt[gp]] + [b_ssr[4 * G + i] for i in range(4)], [b_rkt[gp]])
                            act(rkt[gp][:, :], rkt[gp][:, :], AF.Ln, [b_rkt[gp], b_const], [b_rkt[gp]], scale=1.0, bias=cst[:, 1:2])
                            act(rk_all[:, h, G * 16:(G + 1) * 16], rkt[gp][:, :], AF.Exp, [b_rkt[gp]], [b_rkall[h][G]], scale=-0.5)
                            if DBG and h == 0 and hf == 0 and G == 0:
                                S.dma("sp", dbg_k, KTg[gp][:], reads=b_KTg[gp])
                                S.dma("sp", dbg_v, Vg[gp][:], reads=b_Vg[gp])
                                S.dma("sp", dbg_rk, rk_all[:, h, 0:16], reads=[b_rkall[h][G]])

                        def item_A(h, G, filler=None):
                            hp = h % 2
                            gp = (h * NG + G) % 2
                            steps = []
                            for ot in range(2):
                                s = s_lo + ot
                                if s >= G:
                                    for kb in range(16):
                                        steps.append((ot, s, kb))
                            LAG = 2
                            pend = []
                            for i in range(len(steps) + LAG):
                                if i < len(steps):
                                    ot, s, kb = steps[i]
                                    cs = slice(ot * 512, (ot + 1) * 512)
                                    gblk = 16 * G + kb
                                    bs_, bsb = r3.next()
                                    mm(bs_[:, :], KTg[gp][:, kb * 128:(kb + 1) * 128], QTn[hp][:, cs], True, False,
                                       [b_KTg[gp][kb // 4], b_QT[hp][ot]], [bsb])
                                    mm(bs_[:, :], krT[0:64, gblk * 128:(gblk + 1) * 128], QTr[hp][:, cs], False, True,
                                       [b_kr[gblk // 4], b_QT[hp][ot]], [bsb])
                                    Pt, Ptb = pring.next()
                                    act(Pt[:, :], bs_[:, :], AF.Exp, [bsb, b_rkall[h][G]], [Ptb], scale=rk_all[:, h, gblk:gblk + 1])
                                    if G == s:
                                        tt("dve", Pt[:, :], Pt[:, :], masks[:, kb, :], ALU.mult, [b_masks, Ptb], [Ptb])
                                    pend.append((ot, s, kb, Pt, Ptb))
                                if i >= LAG:
                                    ot, s, kb, Pt, Ptb = pend[i - LAG]
                                    first = (G == 0 and kb == 0)
                                    last = (G == s and kb == 15)
                                    mm(OT[ot][0][:, :], Vg[gp][:, kb, :], Pt[:, :], first, last, [b_Vg[gp][kb // 4], Ptb],
                                       [OT[ot][1]])
                                    if G == s:
                                        mm(SM[ot][0][:, :], ones_bf[:, :], Pt[:, :], kb == 0, kb == 15 and s == 0,
                                           [b_const, Ptb], [SM[ot][1]])
                                        if kb == 15 and s > 0:
                                            mm(SM[ot][0][:, :], ones_f[:, :], accs[hp][ot][:, :], False, True,
                                               [b_const, b_accs[hp][ot]], [SM[ot][1]])
                                    else:
                                        stt("dve", accs[hp][ot][:, :], Pt[:, :], 1.0, accs[hp][ot][:, :], ALU.mult, ALU.add,
                                            [b_accs[hp][ot], Ptb], [b_accs[hp][ot]])
                                if filler is not None:
                                    next(filler, None)
                            if filler is not None:
                                for _ in filler:
                                    pass

                        def item_F(h):
                            hp = h % 2
                            for ot in range(2):
                                cp("act", ssb[ot][:, :], SM[ot][0][:, :], [SM[ot][1]], [b_ssb[ot]])
                                cp("dve", osb[ot][:, :], OT[ot][0][:, :], [OT[ot][1]], [b_osb[ot]])
                            for ot in range(2):
                                cs = slice(ot * 512, (ot + 1) * 512)
                                op("dve", lambda e: e.reciprocal(out=ssb[ot][:, :], in_=ssb[ot][:, :]), [b_ssb[ot]], [b_ssb[ot]])
                                tt("pool", osb[ot][:, :], osb[ot][:, :], ssb[ot][:, :], ALU.mult, [b_osb[ot], b_ssb[ot]], [b_osb[ot]])
                                tt("pool", yTa[:, h, cs], osb[ot][:, :], szaT[:, h, cs], ALU.mult, [b_osb[ot], b_sza[h][ot]],
                                   [b_yTa[h][ot]])

                        import itertools
                        for _ in item_Q(0):
                            pass
                        for _ in item_G(0, 0):
                            pass
                        for h in range(8):
                            for G in range(NG):
                                if G + 1 < NG:
                                    filler = item_G(h, G + 1)
                                elif h + 1 < 8:
                                    filler = itertools.chain(item_Q(h + 1), item_G(h + 1, 0))
                                else:
                                    filler = None
                                item_A(h, G, filler)
                            item_F(h)

                        S.barrier()
                        ck("3")

                    if DBG:
                        S.dma("sp", dbg_yc[hf], yTc[:], reads=[b for l in b_yTc for b in l])
                        S.dma("sp", dbg_ya[hf], yTa[:], reads=[b for l in b_yTa for b in l])
                    with ExitStack() as es:
                        wo = [sb(es, f"wo{i}", [128, KC, 512], BF16) for i in range(2)]
                        b_wo = bufs("wo", 2)
                        NX = 4
                        xres = [sb(es, f"xres{i}", [128, 512], F32) for i in range(NX)]
                        b_xres = bufs("xres", NX)
                        o1 = [sb(es, f"o1{i}", [128, 512], F32) for i in range(2)]
                        b_o1 = bufs("o1", 2)
                        o2 = [sb(es, f"o2{i}", [128, 512], F32) for i in range(3)]
                        b_o2 = bufs("o2", 3)
                        items = [(ct, tb) for ct in range(4) for tb in range(8)]

                        def load_x(n):
                            ct, tb = items[n]
                            r0 = hf * 1024 + tb * 128
                            S.dma("sp", xres[n % NX][:], x_own[r0:r0 + 128, ct * 512:(ct + 1) * 512], writes=[b_xres[n % NX]])

                        S.dma("pool", wo[0][:], w_out_l[0], writes=[b_wo[0]])
                        load_x(0)
                        load_x(1)
                        for n, (ct, tb) in enumerate(items):
                            w, wb = wo[ct % 2], b_wo[ct % 2]
                            if tb == 0 and ct + 1 < 4:
                                S.dma("pool", wo[(ct + 1) % 2][:], w_out_l[ct + 1], writes=[b_wo[(ct + 1) % 2]])
                            if n + 2 < len(items):
                                load_x(n + 2)
                            r0 = hf * 1024 + tb * 128
                            ot = tb // 4
                            ts_ = slice(tb * 128, (tb + 1) * 128)
                            xi, oi, pi = n % NX, n % 3, n % 2
                            bo, bob = bank_ring.next()
                            for kc in range(KC):
                                if kc < 8:
                                    l, lb = yTc[:, kc, ts_], b_yTc[kc][ot]
                                else:
                                    l, lb = yTa[:, kc - 8, ts_], b_yTa[kc - 8][ot]
                                mm(bo[:, :], l, w[:, kc, :], kc == 0, kc == KC - 1, [lb, wb], [bob])
                            tt("dve", o1[pi][:, :], bo[:, :], gate_bc[:, ct * 512:(ct + 1) * 512], ALU.mult, [bob, b_gate],
                               [b_o1[pi]])
                            tt("pool", o2[oi][:, :], o1[pi][:, :], xres[xi][:, :], ALU.add, [b_o1[pi], b_xres[xi]],
                               [b_o2[oi]])
                            S.dma("sp", out_d[r0:r0 + 128, ct * 512:(ct + 1) * 512], o2[oi][:, :], reads=[b_o2[oi]])
                        S.barrier()
                        ck("4")

    except _Stop:
        pass
    S.off = False
    S.barrier()
    top.close()
    return nc


def _prep_inputs(x, c, positions, ada_w, ada_b, norm_g, w_in, conv_w, q_a_g, w_q_b, kv_a_g, w_kv_b, q_g, k_g, w_out):
    f = np.float32
    x = np.asarray(x, f)
    B, SEQ, _ = x.shape
    NT = SEQ // 512
    NSLOT = NT // 4
    NBLK = SEQ // 128
    positions = np.asarray(positions, np.int32)
    ada_w = np.asarray(ada_w, f)[0]
    w_in = np.asarray(w_in, f)[0]
    w_q_b = np.asarray(w_q_b, f)[0]
    w_kv_b = np.asarray(w_kv_b, f)[0]
    w_out = np.asarray(w_out, f)[0]

    def cols(v, n):
        return np.ascontiguousarray(np.asarray(v, f).reshape(n, 128).T)

    def wl(w):
        n = w.shape[1] // 128
        return w.reshape(KC, 128, n, 128).transpose(2, 1, 0, 3)

    perm = (np.arange(64) + 32) % 64
    wr = w_in[:, 4864:4928]
    chunks = [wl(w_in[:, 4608:4864]), wl(np.concatenate([wr, wr[:, perm]], axis=1))]
    conv = np.stack([w_in[:, 0:1024], w_in[:, 2048:3072], w_in[:, 1024:2048], w_in[:, 3072:4096]], 0)
    conv = conv.reshape(4, D, 8, 128).transpose(2, 0, 1, 3).reshape(32, D, 128)
    chunks.append(conv.reshape(32, KC, 128, 128).transpose(0, 2, 1, 3))
    chunks.append(wl(w_in[:, 4096:4608]))
    chunks.append(wl(w_in[:, 4928:5952]))
    w_in_l = np.ascontiguousarray(np.concatenate(chunks, 0))
    assert w_in_l.shape == (47, 128, KC, 128)
    ada_w_l = np.ascontiguousarray(ada_w.reshape(KC, 128, 12, 512).transpose(2, 1, 0, 3))
    w_out_l = np.ascontiguousarray(w_out.reshape(KC, 128, 4, 512).transpose(2, 1, 0, 3))
    wq = w_q_b.reshape(4, 128, 8, 192).transpose(2, 1, 0, 3)
    wqb_l = np.ascontiguousarray(np.concatenate([wq, wq[..., 128:192][..., perm]], -1))
    wkvb_l = np.ascontiguousarray(w_kv_b.reshape(2, 128, 8, 256).transpose(2, 1, 0, 3))

    def g3(g):
        g = np.asarray(g, f)[0]
        o = np.zeros((128, 3), f)
        o[:, 0] = g[0:128]
        o[0:64, 1] = g[128:192]
        o[0:64, 2] = g[128:192][perm]
        return o

    inv_freq = (10000.0 ** (-(np.arange(0, 64, 2, dtype=np.float64)) / 64.0)).astype(f)
    rope_c = np.zeros((64, 4), f)
    rope_c[:, 0] = np.concatenate([inv_freq, inv_freq])
    rope_c[:, 1] = np.concatenate([-np.ones(32), np.ones(32)]) * SHRINK
    rope_c[:, 2] = SHRINK
    kidx = (np.arange(NBLK)[None, :] * 128 + np.arange(128)[:, None]).astype(f)
    shared = dict(
        ada_w_l=ada_w_l, ada_b=np.asarray(ada_b, f).reshape(1, 6144), ng_col=cols(np.asarray(norm_g)[0], KC),
        w_in_l=w_in_l, convw_col=np.ascontiguousarray(np.asarray(conv_w, f)[0].reshape(3, 8, 128).transpose(2, 1, 0)),
        qag_col=cols(np.asarray(q_a_g)[0], 4), kvag_col=cols(np.asarray(kv_a_g)[0], 2), wqb_l=wqb_l, wkvb_l=wkvb_l,
        qg_col=g3(q_g), kg_col=g3(k_g), w_out_l=w_out_l, rope_c=rope_c, kidx=kidx)
    in_maps = []
    meta = []
    for core in range(NCORES):
        b, j = core // 4, core % 4
        tiles = [4 * s + j for s in range(NSLOT)]
        rows = np.concatenate([np.arange(T * 512, (T + 1) * 512) for T in tiles])
        halo = np.zeros((NSLOT * 2, D), f)
        hv = np.zeros((128, NSLOT * 2), f)
        for s, T in enumerate(tiles):
            if T > 0:
                halo[2 * s:2 * s + 2] = x[b, T * 512 - 2:T * 512]
                hv[:, 2 * s:2 * s + 2] = 1.0
        m = dict(shared)
        m.update(x_all=x[b], x_own=np.ascontiguousarray(x[b, rows]), x_halo=halo, halo_valid=hv,
                 pos_all=np.ascontiguousarray(positions[b][None, :]), pos_own=np.ascontiguousarray(positions[b, rows][None, :]),
                 qidx=rows.astype(f)[None, :], c_col=cols(np.asarray(c, f)[b], KC))
        in_maps.append(m)
        meta.append((b, rows))
    return in_maps, meta, (B, SEQ)


_NC_CACHE = {}


def kernel(**inputs):
    in_maps, meta, (B, SEQ) = _prep_inputs(**inputs)
    if SEQ not in _NC_CACHE:
        _NC_CACHE[SEQ] = build(SEQ)
    nc = _NC_CACHE[SEQ]
    res = run_bass_kernel_spmd(nc, in_maps, core_ids=list(range(NCORES)))
    out = np.empty((B, SEQ, D), np.float32)
    for core, (b, rows) in enumerate(meta):
        out[b, rows] = res.results[core]["out"]
    if DBG_HOOK is not None:
        DBG_HOOK(res)
    return out
```

```python
import math
from contextlib import ExitStack

import numpy as np
import concourse.bass as bass
import concourse.mybir as mybir
from concourse.bass_utils import run_bass_kernel_spmd

F32 = mybir.dt.float32
BF16 = mybir.dt.bfloat16
I32 = mybir.dt.int32
AF = mybir.ActivationFunctionType
ALU = mybir.AluOpType

D = 2048
KC = D // 128
NCORES = 8
EPS = 1e-6
TWO_PI = 2.0 * math.pi
CW1 = 6.28125
CW2 = TWO_PI - 6.28125
SHRINK = 1.0 - 2e-6
PI_SAFE = 3.141592
KSTOP = ""
DBG_HOOK = None


class Buf:
    __slots__ = ("name", "lw", "reads", "sem", "dcnt", "excl")

    def __init__(self, name, excl=False):
        self.name = name
        self.excl = excl
        self.lw = None
        self.reads = {}
        self.sem = None
        self.dcnt = 0


def bufs(name, *dims):
    if not dims:
        return Buf(name)
    return [bufs(f"{name}_{i}", *dims[1:]) for i in range(dims[0])]


class Sched:
    def __init__(self, nc):
        self.nc = nc
        self.eng = {"pe": nc.tensor, "act": nc.scalar, "dve": nc.vector, "pool": nc.gpsimd, "sp": nc.sync}
        self.sem = {k: nc.alloc_semaphore(name="es_" + k) for k in self.eng}
        self.cnt = {k: 0 for k in self.eng}
        self.seen = {k: {} for k in self.eng}
        self.dma_bufs = []
        self.free_sems = []
        self.off = False

    def _wait(self, e, deps):
        need = {}
        for d in deps:
            if d is None:
                continue
            k, v = d
            if k == e and (e == "pe" or v <= self.cnt[e] - 4):
                continue
            if need.get(k, 0) < v:
                need[k] = v
        seen = self.seen[e]
        for k, v in need.items():
            if isinstance(k, Buf):
                v = k.dcnt
                so = k.sem
            else:
                so = self.sem[k]
            if seen.get(k, 0) >= v:
                continue
            seen[k] = v
            self.eng[e].wait_ge(so, v)

    @staticmethod
    def _deps(reads, writes):
        deps = []
        for b in reads:
            deps.append(b.lw)
        for b in writes:
            deps.append(b.lw)
            deps.extend(b.reads.items())
        return deps

    def op(self, e, fn, reads=(), writes=()):
        if self.off:
            return None
        if any(b.excl for b in reads):
            writes = list(writes) + [b for b in reads if b.excl]
            reads = [b for b in reads if not b.excl]
        self._wait(e, self._deps(reads, writes))
        ins = fn(self.eng[e])
        self.cnt[e] += 1
        ins.then_inc(self.sem[e], 1)
        v = self.cnt[e]
        for b in reads:
            if b.reads.get(e, 0) < v:
                b.reads[e] = v
        for b in writes:
            b.lw = (e, v)
            b.reads = {}
        return ins

    def dma(self, q, out, in_, reads=(), writes=(), sembuf=None):
        if self.off:
            return None
        self._wait(q, self._deps(reads, writes))
        if sembuf is None:
            sembuf = writes[0] if writes else reads[0]
        if sembuf.sem is None:
            sembuf.sem = self.nc.alloc_semaphore(name=f"ds{len(self.dma_bufs)}_" + sembuf.name)
            self.dma_bufs.append(sembuf)
        ins = self.eng[q].dma_start(out=out, in_=in_)
        sembuf.dcnt += 16
        ins.then_inc(sembuf.sem, 16)
        for b in reads:
            b.reads[sembuf] = sembuf.dcnt
        for b in writes:
            b.lw = (sembuf, sembuf.dcnt)
            b.reads = {}
        return ins

    def barrier(self):
        if self.off:
            return
        deps = [(k, c) for k, c in self.cnt.items() if k != "sp" and c > 0]
        deps += [(b, b.dcnt) for b in self.dma_bufs]
        self._wait("sp", deps)
        self.eng["sp"].sem_inc(self.sem["sp"], 1)
        self.cnt["sp"] += 1
        for e in self.eng:
            if e != "sp":
                self._wait(e, [("sp", self.cnt["sp"])])
                for k, v in self.seen["sp"].items():
                    if self.seen[e].get(k, 0) < v:
                        self.seen[e][k] = v


class _Stop(Exception):
    pass


class Ring:
    def __init__(self, items):
        self.items = list(items)
        self.i = 0

    def next(self):
        it = self.items[self.i % len(self.items)]
        self.i += 1
        return it


def build(SEQ):
    NT = SEQ // 512
    NSLOT = NT // 4
    NHALF = NSLOT // 2
    NBLK = SEQ // 128
    NOWN = NSLOT * 512
    NCC = 47

    nc = bass.Bass("TRN2", target_bir_lowering=False)

    def din(name, shape, dt=F32):
        return nc.dram_tensor(name, list(shape), dt, kind="ExternalInput").ap()

    x_all = din("x_all", [SEQ, D])
    x_own = din("x_own", [NOWN, D])
    x_halo = din("x_halo", [NSLOT * 2, D])
    halo_valid = din("halo_valid", [128, NSLOT * 2])
    pos_all = din("pos_all", [1, SEQ], I32)
    pos_own = din("pos_own", [1, NOWN], I32)
    qidx = din("qidx", [1, NOWN])
    kidx_d = din("kidx", [128, NBLK])
    c_col_d = din("c_col", [128, KC])
    ada_w_l = din("ada_w_l", [12, 128, KC, 512])
    ada_b_d = din("ada_b", [1, 6144])
    ng_col_d = din("ng_col", [128, KC])
    w_in_l = din("w_in_l", [NCC, 128, KC, 128])
    convw_d = din("convw_col", [128, 8, 3])
    qag_d = din("qag_col", [128, 4])
    kvag_d = din("kvag_col", [128, 2])
    wqb_l = din("wqb_l", [8, 128, 4, 256])
    wkvb_l = din("wkvb_l", [8, 128, 2, 256])
    qg_d = din("qg_col", [128, 3])
    kg_d = din("kg_col", [128, 3])
    w_out_l = din("w_out_l", [4, 128, KC, 512])
    rope_c_d = din("rope_c", [64, 4])
    out_d = nc.dram_tensor("out", [NOWN, D], F32, kind="ExternalOutput").ap()
    DBG = DBG_HOOK is not None
    if DBG:
        dbg_yc = nc.dram_tensor("dbg_yc", [NHALF, 128, 8, 1024], BF16, kind="ExternalOutput").ap()
        dbg_ya = nc.dram_tensor("dbg_ya", [NHALF, 128, 8, 1024], BF16, kind="ExternalOutput").ap()
        dbg_qn = nc.dram_tensor("dbg_qn", [128, 1024], BF16, kind="ExternalOutput").ap()
        dbg_qr = nc.dram_tensor("dbg_qr", [64, 1024], BF16, kind="ExternalOutput").ap()
        dbg_k = nc.dram_tensor("dbg_k", [128, 2048], BF16, kind="ExternalOutput").ap()
        dbg_kr = nc.dram_tensor("dbg_kr", [64, 2048], BF16, kind="ExternalOutput").ap()
        dbg_v = nc.dram_tensor("dbg_v", [128, 16, 128], BF16, kind="ExternalOutput").ap()
        dbg_rk = nc.dram_tensor("dbg_rk", [128, 16], F32, kind="ExternalOutput").ap()
        dbg_acc = nc.dram_tensor("dbg_acc", [128, 512], F32, kind="ExternalOutput").ap()
        dbg_ssb = nc.dram_tensor("dbg_ssb", [128, 512], F32, kind="ExternalOutput").ap()
        dbg_ssb2 = nc.dram_tensor("dbg_ssb2", [128, 512], F32, kind="ExternalOutput").ap()
        dbg_cs = nc.dram_tensor("dbg_cs", [64, 2048], F32, kind="ExternalOutput").ap()

    S = Sched(nc)
    op = S.op

    def mm(out, lhsT, rhs, start, stop, reads, writes):
        return op("pe", lambda e: e.matmul(out, lhsT=lhsT, rhs=rhs, start=start, stop=stop), reads, writes)

    def act(out, in_, func, reads, writes, scale=1.0, bias=0.0, accum_out=None):
        if accum_out is not None:
            return op("act", lambda e: e.activation(out=out, in_=in_, func=func, bias=bias, scale=scale,
                                                    accum_out=accum_out), reads, writes)
        return op("act", lambda e: e.activation(out=out, in_=in_, func=func, bias=bias, scale=scale), reads, writes)

    def ts(eng, out, in0, s1, s2, op0, op1, reads, writes):
        if s2 is None:
            return op(eng, lambda e: e.tensor_scalar(out=out, in0=in0, scalar1=s1, scalar2=None, op0=op0), reads, writes)
        return op(eng, lambda e: e.tensor_scalar(out=out, in0=in0, scalar1=s1, scalar2=s2, op0=op0, op1=op1), reads, writes)

    def stt(eng, out, in0, scalar, in1, op0, op1, reads, writes):
        return op(eng, lambda e: e.scalar_tensor_tensor(out=out, in0=in0, scalar=scalar, in1=in1, op0=op0, op1=op1),
                  reads, writes)

    def tt(eng, out, in0, in1, o, reads, writes):
        return op(eng, lambda e: e.tensor_tensor(out=out, in0=in0, in1=in1, op=o), reads, writes)

    def cp(eng, out, in_, reads, writes):
        if eng == "act":
            return op(eng, lambda e: e.activation(out=out, in_=in_, func=AF.Copy), reads, writes)
        return op(eng, lambda e: e.tensor_copy(out=out, in_=in_), reads, writes)

    top = ExitStack()
    kstop = KSTOP

    def ck(name):
        if kstop == name:
            S.off = True

    uid = [0]

    def sb(es, name, shape, dt):
        uid[0] += 1
        return es.enter_context(nc.sbuf_tensor(f"{name}_u{uid[0]}", list(shape), dt))

    banks = [top.enter_context(nc.psum_tensor(f"pb{i}", [128, 512], F32)) for i in range(8)]
    bank_bufs = [Buf(f"pb{i}", excl=True) for i in range(8)]
    tp_tiles = [banks[6][:, :].bitcast(BF16), banks[7][:, :].bitcast(BF16)]
    tp_bufs = [bank_bufs[6], bank_bufs[7]]

    ckvnT = sb(top, "ckvnT", [128, 2, SEQ], BF16)
    krT = sb(top, "krT", [64, SEQ], BF16)
    ssr = sb(top, "ssr", [128, NBLK], F32)
    b_ckvn = bufs("ckvn", NT)
    b_kr = bufs("kr", NT)
    b_ssr = bufs("ssr", NT)
    ident = sb(top, "ident", [128, 128], BF16)
    ones_bf = sb(top, "ones_bf", [128, 128], BF16)
    ones_f = sb(top, "ones_f", [128, 128], F32)
    cst = sb(top, "cst", [128, 2], F32)
    rk_all = sb(top, "rk_all", [128, 8, NBLK], F32)
    b_rkall = bufs("rkall", 8, NBLK // 16)
    b_const = Buf("const")
    gs_col = sb(top, "gs_col", [128, KC], F32)
    shift_col = sb(top, "shift_col", [128, KC], F32)
    gate_bc = sb(top, "gate_bc", [128, D], F32)
    b_mod = Buf("mod")
    b_gate = Buf("gate")
    c_col = sb(top, "c_col", [128, KC], F32)
    ng_col = sb(top, "ng_col", [128, KC], F32)
    convw = sb(top, "convw", [128, 8, 3], F32)
    qag = sb(top, "qag", [128, 4], F32)
    kvag = sb(top, "kvag", [128, 2], F32)
    qg = sb(top, "qg", [128, 3], F32)
    kg = sb(top, "kg", [128, 3], F32)
    rope_c = sb(top, "rope_c", [64, 4], F32)
    kidx = sb(top, "kidx", [128, NBLK], F32)
    hvalid = sb(top, "hvalid", [128, NSLOT * 2], F32)
    b_par = Buf("par")

    for dst, src in ((c_col, c_col_d), (ng_col, ng_col_d), (convw, convw_d), (qag, qag_d), (kvag, kvag_d),
                     (qg, qg_d), (kg, kg_d), (rope_c, rope_c_d), (kidx, kidx_d), (hvalid, halo_valid)):
        S.dma("sp", dst[:], src, writes=[b_par])

    try:
        def make_tables(es, N, tag):
            posi = sb(es, f"posi{tag}", [64, N], I32)
            posf = sb(es, f"posf{tag}", [64, N], F32)
            ang = sb(es, f"ang{tag}", [64, N], F32)
            kf = [sb(es, f"kf{tag}{i}", [64, N], F32) for i in range(2)]
            ki = [sb(es, f"ki{tag}{i}", [64, N], I32) for i in range(2)]
            rr = [sb(es, f"rr{tag}{i}", [64, N], F32) for i in range(2)]
            b_posi, b_posf, b_ang = Buf("posi" + tag), Buf("posf" + tag), Buf("ang" + tag)
            b_kf, b_ki, b_rr = bufs("kf" + tag, 2), bufs("ki" + tag, 2), bufs("rr" + tag, 2)

            def tables(pos_src, cos2, sinS, b_cos, b_sin):
                S.dma("sp", posi[:], pos_src.broadcast_to([64, N]), writes=[b_posi])
                cp("dve", posf[:, :], posi[:, :], [b_posi], [b_posf])
                ts("dve", ang[:, :], posf[:, :], rope_c[:, 0:1], None, ALU.mult, None, [b_posf, b_par], [b_ang])
                for i, (eng, phase, outt, bo, ccol) in enumerate((("dve", 0.0, sinS, b_sin, 1), ("dve", 0.25, cos2, b_cos, 2))):
                    ts(eng, kf[i][:, :], ang[:, :], 1.0 / TWO_PI, phase, ALU.mult, ALU.add, [b_ang], [b_kf[i]])
                    cp(eng, ki[i][:, :], kf[i][:, :], [b_kf[i]], [b_ki[i]])
                    cp(eng, kf[i][:, :], ki[i][:, :], [b_ki[i]], [b_kf[i]])
                    stt(eng, rr[i][:, :], kf[i][:, :], -CW1, ang[:, :], ALU.mult, ALU.add, [b_kf[i], b_ang], [b_rr[i]])
                    stt(eng, rr[i][:, :], kf[i][:, :], -CW2, rr[i][:, :], ALU.mult, ALU.add, [b_kf[i], b_rr[i]], [b_rr[i]])
                    if phase != 0.0:
                        ts(eng, rr[i][:, :], rr[i][:, :], phase * TWO_PI, None, ALU.add, None, [b_rr[i]], [b_rr[i]])
                    ts(eng, rr[i][:, :], rr[i][:, :], -PI_SAFE, PI_SAFE, ALU.max, ALU.min, [b_rr[i]], [b_rr[i]])
                    act(outt, rr[i][:, :], AF.Sin, [b_rr[i], b_par], [bo], scale=rope_c[:, ccol:ccol + 1])

            return tables

        lscope = ExitStack()
        cosK = sb(lscope, "cosK", [64, SEQ], BF16)
        sinK = sb(lscope, "sinK", [64, SEQ], BF16)
        b_cosK, b_sinK = bufs("cosK", NT), bufs("sinK", NT)

        with ExitStack() as es:
            identf = sb(es, "identf", [128, 128], F32)
            b_idf = Buf("identf")
            op("pool", lambda e: e.memset(identf[:, :], 0.0), writes=[b_idf])
            op("pool", lambda e: e.affine_select(out=identf[:, :], in_=identf[:, :], compare_op=ALU.not_equal, fill=1.0,
                                                 base=0, pattern=[[-1, 128]], channel_multiplier=1), writes=[b_idf])
            cp("dve", ident[:, :], identf[:, :], [b_idf], [b_const])
            op("pool", lambda e: e.memset(ones_f[:, :], 1.0), writes=[b_const])
            op("pool", lambda e: e.memset(cst[:, 0:1], EPS), writes=[b_const])
            op("pool", lambda e: e.memset(cst[:, 1:2], 192.0 * EPS), writes=[b_const])
            cp("dve", ones_bf[:, :], ones_f[:, :], [b_const], [b_const])

            tablesK = make_tables(es, 512, "K")
            tdone = [0]

            def tables_upto(n):
                while tdone[0] < min(n, NT):
                    t = tdone[0]
                    tablesK(pos_all[0:1, t * 512:(t + 1) * 512], cosK[:, t * 512:(t + 1) * 512],
                            sinK[:, t * 512:(t + 1) * 512], b_cosK[t], b_sinK[t])
                    tdone[0] += 1

            sc_bf = sb(es, "sc_bf", [128, KC], BF16)
            b_sc = Buf("sc")
            act(sc_bf[:, :], c_col[:, :], AF.Silu, [b_par], [b_sc])
            modrow = sb(es, "modrow", [1, 6144], F32)
            adab = sb(es, "adab", [1, 6144], F32)
            b_adab = Buf("adab")
            S.dma("sp", adab[:], ada_b_d, writes=[b_adab])
            b_modrow = Buf("modrow")
            adaw = [sb(es, f"adaw{i}", [128, KC, 512], BF16) for i in range(2)]
            b_adaw = bufs("adaw", 2, 4)
            for ct in range(12):
                wt, wb = adaw[ct % 2], b_adaw[ct % 2]
                for qd in range(4):
                    S.dma("pool", wt[:, qd * 4:(qd + 1) * 4, :], ada_w_l[ct][:, qd * 4:(qd + 1) * 4, :], writes=[wb[qd]])
                bk, bb = banks[ct % 2], bank_bufs[ct % 2]
                for kc in range(KC):
                    mm(bk[0:1, :], sc_bf[:, kc:kc + 1], wt[:, kc, :], kc == 0, kc == KC - 1, [b_sc, wb[kc // 4]], [bb])
                tt("dve", modrow[0:1, ct * 512:(ct + 1) * 512], bk[0:1, :], adab[0:1, ct * 512:(ct + 1) * 512], ALU.add,
                   [bb, b_adab], [b_modrow])
                tables_upto((ct + 1) * NT // 12 + 1)
            tables_upto(NT)
            cb, cbb = banks[2], bank_bufs[2]
            for c in range(32):
                mm(cb[:, c:c + 1], modrow[0:1, c * 128:(c + 1) * 128], ones_f[0:1, 0:1], True, True, [b_modrow, b_const], [cbb])
            cp("dve", shift_col[:, :], cb[:, 0:KC], [cbb], [b_mod])
            stt("dve", gs_col[:, :], cb[:, KC:2 * KC], 1.0, ng_col[:, :], ALU.add, ALU.mult, [cbb, b_par], [b_mod])
            for ct in range(4):
                bk, bb = banks[3 + ct % 2], bank_bufs[3 + ct % 2]
                mm(bk[:, :], ones_f[0:1, :], modrow[0:1, 4096 + ct * 512:4096 + (ct + 1) * 512], True, True,
                   [b_modrow, b_const], [bb])
                cp("act", gate_bc[:, ct * 512:(ct + 1) * 512], bk[:, :], [bb], [b_gate])
            S.barrier()
            ck("prologue")

        bank_ring = Ring(list(zip(banks[:6], bank_bufs[:6])))
        ring8 = Ring(list(zip(banks, bank_bufs)))

        def make_hT_builder(es, nxt=3):
            xt = [sb(es, f"xt{i}", [128, D], F32) for i in range(nxt)]
            b_xt = bufs("xt", nxt)
            xn = [sb(es, f"xn{i}", [128, D], BF16) for i in range(8)]
            b_xn = bufs("xn", 8)
            junk = sb(es, "junk", [128, D], BF16)
            ssq = [sb(es, f"ssq{i}", [128, 4], F32) for i in range(2)]
            lnq = [sb(es, f"lnq{i}", [128, 4], F32) for i in range(2)]
            rsq = [sb(es, f"rsq{i}", [128, 4], F32) for i in range(2)]
            b_ssq = bufs("ssq", 2)
            b_lnq = bufs("lnq", 2)
            b_rsq = bufs("rsq", 2)
            state = {"n": 0, "xi": 0}

            def prep(src, row0):
                p2 = state["n"] % 2
                state["n"] += 1
                op("pool", lambda e: e.memset(ssq[p2][:, :], 0.0), writes=[b_ssq[p2]])
                xis = []
                for st in range(4):
                    xi = state["xi"] % nxt
                    state["xi"] += 1
                    xis.append(xi)
                    S.dma("sp", xt[xi][:], src[row0 + st * 128:row0 + (st + 1) * 128, :], writes=[b_xt[xi]])
                    act(junk[:, :], xt[xi][:, :], AF.Square, [b_xt[xi]], [b_ssq[p2]], accum_out=ssq[p2][:, st:st + 1])
                    if nxt < 4 or st == 3:
                        pass
                act(lnq[p2][:, :], ssq[p2][:, :], AF.Ln, [b_ssq[p2]], [b_lnq[p2]], scale=1.0 / D, bias=cst[:, 0:1])
                act(rsq[p2][:, :], lnq[p2][:, :], AF.Exp, [b_lnq[p2]], [b_rsq[p2]], scale=-0.5)
                return p2, xis

            def prep_full(src, row0):
                p2 = state["n"] % 2
                state["n"] += 1
                for st in range(4):
                    xi = state["xi"] % nxt
                    state["xi"] += 1
                    ni = p2 * 4 + st
                    S.dma("sp", xt[xi][:], src[row0 + st * 128:row0 + (st + 1) * 128, :], writes=[b_xt[xi]])
                    op("pool", lambda e: e.memset(ssq[p2][:, st:st + 1], 0.0), writes=[b_ssq[p2]])
                    act(junk[:, :], xt[xi][:, :], AF.Square, [b_xt[xi]], [b_ssq[p2]], accum_out=ssq[p2][:, st:st + 1])
                    act(lnq[p2][:, st:st + 1], ssq[p2][:, st:st + 1], AF.Ln, [b_ssq[p2]], [b_lnq[p2]], scale=1.0 / D, bias=cst[:, 0:1])
                    act(rsq[p2][:, st:st + 1], lnq[p2][:, st:st + 1], AF.Exp, [b_lnq[p2]], [b_rsq[p2]], scale=-0.5)
                    ts("dve", xn[ni][:, :], xt[xi][:, :], rsq[p2][:, st:st + 1], None, ALU.mult,
                       None, [b_xt[xi], b_rsq[p2]], [b_xn[ni]])
                return p2

            def finish(p2, hT, b_hT, col0):
                for kc in range(KC):
                    h = kc % 2
                    tph = tp_tiles[h][:, 0:512]
                    for st in range(4):
                        ni = p2 * 4 + st
                        op("pe", lambda e: e.transpose(out=tph[:, st * 128:(st + 1) * 128],
                                                       in_=xn[ni][:, kc * 128:(kc + 1) * 128], identity=ident[:, :]),
                           [b_xn[ni], b_const], [tp_bufs[h]])
                    dst = hT[:, kc, col0:col0 + 512]
                    if kc % 2 == 0:
                        act(dst, tph, AF.Identity, [tp_bufs[h], b_mod], [b_hT[kc]], scale=gs_col[:, kc:kc + 1],
                            bias=shift_col[:, kc:kc + 1])
                    else:
                        ts("dve", dst, tph, gs_col[:, kc:kc + 1], shift_col[:, kc:kc + 1], ALU.mult, ALU.add,
                           [tp_bufs[h], b_mod], [b_hT[kc]])

            return prep_full, finish, junk

        with ExitStack() as es:
            prep, finish, _ = make_hT_builder(es)
            hTt = [sb(es, f"hTt{i}", [128, KC, 512], BF16) for i in range(2)]
            b_hTt = bufs("hTt", 2, KC)
            wkv = sb(es, "wkv", [128, 3, KC, 128], BF16)
            b_wkv = Buf("wkv")
            for i in range(3):
                S.dma("pool", wkv[:, i, :, :], w_in_l[i], writes=[b_wkv])
            sq = [sb(es, f"sqL{i}", [128, 512], BF16) for i in range(3)]
            b_sq = bufs("sqL", 3)
            lnc = sb(es, "lnc", [128, 512], F32)
            rstdc = sb(es, "rstdc", [128, 512], F32)
            b_lnc, b_rstdc = Buf("lnc"), Buf("rstdc")
            t1 = sb(es, "t1L", [64, 512], F32)
            t2 = sb(es, "t2L", [64, 512], F32)
            b_t1, b_t2 = Buf("t1L"), Buf("t2L")

            nxt_set = prep(x_all, 0)
            ck("L1")
            for t in range(NT):
                hT, bh = hTt[t % 2], b_hTt[t % 2]
                finish(nxt_set, hT, bh, 0)
                ck("L2")
                if t + 1 < NT:
                    nxt_set = prep(x_all, (t + 1) * 512)
                pb = [bank_ring.next() for _ in range(4)]
                lhs = [(wkv[:, 0, :, :], 128, 0), (wkv[:, 1, :, :], 128, 0), (wkv[:, 2, :, :], 64, 0), (wkv[:, 2, :, :], 64, 64)]
                for g in range(4):
                    w, m, c0 = lhs[g]
                    for kc in range(KC):
                        mm(pb[g][0][0:m, :], w[:, kc, c0:c0 + m], hT[:, kc, :], kc == 0, kc == KC - 1, [b_wkv, bh[kc]],
                           [pb[g][1]])
                ck("L3")
                ck("L4")
                act(sq[0][:, :], pb[0][0][:, :], AF.Square, [pb[0][1]], [b_sq[0]])
                act(sq[1][:, :], pb[1][0][:, :], AF.Square, [pb[1][1]], [b_sq[1]])
                act(sq[2][0:64, :], pb[2][0][0:64, :], AF.Square, [pb[2][1]], [b_sq[2]])
                sb_, sbb = bank_ring.next()
                mm(sb_[:, :], ones_bf[:, :], sq[0][:, :], True, False, [b_const, b_sq[0]], [sbb])
                mm(sb_[:, :], ones_bf[:, :], sq[1][:, :], False, True, [b_const, b_sq[1]], [sbb])
                act(lnc[:, :], sb_[:, :], AF.Ln, [sbb], [b_lnc], scale=1.0 / 256, bias=cst[:, 0:1])
                act(rstdc[:, :], lnc[:, :], AF.Exp, [b_lnc], [b_rstdc], scale=-0.5)
                for kc in range(2):
                    stt("dve", ckvnT[:, kc, t * 512:(t + 1) * 512], pb[kc][0][:, :], kvag[:, kc:kc + 1], rstdc[:, :],
                        ALU.mult, ALU.mult, [pb[kc][1], b_par, b_rstdc], [b_ckvn[t]])
                rb, rbb = bank_ring.next()
                for st in range(4):
                    mm(rb[:, st:st + 1], sq[2][0:64, st * 128:(st + 1) * 128], ones_bf[0:64, 0:1], True, True,
                       [b_sq[2], b_const], [rbb])
                cp("dve", ssr[:, t * 4:(t + 1) * 4], rb[:, 0:4], [rbb], [b_ssr[t]])
                stt("dve", t1[:, :], pb[2][0][0:64, :], kg[0:64, 1:2], cosK[:, t * 512:(t + 1) * 512], ALU.mult, ALU.mult,
                    [pb[2][1], b_par, b_cosK[t]], [b_t1])
                stt("dve", t2[:, :], pb[3][0][0:64, :], kg[0:64, 2:3], sinK[:, t * 512:(t + 1) * 512], ALU.mult, ALU.mult,
                    [pb[3][1], b_par, b_sinK[t]], [b_t2])
                tt("pool", krT[:, t * 512:(t + 1) * 512], t1[:, :], t2[:, :], ALU.add, [b_t1, b_t2], [b_kr[t]])
                ck("L5")
            S.barrier()
            ck("L")
        lscope.close()

        for hf in range(NHALF):
            with ExitStack() as hs:
                yTc = sb(hs, "yTc", [128, 8, 1024], BF16)
                szaT = sb(hs, "szaT", [128, 8, 1024], BF16)
                cqnT = sb(hs, "cqnT", [128, 4, 1024], BF16)
                b_yTc = bufs(f"yTc{hf}", 8, 2)
                b_sza = bufs(f"sza{hf}", 8, 2)
                b_cqn = bufs(f"cqn{hf}", 4, 2)
                with ExitStack() as h2s:
                    hT2 = sb(h2s, "hT2", [128, KC, 1024], BF16)
                    b_hT2 = bufs(f"hT2{hf}", 2, KC)
                    hTh = sb(h2s, "hTh", [128, KC, 4], BF16)
                    b_hTh = Buf(f"hTh{hf}")

                    with ExitStack() as es:
                        prep, finish, junk = make_hT_builder(es, nxt=2)
                        set0 = prep(x_own, (hf * 2) * 512)
                        set1 = prep(x_own, (hf * 2 + 1) * 512)
                        finish(set0, hT2, b_hT2[0], 0)
                        finish(set1, hT2, b_hT2[1], 512)
                        xh = sb(es, "xh", [4, D], F32)
                        xnh = sb(es, "xnh", [4, D], BF16)
                        ssh = sb(es, "ssh", [4, 1], F32)
                        b_xh, b_xnh, b_ssh = Buf("xh"), Buf("xnh"), Buf("ssh")
                        S.dma("sp", xh[:], x_halo[hf * 4:(hf + 1) * 4, :], writes=[b_xh])
                        op("pool", lambda e: e.memset(ssh[:, :], 0.0), writes=[b_ssh])
                        act(junk[0:4, :], xh[:, :], AF.Square, [b_xh], [b_ssh], accum_out=ssh[:, 0:1])
                        act(ssh[:, :], ssh[:, :], AF.Ln, [b_ssh, b_const], [b_ssh], scale=1.0 / D, bias=cst[0:4, 0:1])
                        act(ssh[:, :], ssh[:, :], AF.Exp, [b_ssh], [b_ssh], scale=-0.5)
                        ts("dve", xnh[:, :], xh[:, :], ssh[:, 0:1], None, ALU.mult, None, [b_xh, b_ssh], [b_xnh])
                        for kc in range(KC):
                            op("pe", lambda e: e.transpose(out=tp_tiles[0][:, kc * 4:(kc + 1) * 4], in_=xnh[0:4, kc * 128:(kc + 1) * 128],
                                                           identity=ident[0:4, 0:4]), [b_xnh, b_const], [tp_bufs[0]])
                        for kc in range(KC):
                            act(hTh[:, kc, :], tp_tiles[0][:, kc * 4:(kc + 1) * 4], AF.Identity, [tp_bufs[0], b_mod], [b_hTh],
                                scale=gs_col[:, kc:kc + 1], bias=shift_col[:, kc:kc + 1])
                        S.barrier()
                        ck("2a")

                    with ExitStack() as es:
                        NW = 8
                        wr = [sb(es, f"wr{i}", [128, KC, 128], BF16) for i in range(NW)]
                        wring = Ring(list(zip(wr, bufs("wr", NW))))
                        order = list(range(3, 47))
                        loaded = {}
                        nload = [0]

                        def prefetch(upto):
                            while nload[0] < len(order) and nload[0] < upto:
                                cc = order[nload[0]]
                                w, wb = wring.next()
                                S.dma("pool", w[:], w_in_l[cc], writes=[wb])
                                loaded[cc] = (w, wb)
                                nload[0] += 1

                        def proj(w, wb, ot, bank, bb):
                            for kc in range(KC):
                                mm(bank[:, :], w[:, kc, :], hT2[:, kc, ot * 512:(ot + 1) * 512], kc == 0, kc == KC - 1,
                                   [wb, b_hT2[ot][kc]], [bb])

                        u_ext = [sb(es, f"uext{i}", [128, 514], F32) for i in range(2)]
                        b_uext = bufs("uext", 2)
                        cc_sb = [sb(es, f"ccsb{i}", [128, 512], F32) for i in range(2)]
                        b_ccsb = bufs("ccsb", 2)
                        acc = [sb(es, f"acc{i}", [128, 512], F32) for i in range(2)]
                        b_acc = bufs("acc", 2)
                        szt = [sb(es, f"szt{i}", [128, 512], F32) for i in range(2)]
                        b_szt = bufs("szt", 2)
                        tmp = [sb(es, f"tmpc{i}", [128, 512], F32) for i in range(2)]
                        b_tmp = bufs("tmpc", 2)
                        hc_sb = sb(es, "hc_sb", [128, 4], F32)
                        uh = sb(es, "uh", [128, 4], F32)
                        b_hc, b_uh = Buf("hc"), Buf("uh")
                        prefetch(4)
                        for i in range(8):
                            prefetch(4 * i + 8)
                            (wx, wxb), (wc, wcb), (wbm, wbb), (wz, wzb) = [loaded[3 + 4 * i + g] for g in range(4)]
                            hb, hbb = bank_ring.next()
                            for kc in range(KC):
                                mm(hb[:, 0:4], wx[:, kc, :], hTh[:, kc, :], kc == 0, kc == KC - 1, [wxb, b_hTh], [hbb])
                            for kc in range(KC):
                                mm(hb[:, 4:8], wc[:, kc, :], hTh[:, kc, :], kc == 0, kc == KC - 1, [wcb, b_hTh], [hbb])
                            cp("act", hc_sb[:, :], hb[:, 4:8], [hbb], [b_hc])
                            tt("dve", uh[:, :], hb[:, 0:4], hc_sb[:, :], ALU.mult, [hbb, b_hc], [b_uh])
                            tt("dve", uh[:, :], uh[:, :], hvalid[:, hf * 4:(hf + 1) * 4], ALU.mult, [b_uh, b_par], [b_uh])
                            for ot in range(2):
                                bx, bxb = bank_ring.next()
                                bc, bcb = bank_ring.next()
                                bbk, bbb = bank_ring.next()
                                bz, bzb = bank_ring.next()
                                proj(wx, wxb, ot, bx, bxb)
                                proj(wc, wcb, ot, bc, bcb)
                                proj(wbm, wbb, ot, bbk, bbb)
                                proj(wz, wzb, ot, bz, bzb)
                                cp("act", cc_sb[ot][:, :], bc[:, :], [bcb], [b_ccsb[ot]])
                                tt("dve", u_ext[ot][:, 2:514], bx[:, :], cc_sb[ot][:, :], ALU.mult, [bxb, b_ccsb[ot]], [b_uext[ot]])
                                cp("dve", u_ext[ot][:, 0:2], uh[:, ot * 2:(ot + 1) * 2], [b_uh], [b_uext[ot]])
                                ts("pool", acc[ot][:, :], u_ext[ot][:, 2:514], convw[:, i, 2:3], None, ALU.mult, None,
                                   [b_uext[ot], b_par], [b_acc[ot]])
                                stt("dve", acc[ot][:, :], u_ext[ot][:, 1:513], convw[:, i, 1:2], acc[ot][:, :], ALU.mult, ALU.add,
                                    [b_uext[ot], b_par, b_acc[ot]], [b_acc[ot]])
                                stt("dve", acc[ot][:, :], u_ext[ot][:, 0:512], convw[:, i, 0:1], acc[ot][:, :], ALU.mult, ALU.add,
                                    [b_uext[ot], b_par, b_acc[ot]], [b_acc[ot]])
                                act(szt[ot][:, :], bz[:, :], AF.Silu, [bzb], [b_szt[ot]])
                                tt("dve", tmp[ot][:, :], acc[ot][:, :], bbk[:, :], ALU.mult, [b_acc[ot], bbb], [b_tmp[ot]])
                                tt("pool", yTc[:, i, ot * 512:(ot + 1) * 512], tmp[ot][:, :], szt[ot][:, :], ALU.mult,
                                   [b_tmp[ot], b_szt[ot]], [b_yTc[i][ot]])
                        prefetch(32 + 4 + 2)
                        wq = [loaded[35 + c] for c in range(4)]
                        sqc = [sb(es, f"sqc{i}", [128, 512], BF16) for i in range(4)]
                        b_sqc = bufs("sqc", 4)
                        lnq2 = sb(es, "lnq2", [128, 512], F32)
                        rsq2 = sb(es, "rsq2", [128, 512], F32)
                        b_lnq2, b_rsq2 = Buf("lnq2"), Buf("rsq2")
                        for ot in range(2):
                            pbs = [bank_ring.next() for _ in range(4)]
                            for c in range(4):
                                proj(wq[c][0], wq[c][1], ot, pbs[c][0], pbs[c][1])
                                act(sqc[c][:, :], pbs[c][0][:, :], AF.Square, [pbs[c][1]], [b_sqc[c]])
                            sbk, sbb = bank_ring.next()
                            for c in range(4):
                                mm(sbk[:, :], ones_bf[:, :], sqc[c][:, :], c == 0, c == 3, [b_const, b_sqc[c]], [sbb])
                            act(lnq2[:, :], sbk[:, :], AF.Ln, [sbb], [b_lnq2], scale=1.0 / 512, bias=cst[:, 0:1])
                            act(rsq2[:, :], lnq2[:, :], AF.Exp, [b_lnq2], [b_rsq2], scale=-0.5)
                            for c in range(4):
                                stt("dve", cqnT[:, c, ot * 512:(ot + 1) * 512], pbs[c][0][:, :], qag[:, c:c + 1], rsq2[:, :],
                                    ALU.mult, ALU.mult, [pbs[c][1], b_par, b_rsq2], [b_cqn[c][ot]])
                        for i in range(8):
                            prefetch(36 + i + 3)
                            wz, wzb = loaded[39 + i]
                            for ot in range(2):
                                bz, bzb = bank_ring.next()
                                proj(wz, wzb, ot, bz, bzb)
                                act(szaT[:, i, ot * 512:(ot + 1) * 512], bz[:, :], AF.Silu, [bzb], [b_sza[i][ot]])
                        S.barrier()
                        ck("2b")

                s_lo = 2 * hf
                NG = s_lo + 2
                with ExitStack() as hs3:
                    yTa = sb(hs3, "yTa", [128, 8, 1024], BF16)
                    b_yTa = bufs(f"yTa{hf}", 8, 2)
                    with ExitStack() as es:
                        cosq = sb(es, "cosq", [64, 1024], F32)
                        sinq = sb(es, "sinq", [64, 1024], F32)
                        b_cosq, b_sinq = bufs("cosq", 2), bufs("sinq", 2)
                        masks = sb(es, "masks", [128, 16, 512], BF16)
                        b_masks = Buf("masks")
                        wqb = [sb(es, f"wqb{i}", [128, 4, 256], BF16) for i in range(2)]
                        wkvb = [sb(es, f"wkvb{i}", [128, 2, 256], BF16) for i in range(2)]
                        b_wqb, b_wkvb = bufs("wqb", 2), bufs("wkvb", 2)
                        S.dma("pool", wqb[0][:], wqb_l[0], writes=[b_wqb[0]])
                        S.dma("pool", wkvb[0][:], wkvb_l[0], writes=[b_wkvb[0]])
                        with ExitStack() as ets:
                            qidx_bc = sb(ets, "qidx_bc", [128, 512], F32)
                            b_qidx = Buf("qidx")
                            tables = make_tables(ets, 512, "Q")
                            for ot in range(2):
                                c0 = hf * 1024 + ot * 512
                                tables(pos_own[0:1, c0:c0 + 512], cosq[:, ot * 512:(ot + 1) * 512], sinq[:, ot * 512:(ot + 1) * 512],
                                       b_cosq[ot], b_sinq[ot])
                            S.dma("sp", qidx_bc[:], qidx[0:1, hf * 1024:hf * 1024 + 512].broadcast_to([128, 512]),
                                  writes=[b_qidx])
                            for kb in range(16):
                                ts("dve", masks[:, kb, :], qidx_bc[:, :], kidx[:, 16 * s_lo + kb:16 * s_lo + kb + 1], None,
                                   ALU.is_ge, None, [b_qidx, b_par], [b_masks])
                            print("phase3 sbuf remaining", nc.sbuf_bytes_remaining)
                            S.barrier()
                            ck("3t")
                        QTn = [sb(es, f"QTn{i}", [128, 1024], BF16) for i in range(2)]
                        QTr = [sb(es, f"QTr{i}", [64, 1024], BF16) for i in range(2)]
                        b_QT = bufs("QT", 2, 2)
                        KTg = [sb(es, f"KTg{i}", [128, 2048], BF16) for i in range(2)]
                        Vg = [sb(es, f"Vg{i}", [128, 16, 128], BF16) for i in range(2)]
                        b_KTg = bufs("KTg", 2, 4)
                        b_Vg = bufs("Vg", 2, 4)
                        rk = [sb(es, f"rk{i}", [128, 16], F32) for i in range(2)]
                        rkt = [sb(es, f"rkt{i}", [128, 16], F32) for i in range(2)]
                        b_rk, b_rkt = bufs("rk", 2), bufs("rkt", 2)
                        NPB = 6
                        Pb = [sb(es, f"Pb{i}", [128, 512], BF16) for i in range(NPB)]
                        pring = Ring(list(zip(Pb, bufs("Pb", NPB))))
                        sqn = sb(es, "sqn", [128, 512], BF16)
                        sqr = sb(es, "sqr", [64, 512], BF16)
                        b_sqn, b_sqr = Buf("sqn"), Buf("sqr")
                        lnq3 = sb(es, "lnq3", [128, 512], F32)
                        rsq3 = sb(es, "rsq3", [128, 512], F32)
                        b_lnq3, b_rsq3 = Buf("lnq3"), Buf("rsq3")
                        q1 = sb(es, "q1", [64, 512], F32)
                        q2 = sb(es, "q2", [64, 512], F32)
                        b_q1, b_q2 = Buf("q1"), Buf("q2")
                        OT = [(banks[0], bank_bufs[0]), (banks[1], bank_bufs[1])]
                        SM = [(banks[2], bank_bufs[2]), (banks[3], bank_bufs[3])]
                        r3 = Ring([(banks[i], bank_bufs[i]) for i in (4, 5, 6, 7)])
                        accs = [[sb(es, f"accs{a}{b}", [128, 512], F32) for b in range(2)] for a in range(2)]
                        b_accs = bufs("accs", 2, 2)
                        junk3 = sb(es, "junk3", [128, 128], BF16)
                        osb = [sb(es, f"osb{i}", [128, 512], F32) for i in range(2)]
                        ssb = [sb(es, f"ssb{i}", [128, 512], F32) for i in range(2)]
                        b_osb, b_ssb = bufs("osb", 2), bufs("ssb", 2)
                        ATT_BIAS = -0.5 * math.log(192.0)


                        def item_Q(h):
                            hp = h % 2
                            for ot in range(2):
                                op("pool", lambda e: e.memset(accs[hp][ot][:, :], 0.0), writes=[b_accs[hp][ot]])
                            if h + 1 < 8:
                                S.dma("pool", wqb[1 - hp][:], wqb_l[h + 1], writes=[b_wqb[1 - hp]])
                                S.dma("pool", wkvb[1 - hp][:], wkvb_l[h + 1], writes=[b_wkvb[1 - hp]])
                            for ot in range(2):
                                cs = slice(ot * 512, (ot + 1) * 512)
                                bn, bnb = r3.next()
                                for kc in range(4):
                                    mm(bn[:, :], wqb[hp][:, kc, 0:128], cqnT[:, kc, cs], kc == 0, kc == 3,
                                       [b_wqb[hp], b_cqn[kc][ot]], [bnb])
                                br, brb = r3.next()
                                for kc in range(4):
                                    mm(br[0:64, :], wqb[hp][:, kc, 128:192], cqnT[:, kc, cs], kc == 0, kc == 3,
                                       [b_wqb[hp], b_cqn[kc][ot]], [brb])
                                yield
                                act(sqn[:, :], bn[:, :], AF.Square, [bnb], [b_sqn])
                                act(sqr[:, :], br[0:64, :], AF.Square, [brb], [b_sqr])
                                bs_, bsb = r3.next()
                                mm(bs_[:, :], ones_bf[:, :], sqn[:, :], True, False, [b_const, b_sqn], [bsb])
                                mm(bs_[:, :], ones_bf[0:64, :], sqr[:, :], False, True, [b_const, b_sqr], [bsb])
                                act(lnq3[:, :], bs_[:, :], AF.Ln, [bsb, b_const], [b_lnq3], scale=1.0 / 192, bias=cst[:, 0:1])
                                act(rsq3[:, :], lnq3[:, :], AF.Exp, [b_lnq3], [b_rsq3], scale=-0.5)
                                stt("dve", QTn[hp][:, cs], bn[:, :], qg[:, 0:1], rsq3[:, :], ALU.mult, ALU.mult,
                                    [bnb, b_par, b_rsq3], [b_QT[hp][ot]])
                                stt("dve", q1[:, :], br[0:64, :], qg[0:64, 1:2], cosq[:, cs], ALU.mult, ALU.mult,
                                    [brb, b_par, b_cosq[ot]], [b_q1])
                                yield
                                bp, bpb = r3.next()
                                for kc in range(4):
                                    mm(bp[0:64, :], wqb[hp][:, kc, 192:256], cqnT[:, kc, cs], kc == 0, kc == 3,
                                       [b_wqb[hp], b_cqn[kc][ot]], [bpb])
                                stt("dve", q2[:, :], bp[0:64, :], qg[0:64, 2:3], sinq[:, cs], ALU.mult, ALU.mult,
                                    [bpb, b_par, b_sinq[ot]], [b_q2])
                                tt("pool", q1[:, :], q1[:, :], q2[:, :], ALU.add, [b_q1, b_q2], [b_q1])
                                tt("pool", QTr[hp][:, cs], q1[:, :], rsq3[0:64, :], ALU.mult, [b_q1, b_rsq3], [b_QT[hp][ot]])
                                yield
                            if DBG and h == 0 and hf == 0:
                                S.dma("sp", dbg_qn, QTn[hp][:], reads=b_QT[hp])
                                S.dma("sp", dbg_qr, QTr[hp][:], reads=b_QT[hp])
                                S.dma("sp", dbg_kr, krT[:, 0:2048], reads=b_kr[0:4])
                                S.dma("sp", dbg_cs[:, 0:1024], cosq[:], reads=b_cosq)
                                S.dma("sp", dbg_cs[:, 1024:2048], sinq[:], reads=b_sinq)

                        def item_G(h, G):
                            hp = h % 2
                            gp = (h * NG + G) % 2
                            cached = G < s_lo
                            if not cached:
                                op("pool", lambda e: e.memset(rkt[gp][:, :], 0.0), writes=[b_rkt[gp]])
                            for t4 in range(4):
                                gt = 4 * G + t4
                                bk, bkb = r3.next()
                                for kc in range(2):
                                    mm(bk[:, :], wkvb[hp][:, kc, 0:128], ckvnT[:, kc, gt * 512:(gt + 1) * 512], kc == 0, kc == 1,
                                       [b_wkvb[hp], b_ckvn[gt]], [bkb])
                                ts("dve", KTg[gp][:, t4 * 512:(t4 + 1) * 512], bk[:, :], kg[:, 0:1], None, ALU.mult, None,
                                   [bkb, b_par], [b_KTg[gp][t4]])
                                yield
                            if cached:
                                for b4 in range(4):
                                    bv, bvb = r3.next()
                                    for q in range(4):
                                        blk = 16 * G + b4 * 4 + q
                                        for kc in range(2):
                                            mm(bv[:, q * 128:(q + 1) * 128], ckvnT[:, kc, blk * 128:(blk + 1) * 128],
                                               wkvb[hp][:, kc, 128:256], kc == 0, kc == 1, [b_wkvb[hp], b_ckvn[blk // 4]], [bvb])
                                    cp("dve", Vg[gp][:, b4 * 4:(b4 + 1) * 4, :], bv[:, :].rearrange("p (a b) -> p a b", a=4),
                                       [bvb], [b_Vg[gp][b4]])
                                    yield
                                return
                            for b2 in range(8):
                                bv, bvb = r3.next()
                                for q in range(2):
                                    blk = 16 * G + b2 * 2 + q
                                    for kc in range(2):
                                        mm(bv[:, q * 256:(q + 1) * 256], ckvnT[:, kc, blk * 128:(blk + 1) * 128],
                                           wkvb[hp][:, kc, 0:256], kc == 0, kc == 1, [b_wkvb[hp], b_ckvn[blk // 4]], [bvb])
                                for q in range(2):
                                    c = b2 * 2 + q
                                    act(junk3[:, :], bv[:, q * 256:q * 256 + 128], AF.Square, [bvb], [b_rkt[gp]],
                                        accum_out=rkt[gp][:, c:c + 1])
                                cp("dve", Vg[gp][:, b2 * 2:(b2 + 1) * 2, :],
                                   bv[:, :].rearrange("p (a b) -> p a b", a=2)[:, :, 128:256], [bvb], [b_Vg[gp][b2 // 2]])
                                yield
                            tt("dve", rkt[gp][:, :], rkt[gp][:, :], ssr[:, G * 16:(G + 1) * 16], ALU.add,
                               [b_rkt[gp]] + [b_ssr[4 * G + i] for i in range(4)], [b_rkt[gp]])
                            act(rkt[gp][:, :], rkt[gp][:, :], AF.Ln, [b_rkt[gp], b_const], [b_rkt[gp]], scale=1.0, bias=cst[:, 1:2])
                            act(rk_all[:, h, G * 16:(G + 1) * 16], rkt[gp][:, :], AF.Exp, [b_rkt[gp]], [b_rkall[h][G]], scale=-0.5)
                            if DBG and h == 0 and hf == 0 and G == 0:
                                S.dma("sp", dbg_k, KTg[gp][:], reads=b_KTg[gp])
                                S.dma("sp", dbg_v, Vg[gp][:], reads=b_Vg[gp])
                                S.dma("sp", dbg_rk, rk_all[:, h, 0:16], reads=[b_rkall[h][G]])

                        def item_A(h, G, filler=None):
                            hp = h % 2
                            gp = (h * NG + G) % 2
                            steps = []
                            for ot in range(2):
                                s = s_lo + ot
                                if s >= G:
                                    for kb in range(16):
                                        steps.append((ot, s, kb))
                            LAG = 2
                            pend = []
                            for i in range(len(steps) + LAG):
                                if i < len(steps):
                                    ot, s, kb = steps[i]
                                    cs = slice(ot * 512, (ot + 1) * 512)
                                    gblk = 16 * G + kb
                                    bs_, bsb = r3.next()
                                    mm(bs_[:, :], KTg[gp][:, kb * 128:(kb + 1) * 128], QTn[hp][:, cs], True, False,
                                       [b_KTg[gp][kb // 4], b_QT[hp][ot]], [bsb])
                                    mm(bs_[:, :], krT[0:64, gblk * 128:(gblk + 1) * 128], QTr[hp][:, cs], False, True,
                                       [b_kr[gblk // 4], b_QT[hp][ot]], [bsb])
                                    Pt, Ptb = pring.next()
                                    act(Pt[:, :], bs_[:, :], AF.Exp, [bsb, b_rkall[h][G]], [Ptb], scale=rk_all[:, h, gblk:gblk + 1])
                                    if G == s:
                                        tt("dve", Pt[:, :], Pt[:, :], masks[:, kb, :], ALU.mult, [b_masks, Ptb], [Ptb])
                                    pend.append((ot, s, kb, Pt, Ptb))
                                if i >= LAG:
                                    ot, s, kb, Pt, Ptb = pend[i - LAG]
                                    first = (G == 0 and kb == 0)
                                    last = (G == s and kb == 15)
                                    mm(OT[ot][0][:, :], Vg[gp][:, kb, :], Pt[:, :], first, last, [b_Vg[gp][kb // 4], Ptb],
                                       [OT[ot][1]])
                                    if G == s:
                                        mm(SM[ot][0][:, :], ones_bf[:, :], Pt[:, :], kb == 0, kb == 15 and s == 0,
                                           [b_const, Ptb], [SM[ot][1]])
                                        if kb == 15 and s > 0:
                                            mm(SM[ot][0][:, :], ones_f[:, :], accs[hp][ot][:, :], False, True,
                                               [b_const, b_accs[hp][ot]], [SM[ot][1]])
                                    else:
                                        stt("dve", accs[hp][ot][:, :], Pt[:, :], 1.0, accs[hp][ot][:, :], ALU.mult, ALU.add,
                                            [b_accs[hp][ot], Ptb], [b_accs[hp][ot]])
                                if filler is not None:
                                    next(filler, None)
                            if filler is not None:
                                for _ in filler:
                                    pass

                        def item_F(h):
                            hp = h % 2
                            for ot in range(2):
                                cp("act", ssb[ot][:, :], SM[ot][0][:, :], [SM[ot][1]], [b_ssb[ot]])
                                cp("dve", osb[ot][:, :], OT[ot][0][:, :], [OT[ot][1]], [b_osb[ot]])
                            for ot in range(2):
                                cs = slice(ot * 512, (ot + 1) * 512)
                                op("dve", lambda e: e.reciprocal(out=ssb[ot][:, :], in_=ssb[ot][:, :]), [b_ssb[ot]], [b_ssb[ot]])
                                tt("pool", osb[ot][:, :], osb[ot][:, :], ssb[ot][:, :], ALU.mult, [b_osb[ot], b_ssb[ot]], [b_osb[ot]])
                                tt("pool", yTa[:, h, cs], osb[ot][:, :], szaT[:, h, cs], ALU.mult, [b_osb[ot], b_sza[h][ot]],
                                   [b_yTa[h][ot]])

                        import itertools
                        for _ in item_Q(0):
                            pass
                        for _ in item_G(0, 0):
                            pass
                        for h in range(8):
                            for G in range(NG):
                                if G + 1 < NG:
                                    filler = item_G(h, G + 1)
                                elif h + 1 < 8:
                                    filler = itertools.chain(item_Q(h + 1), item_G(h + 1, 0))
                                else:
                                    filler = None
                                item_A(h, G, filler)
                            item_F(h)

                        S.barrier()
                        ck("3")

                    if DBG:
                        S.dma("sp", dbg_yc[hf], yTc[:], reads=[b for l in b_yTc for b in l])
                        S.dma("sp", dbg_ya[hf], yTa[:], reads=[b for l in b_yTa for b in l])
                    with ExitStack() as es:
                        wo = [sb(es, f"wo{i}", [128, KC, 512], BF16) for i in range(2)]
                        b_wo = bufs("wo", 2)
                        NX = 4
                        xres = [sb(es, f"xres{i}", [128, 512], F32) for i in range(NX)]
                        b_xres = bufs("xres", NX)
                        o1 = [sb(es, f"o1{i}", [128, 512], F32) for i in range(2)]
                        b_o1 = bufs("o1", 2)
                        o2 = [sb(es, f"o2{i}", [128, 512], F32) for i in range(3)]
                        b_o2 = bufs("o2", 3)
                        items = [(ct, tb) for ct in range(4) for tb in range(8)]

                        def load_x(n):
                            ct, tb = items[n]
                            r0 = hf * 1024 + tb * 128
                            S.dma("sp", xres[n % NX][:], x_own[r0:r0 + 128, ct * 512:(ct + 1) * 512], writes=[b_xres[n % NX]])

                        S.dma("pool", wo[0][:], w_out_l[0], writes=[b_wo[0]])
                        load_x(0)
                        load_x(1)
                        for n, (ct, tb) in enumerate(items):
                            w, wb = wo[ct % 2], b_wo[ct % 2]
                            if tb == 0 and ct + 1 < 4:
                                S.dma("pool", wo[(ct + 1) % 2][:], w_out_l[ct + 1], writes=[b_wo[(ct + 1) % 2]])
                            if n + 2 < len(items):
                                load_x(n + 2)
                            r0 = hf * 1024 + tb * 128
                            ot = tb // 4
                            ts_ = slice(tb * 128, (tb + 1) * 128)
                            xi, oi, pi = n % NX, n % 3, n % 2
                            bo, bob = bank_ring.next()
                            for kc in range(KC):
                                if kc < 8:
                                    l, lb = yTc[:, kc, ts_], b_yTc[kc][ot]
                                else:
                                    l, lb = yTa[:, kc - 8, ts_], b_yTa[kc - 8][ot]
                                mm(bo[:, :], l, w[:, kc, :], kc == 0, kc == KC - 1, [lb, wb], [bob])
                            tt("dve", o1[pi][:, :], bo[:, :], gate_bc[:, ct * 512:(ct + 1) * 512], ALU.mult, [bob, b_gate],
                               [b_o1[pi]])
                            tt("pool", o2[oi][:, :], o1[pi][:, :], xres[xi][:, :], ALU.add, [b_o1[pi], b_xres[xi]],
                               [b_o2[oi]])
                            S.dma("sp", out_d[r0:r0 + 128, ct * 512:(ct + 1) * 512], o2[oi][:, :], reads=[b_o2[oi]])
                        S.barrier()
                        ck("4")

    except _Stop:
        pass
    S.off = False
    S.barrier()
    top.close()
    return nc


def _prep_inputs(x, c, positions, ada_w, ada_b, norm_g, w_in, conv_w, q_a_g, w_q_b, kv_a_g, w_kv_b, q_g, k_g, w_out):
    f = np.float32
    x = np.asarray(x, f)
    B, SEQ, _ = x.shape
    NT = SEQ // 512
    NSLOT = NT // 4
    NBLK = SEQ // 128
    positions = np.asarray(positions, np.int32)
    ada_w = np.asarray(ada_w, f)[0]
    w_in = np.asarray(w_in, f)[0]
    w_q_b = np.asarray(w_q_b, f)[0]
    w_kv_b = np.asarray(w_kv_b, f)[0]
    w_out = np.asarray(w_out, f)[0]

    def cols(v, n):
        return np.ascontiguousarray(np.asarray(v, f).reshape(n, 128).T)

    def wl(w):
        n = w.shape[1] // 128
        return w.reshape(KC, 128, n, 128).transpose(2, 1, 0, 3)

    perm = (np.arange(64) + 32) % 64
    wr = w_in[:, 4864:4928]
    chunks = [wl(w_in[:, 4608:4864]), wl(np.concatenate([wr, wr[:, perm]], axis=1))]
    conv = np.stack([w_in[:, 0:1024], w_in[:, 2048:3072], w_in[:, 1024:2048], w_in[:, 3072:4096]], 0)
    conv = conv.reshape(4, D, 8, 128).transpose(2, 0, 1, 3).reshape(32, D, 128)
    chunks.append(conv.reshape(32, KC, 128, 128).transpose(0, 2, 1, 3))
    chunks.append(wl(w_in[:, 4096:4608]))
    chunks.append(wl(w_in[:, 4928:5952]))
    w_in_l = np.ascontiguousarray(np.concatenate(chunks, 0))
    assert w_in_l.shape == (47, 128, KC, 128)
    ada_w_l = np.ascontiguousarray(ada_w.reshape(KC, 128, 12, 512).transpose(2, 1, 0, 3))
    w_out_l = np.ascontiguousarray(w_out.reshape(KC, 128, 4, 512).transpose(2, 1, 0, 3))
    wq = w_q_b.reshape(4, 128, 8, 192).transpose(2, 1, 0, 3)
    wqb_l = np.ascontiguousarray(np.concatenate([wq, wq[..., 128:192][..., perm]], -1))
    wkvb_l = np.ascontiguousarray(w_kv_b.reshape(2, 128, 8, 256).transpose(2, 1, 0, 3))

    def g3(g):
        g = np.asarray(g, f)[0]
        o = np.zeros((128, 3), f)
        o[:, 0] = g[0:128]
        o[0:64, 1] = g[128:192]
        o[0:64, 2] = g[128:192][perm]
        return o

    inv_freq = (10000.0 ** (-(np.arange(0, 64, 2, dtype=np.float64)) / 64.0)).astype(f)
    rope_c = np.zeros((64, 4), f)
    rope_c[:, 0] = np.concatenate([inv_freq, inv_freq])
    rope_c[:, 1] = np.concatenate([-np.ones(32), np.ones(32)]) * SHRINK
    rope_c[:, 2] = SHRINK
    kidx = (np.arange(NBLK)[None, :] * 128 + np.arange(128)[:, None]).astype(f)
    shared = dict(
        ada_w_l=ada_w_l, ada_b=np.asarray(ada_b, f).reshape(1, 6144), ng_col=cols(np.asarray(norm_g)[0], KC),
        w_in_l=w_in_l, convw_col=np.ascontiguousarray(np.asarray(conv_w, f)[0].reshape(3, 8, 128).transpose(2, 1, 0)),
        qag_col=cols(np.asarray(q_a_g)[0], 4), kvag_col=cols(np.asarray(kv_a_g)[0], 2), wqb_l=wqb_l, wkvb_l=wkvb_l,
        qg_col=g3(q_g), kg_col=g3(k_g), w_out_l=w_out_l, rope_c=rope_c, kidx=kidx)
    in_maps = []
    meta = []
    for core in range(NCORES):
        b, j = core // 4, core % 4
        tiles = [4 * s + j for s in range(NSLOT)]
        rows = np.concatenate([np.arange(T * 512, (T + 1) * 512) for T in tiles])
        halo = np.zeros((NSLOT * 2, D), f)
        hv = np.zeros((128, NSLOT * 2), f)
        for s, T in enumerate(tiles):
            if T > 0:
                halo[2 * s:2 * s + 2] = x[b, T * 512 - 2:T * 512]
                hv[:, 2 * s:2 * s + 2] = 1.0
        m = dict(shared)
        m.update(x_all=x[b], x_own=np.ascontiguousarray(x[b, rows]), x_halo=halo, halo_valid=hv,
                 pos_all=np.ascontiguousarray(positions[b][None, :]), pos_own=np.ascontiguousarray(positions[b, rows][None, :]),
                 qidx=rows.astype(f)[None, :], c_col=cols(np.asarray(c, f)[b], KC))
        in_maps.append(m)
        meta.append((b, rows))
    return in_maps, meta, (B, SEQ)


_NC_CACHE = {}


def kernel(**inputs):
    in_maps, meta, (B, SEQ) = _prep_inputs(**inputs)
    if SEQ not in _NC_CACHE:
        _NC_CACHE[SEQ] = build(SEQ)
    nc = _NC_CACHE[SEQ]
    res = run_bass_kernel_spmd(nc, in_maps, core_ids=list(range(NCORES)))
    out = np.empty((B, SEQ, D), np.float32)
    for core, (b, rows) in enumerate(meta):
        out[b, rows] = res.results[core]["out"]
    if DBG_HOOK is not None:
        DBG_HOOK(res)
    return out
```

```python
import math
from contextlib import ExitStack

import numpy as np
import concourse.bass as bass
import concourse.mybir as mybir
from concourse.bass_utils import run_bass_kernel_spmd

F32 = mybir.dt.float32
BF16 = mybir.dt.bfloat16
I32 = mybir.dt.int32
AF = mybir.ActivationFunctionType
ALU = mybir.AluOpType

D = 2048
KC = D // 128
NCORES = 8
EPS = 1e-6
TWO_PI = 2.0 * math.pi
CW1 = 6.28125
CW2 = TWO_PI - 6.28125
SHRINK = 1.0 - 2e-6
PI_SAFE = 3.141592
KSTOP = ""
DBG_HOOK = None


class Buf:
    __slots__ = ("name", "lw", "reads", "sem", "dcnt", "excl")

    def __init__(self, name, excl=False):
        self.name = name
        self.excl = excl
        self.lw = None
        self.reads = {}
        self.sem = None
        self.dcnt = 0


def bufs(name, *dims):
    if not dims:
        return Buf(name)
    return [bufs(f"{name}_{i}", *dims[1:]) for i in range(dims[0])]


class Sched:
    def __init__(self, nc):
        self.nc = nc
        self.eng = {"pe": nc.tensor, "act": nc.scalar, "dve": nc.vector, "pool": nc.gpsimd, "sp": nc.sync}
        self.sem = {k: nc.alloc_semaphore(name="es_" + k) for k in self.eng}
        self.cnt = {k: 0 for k in self.eng}
        self.seen = {k: {} for k in self.eng}
        self.dma_bufs = []
        self.free_sems = []
        self.off = False

    def _wait(self, e, deps):
        need = {}
        for d in deps:
            if d is None:
                continue
            k, v = d
            if k == e and (e == "pe" or v <= self.cnt[e] - 4):
                continue
            if need.get(k, 0) < v:
                need[k] = v
        seen = self.seen[e]
        for k, v in need.items():
            if isinstance(k, Buf):
                v = k.dcnt
                so = k.sem
            else:
                so = self.sem[k]
            if seen.get(k, 0) >= v:
                continue
            seen[k] = v
            self.eng[e].wait_ge(so, v)

    @staticmethod
    def _deps(reads, writes):
        deps = []
        for b in reads:
            deps.append(b.lw)
        for b in writes:
            deps.append(b.lw)
            deps.extend(b.reads.items())
        return deps

    def op(self, e, fn, reads=(), writes=()):
        if self.off:
            return None
        if any(b.excl for b in reads):
            writes = list(writes) + [b for b in reads if b.excl]
            reads = [b for b in reads if not b.excl]
        self._wait(e, self._deps(reads, writes))
        ins = fn(self.eng[e])
        self.cnt[e] += 1
        ins.then_inc(self.sem[e], 1)
        v = self.cnt[e]
        for b in reads:
            if b.reads.get(e, 0) < v:
                b.reads[e] = v
        for b in writes:
            b.lw = (e, v)
            b.reads = {}
        return ins

    def dma(self, q, out, in_, reads=(), writes=(), sembuf=None):
        if self.off:
            return None
        self._wait(q, self._deps(reads, writes))
        if sembuf is None:
            sembuf = writes[0] if writes else reads[0]
        if sembuf.sem is None:
            sembuf.sem = self.nc.alloc_semaphore(name=f"ds{len(self.dma_bufs)}_" + sembuf.name)
            self.dma_bufs.append(sembuf)
        ins = self.eng[q].dma_start(out=out, in_=in_)
        sembuf.dcnt += 16
        ins.then_inc(sembuf.sem, 16)
        for b in reads:
            b.reads[sembuf] = sembuf.dcnt
        for b in writes:
            b.lw = (sembuf, sembuf.dcnt)
            b.reads = {}
        return ins

    def barrier(self):
        if self.off:
            return
        deps = [(k, c) for k, c in self.cnt.items() if k != "sp" and c > 0]
        deps += [(b, b.dcnt) for b in self.dma_bufs]
        self._wait("sp", deps)
        self.eng["sp"].sem_inc(self.sem["sp"], 1)
        self.cnt["sp"] += 1
        for e in self.eng:
            if e != "sp":
                self._wait(e, [("sp", self.cnt["sp"])])
                for k, v in self.seen["sp"].items():
                    if self.seen[e].get(k, 0) < v:
                        self.seen[e][k] = v


class _Stop(Exception):
    pass


class Ring:
    def __init__(self, items):
        self.items = list(items)
        self.i = 0

    def next(self):
        it = self.items[self.i % len(self.items)]
        self.i += 1
        return it


def build(SEQ):
    NT = SEQ // 512
    NSLOT = NT // 4
    NHALF = NSLOT // 2
    NBLK = SEQ // 128
    NOWN = NSLOT * 512
    NCC = 47

    nc = bass.Bass("TRN2", target_bir_lowering=False)

    def din(name, shape, dt=F32):
        return nc.dram_tensor(name, list(shape), dt, kind="ExternalInput").ap()

    x_all = din("x_all", [SEQ, D])
    x_own = din("x_own", [NOWN, D])
    x_halo = din("x_halo", [NSLOT * 2, D])
    halo_valid = din("halo_valid", [128, NSLOT * 2])
    pos_all = din("pos_all", [1, SEQ], I32)
    pos_own = din("pos_own", [1, NOWN], I32)
    qidx = din("qidx", [1, NOWN])
    kidx_d = din("kidx", [128, NBLK])
    c_col_d = din("c_col", [128, KC])
    ada_w_l = din("ada_w_l", [12, 128, KC, 512])
    ada_b_d = din("ada_b", [1, 6144])
    ng_col_d = din("ng_col", [128, KC])
    w_in_l = din("w_in_l", [NCC, 128, KC, 128])
    convw_d = din("convw_col", [128, 8, 3])
    qag_d = din("qag_col", [128, 4])
    kvag_d = din("kvag_col", [128, 2])
    wqb_l = din("wqb_l", [8, 128, 4, 256])
    wkvb_l = din("wkvb_l", [8, 128, 2, 256])
    qg_d = din("qg_col", [128, 3])
    kg_d = din("kg_col", [128, 3])
    w_out_l = din("w_out_l", [4, 128, KC, 512])
    rope_c_d = din("rope_c", [64, 4])
    out_d = nc.dram_tensor("out", [NOWN, D], F32, kind="ExternalOutput").ap()
    DBG = DBG_HOOK is not None
    if DBG:
        dbg_yc = nc.dram_tensor("dbg_yc", [NHALF, 128, 8, 1024], BF16, kind="ExternalOutput").ap()
        dbg_ya = nc.dram_tensor("dbg_ya", [NHALF, 128, 8, 1024], BF16, kind="ExternalOutput").ap()
        dbg_qn = nc.dram_tensor("dbg_qn", [128, 1024], BF16, kind="ExternalOutput").ap()
        dbg_qr = nc.dram_tensor("dbg_qr", [64, 1024], BF16, kind="ExternalOutput").ap()
        dbg_k = nc.dram_tensor("dbg_k", [128, 2048], BF16, kind="ExternalOutput").ap()
        dbg_kr = nc.dram_tensor("dbg_kr", [64, 2048], BF16, kind="ExternalOutput").ap()
        dbg_v = nc.dram_tensor("dbg_v", [128, 16, 128], BF16, kind="ExternalOutput").ap()
        dbg_rk = nc.dram_tensor("dbg_rk", [128, 16], F32, kind="ExternalOutput").ap()
        dbg_acc = nc.dram_tensor("dbg_acc", [128, 512], F32, kind="ExternalOutput").ap()
        dbg_ssb = nc.dram_tensor("dbg_ssb", [128, 512], F32, kind="ExternalOutput").ap()
        dbg_ssb2 = nc.dram_tensor("dbg_ssb2", [128, 512], F32, kind="ExternalOutput").ap()
        dbg_cs = nc.dram_tensor("dbg_cs", [64, 2048], F32, kind="ExternalOutput").ap()

    S = Sched(nc)
    op = S.op

    def mm(out, lhsT, rhs, start, stop, reads, writes):
        return op("pe", lambda e: e.matmul(out, lhsT=lhsT, rhs=rhs, start=start, stop=stop), reads, writes)

    def act(out, in_, func, reads, writes, scale=1.0, bias=0.0, accum_out=None):
        if accum_out is not None:
            return op("act", lambda e: e.activation(out=out, in_=in_, func=func, bias=bias, scale=scale,
                                                    accum_out=accum_out), reads, writes)
        return op("act", lambda e: e.activation(out=out, in_=in_, func=func, bias=bias, scale=scale), reads, writes)

    def ts(eng, out, in0, s1, s2, op0, op1, reads, writes):
        if s2 is None:
            return op(eng, lambda e: e.tensor_scalar(out=out, in0=in0, scalar1=s1, scalar2=None, op0=op0), reads, writes)
        return op(eng, lambda e: e.tensor_scalar(out=out, in0=in0, scalar1=s1, scalar2=s2, op0=op0, op1=op1), reads, writes)

    def stt(eng, out, in0, scalar, in1, op0, op1, reads, writes):
        return op(eng, lambda e: e.scalar_tensor_tensor(out=out, in0=in0, scalar=scalar, in1=in1, op0=op0, op1=op1),
                  reads, writes)

    def tt(eng, out, in0, in1, o, reads, writes):
        return op(eng, lambda e: e.tensor_tensor(out=out, in0=in0, in1=in1, op=o), reads, writes)

    def cp(eng, out, in_, reads, writes):
        if eng == "act":
            return op(eng, lambda e: e.activation(out=out, in_=in_, func=AF.Copy), reads, writes)
        return op(eng, lambda e: e.tensor_copy(out=out, in_=in_), reads, writes)

    top = ExitStack()
    kstop = KSTOP

    def ck(name):
        if kstop == name:
            S.off = True

    uid = [0]

    def sb(es, name, shape, dt):
        uid[0] += 1
        return es.enter_context(nc.sbuf_tensor(f"{name}_u{uid[0]}", list(shape), dt))

    banks = [top.enter_context(nc.psum_tensor(f"pb{i}", [128, 512], F32)) for i in range(8)]
    bank_bufs = [Buf(f"pb{i}", excl=True) for i in range(8)]
    tp_tiles = [banks[6][:, :].bitcast(BF16), banks[7][:, :].bitcast(BF16)]
    tp_bufs = [bank_bufs[6], bank_bufs[7]]

    ckvnT = sb(top, "ckvnT", [128, 2, SEQ], BF16)
    krT = sb(top, "krT", [64, SEQ], BF16)
    ssr = sb(top, "ssr", [128, NBLK], F32)
    b_ckvn = bufs("ckvn", NT)
    b_kr = bufs("kr", NT)
    b_ssr = bufs("ssr", NT)
    ident = sb(top, "ident", [128, 128], BF16)
    ones_bf = sb(top, "ones_bf", [128, 128], BF16)
    ones_f = sb(top, "ones_f", [128, 128], F32)
    cst = sb(top, "cst", [128, 2], F32)
    rk_all = sb(top, "rk_all", [128, 8, NBLK], F32)
    b_rkall = bufs("rkall", 8, NBLK // 16)
    b_const = Buf("const")
    gs_col = sb(top, "gs_col", [128, KC], F32)
    shift_col = sb(top, "shift_col", [128, KC], F32)
    gate_bc = sb(top, "gate_bc", [128, D], F32)
    b_mod = Buf("mod")
    b_gate = Buf("gate")
    c_col = sb(top, "c_col", [128, KC], F32)
    ng_col = sb(top, "ng_col", [128, KC], F32)
    convw = sb(top, "convw", [128, 8, 3], F32)
    qag = sb(top, "qag", [128, 4], F32)
    kvag = sb(top, "kvag", [128, 2], F32)
    qg = sb(top, "qg", [128, 3], F32)
    kg = sb(top, "kg", [128, 3], F32)
    rope_c = sb(top, "rope_c", [64, 4], F32)
    kidx = sb(top, "kidx", [128, NBLK], F32)
    hvalid = sb(top, "hvalid", [128, NSLOT * 2], F32)
    b_par = Buf("par")

    for dst, src in ((c_col, c_col_d), (ng_col, ng_col_d), (convw, convw_d), (qag, qag_d), (kvag, kvag_d),
                     (qg, qg_d), (kg, kg_d), (rope_c, rope_c_d), (kidx, kidx_d), (hvalid, halo_valid)):
        S.dma("sp", dst[:], src, writes=[b_par])

    try:
        def make_tables(es, N, tag):
            posi = sb(es, f"posi{tag}", [64, N], I32)
            posf = sb(es, f"posf{tag}", [64, N], F32)
            ang = sb(es, f"ang{tag}", [64, N], F32)
            kf = [sb(es, f"kf{tag}{i}", [64, N], F32) for i in range(2)]
            ki = [sb(es, f"ki{tag}{i}", [64, N], I32) for i in range(2)]
            rr = [sb(es, f"rr{tag}{i}", [64, N], F32) for i in range(2)]
            b_posi, b_posf, b_ang = Buf("posi" + tag), Buf("posf" + tag), Buf("ang" + tag)
            b_kf, b_ki, b_rr = bufs("kf" + tag, 2), bufs("ki" + tag, 2), bufs("rr" + tag, 2)

            def tables(pos_src, cos2, sinS, b_cos, b_sin):
                S.dma("sp", posi[:], pos_src.broadcast_to([64, N]), writes=[b_posi])
                cp("dve", posf[:, :], posi[:, :], [b_posi], [b_posf])
                ts("dve", ang[:, :], posf[:, :], rope_c[:, 0:1], None, ALU.mult, None, [b_posf, b_par], [b_ang])
                for i, (eng, phase, outt, bo, ccol) in enumerate((("dve", 0.0, sinS, b_sin, 1), ("dve", 0.25, cos2, b_cos, 2))):
                    ts(eng, kf[i][:, :], ang[:, :], 1.0 / TWO_PI, phase, ALU.mult, ALU.add, [b_ang], [b_kf[i]])
                    cp(eng, ki[i][:, :], kf[i][:, :], [b_kf[i]], [b_ki[i]])
                    cp(eng, kf[i][:, :], ki[i][:, :], [b_ki[i]], [b_kf[i]])
                    stt(eng, rr[i][:, :], kf[i][:, :], -CW1, ang[:, :], ALU.mult, ALU.add, [b_kf[i], b_ang], [b_rr[i]])
                    stt(eng, rr[i][:, :], kf[i][:, :], -CW2, rr[i][:, :], ALU.mult, ALU.add, [b_kf[i], b_rr[i]], [b_rr[i]])
                    if phase != 0.0:
                        ts(eng, rr[i][:, :], rr[i][:, :], phase * TWO_PI, None, ALU.add, None, [b_rr[i]], [b_rr[i]])
                    ts(eng, rr[i][:, :], rr[i][:, :], -PI_SAFE, PI_SAFE, ALU.max, ALU.min, [b_rr[i]], [b_rr[i]])
                    act(outt, rr[i][:, :], AF.Sin, [b_rr[i], b_par], [bo], scale=rope_c[:, ccol:ccol + 1])

            return tables

        lscope = ExitStack()
        cosK = sb(lscope, "cosK", [64, SEQ], BF16)
        sinK = sb(lscope, "sinK", [64, SEQ], BF16)
        b_cosK, b_sinK = bufs("cosK", NT), bufs("sinK", NT)
        wkv = sb(lscope, "wkv", [128, 3, KC, 128], BF16)
        b_wkv = Buf("wkv")

        with ExitStack() as es:
            identf = sb(es, "identf", [128, 128], F32)
            b_idf = Buf("identf")
            op("pool", lambda e: e.memset(identf[:, :], 0.0), writes=[b_idf])
            op("pool", lambda e: e.affine_select(out=identf[:, :], in_=identf[:, :], compare_op=ALU.not_equal, fill=1.0,
                                                 base=0, pattern=[[-1, 128]], channel_multiplier=1), writes=[b_idf])
            cp("dve", ident[:, :], identf[:, :], [b_idf], [b_const])
            op("pool", lambda e: e.memset(ones_f[:, :], 1.0), writes=[b_const])
            op("pool", lambda e: e.memset(cst[:, 0:1], EPS), writes=[b_const])
            op("pool", lambda e: e.memset(cst[:, 1:2], 192.0 * EPS), writes=[b_const])
            cp("dve", ones_bf[:, :], ones_f[:, :], [b_const], [b_const])

            tablesK = make_tables(es, 512, "K")
            tdone = [0]

            def tables_upto(n):
                while tdone[0] < min(n, NT):
                    t = tdone[0]
                    tablesK(pos_all[0:1, t * 512:(t + 1) * 512], cosK[:, t * 512:(t + 1) * 512],
                            sinK[:, t * 512:(t + 1) * 512], b_cosK[t], b_sinK[t])
                    tdone[0] += 1

            sc_bf = sb(es, "sc_bf", [128, KC], BF16)
            b_sc = Buf("sc")
            act(sc_bf[:, :], c_col[:, :], AF.Silu, [b_par], [b_sc])
            modrow = sb(es, "modrow", [1, 6144], F32)
            adab = sb(es, "adab", [1, 6144], F32)
            b_adab = Buf("adab")
            S.dma("sp", adab[:], ada_b_d, writes=[b_adab])
            b_modrow = Buf("modrow")
            adaw = [sb(es, f"adaw{i}", [128, KC, 512], BF16) for i in range(2)]
            b_adaw = bufs("adaw", 2, 4)
            for ct in range(12):
                wt, wb = adaw[ct % 2], b_adaw[ct % 2]
                for qd in range(4):
                    S.dma("pool", wt[:, qd * 4:(qd + 1) * 4, :], ada_w_l[ct][:, qd * 4:(qd + 1) * 4, :], writes=[wb[qd]])
                if ct == 10:
                    for i in range(3):
                        S.dma("pool", wkv[:, i, :, :], w_in_l[i], writes=[b_wkv])
                bk, bb = banks[ct % 2], bank_bufs[ct % 2]
                for kc in range(KC):
                    mm(bk[0:1, :], sc_bf[:, kc:kc + 1], wt[:, kc, :], kc == 0, kc == KC - 1, [b_sc, wb[kc // 4]], [bb])
                tt("dve", modrow[0:1, ct * 512:(ct + 1) * 512], bk[0:1, :], adab[0:1, ct * 512:(ct + 1) * 512], ALU.add,
                   [bb, b_adab], [b_modrow])
                tables_upto((ct + 1) * NT // 12 + 1)
            tables_upto(NT)
            cb, cbb = banks[2], bank_bufs[2]
            for c in range(32):
                mm(cb[:, c:c + 1], modrow[0:1, c * 128:(c + 1) * 128], ones_f[0:1, 0:1], True, True, [b_modrow, b_const], [cbb])
            cp("dve", shift_col[:, :], cb[:, 0:KC], [cbb], [b_mod])
            stt("dve", gs_col[:, :], cb[:, KC:2 * KC], 1.0, ng_col[:, :], ALU.add, ALU.mult, [cbb, b_par], [b_mod])
            for ct in range(4):
                bk, bb = banks[3 + ct % 2], bank_bufs[3 + ct % 2]
                mm(bk[:, :], ones_f[0:1, :], modrow[0:1, 4096 + ct * 512:4096 + (ct + 1) * 512], True, True,
                   [b_modrow, b_const], [bb])
                cp("act", gate_bc[:, ct * 512:(ct + 1) * 512], bk[:, :], [bb], [b_gate])
            S.barrier()
            ck("prologue")

        bank_ring = Ring(list(zip(banks[:6], bank_bufs[:6])))
        ring8 = Ring(list(zip(banks, bank_bufs)))

        def make_hT_builder(es, nxt=3):
            xt = [sb(es, f"xt{i}", [128, D], F32) for i in range(nxt)]
            b_xt = bufs("xt", nxt)
            xn = [sb(es, f"xn{i}", [128, D], BF16) for i in range(8)]
            b_xn = bufs("xn", 8)
            junk = sb(es, "junk", [128, D], BF16)
            ssq = [sb(es, f"ssq{i}", [128, 4], F32) for i in range(2)]
            lnq = [sb(es, f"lnq{i}", [128, 4], F32) for i in range(2)]
            rsq = [sb(es, f"rsq{i}", [128, 4], F32) for i in range(2)]
            b_ssq = bufs("ssq", 2)
            b_lnq = bufs("lnq", 2)
            b_rsq = bufs("rsq", 2)
            state = {"n": 0, "xi": 0}

            def prep(src, row0):
                p2 = state["n"] % 2
                state["n"] += 1
                op("pool", lambda e: e.memset(ssq[p2][:, :], 0.0), writes=[b_ssq[p2]])
                xis = []
                for st in range(4):
                    xi = state["xi"] % nxt
                    state["xi"] += 1
                    xis.append(xi)
                    S.dma("sp", xt[xi][:], src[row0 + st * 128:row0 + (st + 1) * 128, :], writes=[b_xt[xi]])
                    act(junk[:, :], xt[xi][:, :], AF.Square, [b_xt[xi]], [b_ssq[p2]], accum_out=ssq[p2][:, st:st + 1])
                    if nxt < 4 or st == 3:
                        pass
                act(lnq[p2][:, :], ssq[p2][:, :], AF.Ln, [b_ssq[p2]], [b_lnq[p2]], scale=1.0 / D, bias=cst[:, 0:1])
                act(rsq[p2][:, :], lnq[p2][:, :], AF.Exp, [b_lnq[p2]], [b_rsq[p2]], scale=-0.5)
                return p2, xis

            def prep_full(src, row0):
                p2 = state["n"] % 2
                state["n"] += 1
                for st in range(4):
                    xi = state["xi"] % nxt
                    state["xi"] += 1
                    ni = p2 * 4 + st
                    S.dma("sp", xt[xi][:], src[row0 + st * 128:row0 + (st + 1) * 128, :], writes=[b_xt[xi]])
                    op("pool", lambda e: e.memset(ssq[p2][:, st:st + 1], 0.0), writes=[b_ssq[p2]])
                    act(junk[:, :], xt[xi][:, :], AF.Square, [b_xt[xi]], [b_ssq[p2]], accum_out=ssq[p2][:, st:st + 1])
                    act(lnq[p2][:, st:st + 1], ssq[p2][:, st:st + 1], AF.Ln, [b_ssq[p2]], [b_lnq[p2]], scale=1.0 / D, bias=cst[:, 0:1])
                    act(rsq[p2][:, st:st + 1], lnq[p2][:, st:st + 1], AF.Exp, [b_lnq[p2]], [b_rsq[p2]], scale=-0.5)
                    ts("dve", xn[ni][:, :], xt[xi][:, :], rsq[p2][:, st:st + 1], None, ALU.mult,
                       None, [b_xt[xi], b_rsq[p2]], [b_xn[ni]])
                return p2

            def finish(p2, hT, b_hT, col0):
                for kc in range(KC):
                    h = kc % 2
                    tph = tp_tiles[h][:, 0:512]
                    for st in range(4):
                        ni = p2 * 4 + st
                        op("pe", lambda e: e.transpose(out=tph[:, st * 128:(st + 1) * 128],
                                                       in_=xn[ni][:, kc * 128:(kc + 1) * 128], identity=ident[:, :]),
                           [b_xn[ni], b_const], [tp_bufs[h]])
                    dst = hT[:, kc, col0:col0 + 512]
                    if kc % 2 == 0:
                        act(dst, tph, AF.Identity, [tp_bufs[h], b_mod], [b_hT[kc]], scale=gs_col[:, kc:kc + 1],
                            bias=shift_col[:, kc:kc + 1])
                    else:
                        ts("dve", dst, tph, gs_col[:, kc:kc + 1], shift_col[:, kc:kc + 1], ALU.mult, ALU.add,
                           [tp_bufs[h], b_mod], [b_hT[kc]])

            return prep_full, finish, junk

        with ExitStack() as es:
            prep, finish, _ = make_hT_builder(es)
            hTt = [sb(es, f"hTt{i}", [128, KC, 512], BF16) for i in range(2)]
            b_hTt = bufs("hTt", 2, KC)
            sq = [sb(es, f"sqL{i}", [128, 512], BF16) for i in range(3)]
            b_sq = bufs("sqL", 3)
            lnc = sb(es, "lnc", [128, 512], F32)
            rstdc = sb(es, "rstdc", [128, 512], F32)
            b_lnc, b_rstdc = Buf("lnc"), Buf("rstdc")
            t1 = sb(es, "t1L", [64, 512], F32)
            t2 = sb(es, "t2L", [64, 512], F32)
            b_t1, b_t2 = Buf("t1L"), Buf("t2L")

            nxt_set = prep(x_all, 0)
            ck("L1")
            for t in range(NT):
                hT, bh = hTt[t % 2], b_hTt[t % 2]
                finish(nxt_set, hT, bh, 0)
                ck("L2")
                if t + 1 < NT:
                    nxt_set = prep(x_all, (t + 1) * 512)
                pb = [bank_ring.next() for _ in range(4)]
                lhs = [(wkv[:, 0, :, :], 128, 0), (wkv[:, 1, :, :], 128, 0), (wkv[:, 2, :, :], 64, 0), (wkv[:, 2, :, :], 64, 64)]
                for g in range(4):
                    w, m, c0 = lhs[g]
                    for kc in range(KC):
                        mm(pb[g][0][0:m, :], w[:, kc, c0:c0 + m], hT[:, kc, :], kc == 0, kc == KC - 1, [b_wkv, bh[kc]],
                           [pb[g][1]])
                ck("L3")
                ck("L4")
                act(sq[0][:, :], pb[0][0][:, :], AF.Square, [pb[0][1]], [b_sq[0]])
                act(sq[1][:, :], pb[1][0][:, :], AF.Square, [pb[1][1]], [b_sq[1]])
                act(sq[2][0:64, :], pb[2][0][0:64, :], AF.Square, [pb[2][1]], [b_sq[2]])
                sb_, sbb = bank_ring.next()
                mm(sb_[:, :], ones_bf[:, :], sq[0][:, :], True, False, [b_const, b_sq[0]], [sbb])
                mm(sb_[:, :], ones_bf[:, :], sq[1][:, :], False, True, [b_const, b_sq[1]], [sbb])
                act(lnc[:, :], sb_[:, :], AF.Ln, [sbb], [b_lnc], scale=1.0 / 256, bias=cst[:, 0:1])
                act(rstdc[:, :], lnc[:, :], AF.Exp, [b_lnc], [b_rstdc], scale=-0.5)
                for kc in range(2):
                    stt("dve", ckvnT[:, kc, t * 512:(t + 1) * 512], pb[kc][0][:, :], kvag[:, kc:kc + 1], rstdc[:, :],
                        ALU.mult, ALU.mult, [pb[kc][1], b_par, b_rstdc], [b_ckvn[t]])
                rb, rbb = bank_ring.next()
                for st in range(4):
                    mm(rb[:, st:st + 1], sq[2][0:64, st * 128:(st + 1) * 128], ones_bf[0:64, 0:1], True, True,
                       [b_sq[2], b_const], [rbb])
                cp("dve", ssr[:, t * 4:(t + 1) * 4], rb[:, 0:4], [rbb], [b_ssr[t]])
                stt("dve", t1[:, :], pb[2][0][0:64, :], kg[0:64, 1:2], cosK[:, t * 512:(t + 1) * 512], ALU.mult, ALU.mult,
                    [pb[2][1], b_par, b_cosK[t]], [b_t1])
                stt("dve", t2[:, :], pb[3][0][0:64, :], kg[0:64, 2:3], sinK[:, t * 512:(t + 1) * 512], ALU.mult, ALU.mult,
                    [pb[3][1], b_par, b_sinK[t]], [b_t2])
                tt("pool", krT[:, t * 512:(t + 1) * 512], t1[:, :], t2[:, :], ALU.add, [b_t1, b_t2], [b_kr[t]])
                ck("L5")
            S.barrier()
            ck("L")
        lscope.close()

        for hf in range(NHALF):
            with ExitStack() as hs:
                yTc = sb(hs, "yTc", [128, 8, 1024], BF16)
                szaT = sb(hs, "szaT", [128, 8, 1024], BF16)
                cqnT = sb(hs, "cqnT", [128, 4, 1024], BF16)
                b_yTc = bufs(f"yTc{hf}", 8, 2)
                b_sza = bufs(f"sza{hf}", 8, 2)
                b_cqn = bufs(f"cqn{hf}", 4, 2)
                with ExitStack() as h2s:
                    hT2 = sb(h2s, "hT2", [128, KC, 1024], BF16)
                    b_hT2 = bufs(f"hT2{hf}", 2, KC)
                    hTh = sb(h2s, "hTh", [128, KC, 4], BF16)
                    b_hTh = Buf(f"hTh{hf}")

                    with ExitStack() as es:
                        prep, finish, junk = make_hT_builder(es, nxt=2)
                        set0 = prep(x_own, (hf * 2) * 512)
                        set1 = prep(x_own, (hf * 2 + 1) * 512)
                        finish(set0, hT2, b_hT2[0], 0)
                        finish(set1, hT2, b_hT2[1], 512)
                        xh = sb(es, "xh", [4, D], F32)
                        xnh = sb(es, "xnh", [4, D], BF16)
                        ssh = sb(es, "ssh", [4, 1], F32)
                        b_xh, b_xnh, b_ssh = Buf("xh"), Buf("xnh"), Buf("ssh")
                        S.dma("sp", xh[:], x_halo[hf * 4:(hf + 1) * 4, :], writes=[b_xh])
                        op("pool", lambda e: e.memset(ssh[:, :], 0.0), writes=[b_ssh])
                        act(junk[0:4, :], xh[:, :], AF.Square, [b_xh], [b_ssh], accum_out=ssh[:, 0:1])
                        act(ssh[:, :], ssh[:, :], AF.Ln, [b_ssh, b_const], [b_ssh], scale=1.0 / D, bias=cst[0:4, 0:1])
                        act(ssh[:, :], ssh[:, :], AF.Exp, [b_ssh], [b_ssh], scale=-0.5)
                        ts("dve", xnh[:, :], xh[:, :], ssh[:, 0:1], None, ALU.mult, None, [b_xh, b_ssh], [b_xnh])
                        for kc in range(KC):
                            op("pe", lambda e: e.transpose(out=tp_tiles[0][:, kc * 4:(kc + 1) * 4], in_=xnh[0:4, kc * 128:(kc + 1) * 128],
                                                           identity=ident[0:4, 0:4]), [b_xnh, b_const], [tp_bufs[0]])
                        for kc in range(KC):
                            act(hTh[:, kc, :], tp_tiles[0][:, kc * 4:(kc + 1) * 4], AF.Identity, [tp_bufs[0], b_mod], [b_hTh],
                                scale=gs_col[:, kc:kc + 1], bias=shift_col[:, kc:kc + 1])
                        S.barrier()
                        ck("2a")

                    with ExitStack() as es:
                        NW = 8
                        wr = [sb(es, f"wr{i}", [128, KC, 128], BF16) for i in range(NW)]
                        wring = Ring(list(zip(wr, bufs("wr", NW))))
                        order = list(range(3, 47))
                        loaded = {}
                        nload = [0]

                        def prefetch(upto):
                            while nload[0] < len(order) and nload[0] < upto:
                                cc = order[nload[0]]
                                w, wb = wring.next()
                                S.dma("pool", w[:], w_in_l[cc], writes=[wb])
                                loaded[cc] = (w, wb)
                                nload[0] += 1

                        def proj(w, wb, ot, bank, bb):
                            for kc in range(KC):
                                mm(bank[:, :], w[:, kc, :], hT2[:, kc, ot * 512:(ot + 1) * 512], kc == 0, kc == KC - 1,
                                   [wb, b_hT2[ot][kc]], [bb])

                        u_ext = [sb(es, f"uext{i}", [128, 514], F32) for i in range(2)]
                        b_uext = bufs("uext", 2)
                        cc_sb = [sb(es, f"ccsb{i}", [128, 512], F32) for i in range(2)]
                        b_ccsb = bufs("ccsb", 2)
                        acc = [sb(es, f"acc{i}", [128, 512], F32) for i in range(2)]
                        b_acc = bufs("acc", 2)
                        szt = [sb(es, f"szt{i}", [128, 512], F32) for i in range(2)]
                        b_szt = bufs("szt", 2)
                        tmp = [sb(es, f"tmpc{i}", [128, 512], F32) for i in range(2)]
                        b_tmp = bufs("tmpc", 2)
                        hc_sb = sb(es, "hc_sb", [128, 4], F32)
                        uh = sb(es, "uh", [128, 4], F32)
                        b_hc, b_uh = Buf("hc"), Buf("uh")
                        prefetch(4)
                        for i in range(8):
                            prefetch(4 * i + 8)
                            (wx, wxb), (wc, wcb), (wbm, wbb), (wz, wzb) = [loaded[3 + 4 * i + g] for g in range(4)]
                            hb, hbb = bank_ring.next()
                            for kc in range(KC):
                                mm(hb[:, 0:4], wx[:, kc, :], hTh[:, kc, :], kc == 0, kc == KC - 1, [wxb, b_hTh], [hbb])
                            for kc in range(KC):
                                mm(hb[:, 4:8], wc[:, kc, :], hTh[:, kc, :], kc == 0, kc == KC - 1, [wcb, b_hTh], [hbb])
                            cp("act", hc_sb[:, :], hb[:, 4:8], [hbb], [b_hc])
                            tt("dve", uh[:, :], hb[:, 0:4], hc_sb[:, :], ALU.mult, [hbb, b_hc], [b_uh])
                            tt("dve", uh[:, :], uh[:, :], hvalid[:, hf * 4:(hf + 1) * 4], ALU.mult, [b_uh, b_par], [b_uh])
                            for ot in range(2):
                                bx, bxb = bank_ring.next()
                                bc, bcb = bank_ring.next()
                                bbk, bbb = bank_ring.next()
                                bz, bzb = bank_ring.next()
                                proj(wx, wxb, ot, bx, bxb)
                                proj(wc, wcb, ot, bc, bcb)
                                proj(wbm, wbb, ot, bbk, bbb)
                                proj(wz, wzb, ot, bz, bzb)
                                cp("act", cc_sb[ot][:, :], bc[:, :], [bcb], [b_ccsb[ot]])
                                tt("dve", u_ext[ot][:, 2:514], bx[:, :], cc_sb[ot][:, :], ALU.mult, [bxb, b_ccsb[ot]], [b_uext[ot]])
                                cp("dve", u_ext[ot][:, 0:2], uh[:, ot * 2:(ot + 1) * 2], [b_uh], [b_uext[ot]])
                                ts("pool", acc[ot][:, :], u_ext[ot][:, 2:514], convw[:, i, 2:3], None, ALU.mult, None,
                                   [b_uext[ot], b_par], [b_acc[ot]])
                                stt("dve", acc[ot][:, :], u_ext[ot][:, 1:513], convw[:, i, 1:2], acc[ot][:, :], ALU.mult, ALU.add,
                                    [b_uext[ot], b_par, b_acc[ot]], [b_acc[ot]])
                                stt("dve", acc[ot][:, :], u_ext[ot][:, 0:512], convw[:, i, 0:1], acc[ot][:, :], ALU.mult, ALU.add,
                                    [b_uext[ot], b_par, b_acc[ot]], [b_acc[ot]])
                                act(szt[ot][:, :], bz[:, :], AF.Silu, [bzb], [b_szt[ot]])
                                tt("dve", tmp[ot][:, :], acc[ot][:, :], bbk[:, :], ALU.mult, [b_acc[ot], bbb], [b_tmp[ot]])
                                tt("pool", yTc[:, i, ot * 512:(ot + 1) * 512], tmp[ot][:, :], szt[ot][:, :], ALU.mult,
                                   [b_tmp[ot], b_szt[ot]], [b_yTc[i][ot]])
                        prefetch(32 + 4 + 2)
                        wq = [loaded[35 + c] for c in range(4)]
                        sqc = [sb(es, f"sqc{i}", [128, 512], BF16) for i in range(4)]
                        b_sqc = bufs("sqc", 4)
                        lnq2 = sb(es, "lnq2", [128, 512], F32)
                        rsq2 = sb(es, "rsq2", [128, 512], F32)
                        b_lnq2, b_rsq2 = Buf("lnq2"), Buf("rsq2")
                        for ot in range(2):
                            pbs = [bank_ring.next() for _ in range(4)]
                            for c in range(4):
                                proj(wq[c][0], wq[c][1], ot, pbs[c][0], pbs[c][1])
                                act(sqc[c][:, :], pbs[c][0][:, :], AF.Square, [pbs[c][1]], [b_sqc[c]])
                            sbk, sbb = bank_ring.next()
                            for c in range(4):
                                mm(sbk[:, :], ones_bf[:, :], sqc[c][:, :], c == 0, c == 3, [b_const, b_sqc[c]], [sbb])
                            act(lnq2[:, :], sbk[:, :], AF.Ln, [sbb], [b_lnq2], scale=1.0 / 512, bias=cst[:, 0:1])
                            act(rsq2[:, :], lnq2[:, :], AF.Exp, [b_lnq2], [b_rsq2], scale=-0.5)
                            for c in range(4):
                                stt("dve", cqnT[:, c, ot * 512:(ot + 1) * 512], pbs[c][0][:, :], qag[:, c:c + 1], rsq2[:, :],
                                    ALU.mult, ALU.mult, [pbs[c][1], b_par, b_rsq2], [b_cqn[c][ot]])
                        for i in range(8):
                            prefetch(36 + i + 3)
                            wz, wzb = loaded[39 + i]
                            for ot in range(2):
                                bz, bzb = bank_ring.next()
                                proj(wz, wzb, ot, bz, bzb)
                                act(szaT[:, i, ot * 512:(ot + 1) * 512], bz[:, :], AF.Silu, [bzb], [b_sza[i][ot]])
                        S.barrier()
                        ck("2b")

                s_lo = 2 * hf
                NG = s_lo + 2
                with ExitStack() as hs3:
                    yTa = sb(hs3, "yTa", [128, 8, 1024], BF16)
                    b_yTa = bufs(f"yTa{hf}", 8, 2)
                    with ExitStack() as es:
                        cosq = sb(es, "cosq", [64, 1024], F32)
                        sinq = sb(es, "sinq", [64, 1024], F32)
                        b_cosq, b_sinq = bufs("cosq", 2), bufs("sinq", 2)
                        masks = sb(es, "masks", [128, 16, 512], BF16)
                        b_masks = Buf("masks")
                        wqb = [sb(es, f"wqb{i}", [128, 4, 256], BF16) for i in range(2)]
                        wkvb = [sb(es, f"wkvb{i}", [128, 2, 256], BF16) for i in range(2)]
                        b_wqb, b_wkvb = bufs("wqb", 2), bufs("wkvb", 2)
                        S.dma("pool", wqb[0][:], wqb_l[0], writes=[b_wqb[0]])
                        S.dma("pool", wkvb[0][:], wkvb_l[0], writes=[b_wkvb[0]])
                        with ExitStack() as ets:
                            qidx_bc = sb(ets, "qidx_bc", [128, 512], F32)
                            b_qidx = Buf("qidx")
                            tables = make_tables(ets, 512, "Q")
                            for ot in range(2):
                                c0 = hf * 1024 + ot * 512
                                tables(pos_own[0:1, c0:c0 + 512], cosq[:, ot * 512:(ot + 1) * 512], sinq[:, ot * 512:(ot + 1) * 512],
                                       b_cosq[ot], b_sinq[ot])
                            S.dma("sp", qidx_bc[:], qidx[0:1, hf * 1024:hf * 1024 + 512].broadcast_to([128, 512]),
                                  writes=[b_qidx])
                            for kb in range(16):
                                ts("dve", masks[:, kb, :], qidx_bc[:, :], kidx[:, 16 * s_lo + kb:16 * s_lo + kb + 1], None,
                                   ALU.is_ge, None, [b_qidx, b_par], [b_masks])
                            print("phase3 sbuf remaining", nc.sbuf_bytes_remaining)
                            S.barrier()
                            ck("3t")
                        QTn = [sb(es, f"QTn{i}", [128, 1024], BF16) for i in range(2)]
                        QTr = [sb(es, f"QTr{i}", [64, 1024], BF16) for i in range(2)]
                        b_QT = bufs("QT", 2, 2)
                        KTg = [sb(es, f"KTg{i}", [128, 2048], BF16) for i in range(2)]
                        Vg = [sb(es, f"Vg{i}", [128, 16, 128], BF16) for i in range(2)]
                        b_KTg = bufs("KTg", 2, 4)
                        b_Vg = bufs("Vg", 2, 4)
                        rk = [sb(es, f"rk{i}", [128, 16], F32) for i in range(2)]
                        rkt = [sb(es, f"rkt{i}", [128, 16], F32) for i in range(2)]
                        b_rk, b_rkt = bufs("rk", 2), bufs("rkt", 2)
                        NPB = 6
                        Pb = [sb(es, f"Pb{i}", [128, 512], BF16) for i in range(NPB)]
                        pring = Ring(list(zip(Pb, bufs("Pb", NPB))))
                        sqn = sb(es, "sqn", [128, 512], BF16)
                        sqr = sb(es, "sqr", [64, 512], BF16)
                        b_sqn, b_sqr = Buf("sqn"), Buf("sqr")
                        lnq3 = sb(es, "lnq3", [128, 512], F32)
                        rsq3 = sb(es, "rsq3", [128, 512], F32)
                        b_lnq3, b_rsq3 = Buf("lnq3"), Buf("rsq3")
                        q1 = sb(es, "q1", [64, 512], F32)
                        q2 = sb(es, "q2", [64, 512], F32)
                        b_q1, b_q2 = Buf("q1"), Buf("q2")
                        OT = [(banks[0], bank_bufs[0]), (banks[1], bank_bufs[1])]
                        SM = [(banks[2], bank_bufs[2]), (banks[3], bank_bufs[3])]
                        r3 = Ring([(banks[i], bank_bufs[i]) for i in (4, 5, 6, 7)])
                        accs = [[sb(es, f"accs{a}{b}", [128, 512], F32) for b in range(2)] for a in range(2)]
                        b_accs = bufs("accs", 2, 2)
                        junk3 = sb(es, "junk3", [128, 128], BF16)
                        osb = [sb(es, f"osb{i}", [128, 512], F32) for i in range(2)]
                        ssb = [sb(es, f"ssb{i}", [128, 512], F32) for i in range(2)]
                        b_osb, b_ssb = bufs("osb", 2), bufs("ssb", 2)
                        ATT_BIAS = -0.5 * math.log(192.0)


                        def item_Q(h):
                            hp = h % 2
                            for ot in range(2):
                                op("pool", lambda e: e.memset(accs[hp][ot][:, :], 0.0), writes=[b_accs[hp][ot]])
                            if h + 1 < 8:
                                S.dma("pool", wqb[1 - hp][:], wqb_l[h + 1], writes=[b_wqb[1 - hp]])
                                S.dma("pool", wkvb[1 - hp][:], wkvb_l[h + 1], writes=[b_wkvb[1 - hp]])
                            for ot in range(2):
                                cs = slice(ot * 512, (ot + 1) * 512)
                                bn, bnb = r3.next()
                                for kc in range(4):
                                    mm(bn[:, :], wqb[hp][:, kc, 0:128], cqnT[:, kc, cs], kc == 0, kc == 3,
                                       [b_wqb[hp], b_cqn[kc][ot]], [bnb])
                                br, brb = r3.next()
                                for kc in range(4):
                                    mm(br[0:64, :], wqb[hp][:, kc, 128:192], cqnT[:, kc, cs], kc == 0, kc == 3,
                                       [b_wqb[hp], b_cqn[kc][ot]], [brb])
                                yield
                                act(sqn[:, :], bn[:, :], AF.Square, [bnb], [b_sqn])
                                act(sqr[:, :], br[0:64, :], AF.Square, [brb], [b_sqr])
                                bs_, bsb = r3.next()
                                mm(bs_[:, :], ones_bf[:, :], sqn[:, :], True, False, [b_const, b_sqn], [bsb])
                                mm(bs_[:, :], ones_bf[0:64, :], sqr[:, :], False, True, [b_const, b_sqr], [bsb])
                                act(lnq3[:, :], bs_[:, :], AF.Ln, [bsb, b_const], [b_lnq3], scale=1.0 / 192, bias=cst[:, 0:1])
                                act(rsq3[:, :], lnq3[:, :], AF.Exp, [b_lnq3], [b_rsq3], scale=-0.5)
                                stt("dve", QTn[hp][:, cs], bn[:, :], qg[:, 0:1], rsq3[:, :], ALU.mult, ALU.mult,
                                    [bnb, b_par, b_rsq3], [b_QT[hp][ot]])
                                stt("dve", q1[:, :], br[0:64, :], qg[0:64, 1:2], cosq[:, cs], ALU.mult, ALU.mult,
                                    [brb, b_par, b_cosq[ot]], [b_q1])
                                yield
                                bp, bpb = r3.next()
                                for kc in range(4):
                                    mm(bp[0:64, :], wqb[hp][:, kc, 192:256], cqnT[:, kc, cs], kc == 0, kc == 3,
                                       [b_wqb[hp], b_cqn[kc][ot]], [bpb])
                                stt("dve", q2[:, :], bp[0:64, :], qg[0:64, 2:3], sinq[:, cs], ALU.mult, ALU.mult,
                                    [bpb, b_par, b_sinq[ot]], [b_q2])
                                tt("pool", q1[:, :], q1[:, :], q2[:, :], ALU.add, [b_q1, b_q2], [b_q1])
                                tt("pool", QTr[hp][:, cs], q1[:, :], rsq3[0:64, :], ALU.mult, [b_q1, b_rsq3], [b_QT[hp][ot]])
                                yield
                            if DBG and h == 0 and hf == 0:
                                S.dma("sp", dbg_qn, QTn[hp][:], reads=b_QT[hp])
                                S.dma("sp", dbg_qr, QTr[hp][:], reads=b_QT[hp])
                                S.dma("sp", dbg_kr, krT[:, 0:2048], reads=b_kr[0:4])
                                S.dma("sp", dbg_cs[:, 0:1024], cosq[:], reads=b_cosq)
                                S.dma("sp", dbg_cs[:, 1024:2048], sinq[:], reads=b_sinq)

                        def item_G(h, G):
                            hp = h % 2
                            gp = (h * NG + G) % 2
                            cached = G < s_lo
                            if not cached:
                                op("pool", lambda e: e.memset(rkt[gp][:, :], 0.0), writes=[b_rkt[gp]])
                            for t4 in range(4):
                                gt = 4 * G + t4
                                bk, bkb = r3.next()
                                for kc in range(2):
                                    mm(bk[:, :], wkvb[hp][:, kc, 0:128], ckvnT[:, kc, gt * 512:(gt + 1) * 512], kc == 0, kc == 1,
                                       [b_wkvb[hp], b_ckvn[gt]], [bkb])
                                ts("dve", KTg[gp][:, t4 * 512:(t4 + 1) * 512], bk[:, :], kg[:, 0:1], None, ALU.mult, None,
                                   [bkb, b_par], [b_KTg[gp][t4]])
                                yield
                            if cached:
                                for b4 in range(4):
                                    bv, bvb = r3.next()
                                    for q in range(4):
                                        blk = 16 * G + b4 * 4 + q
                                        for kc in range(2):
                                            mm(bv[:, q * 128:(q + 1) * 128], ckvnT[:, kc, blk * 128:(blk + 1) * 128],
                                               wkvb[hp][:, kc, 128:256], kc == 0, kc == 1, [b_wkvb[hp], b_ckvn[blk // 4]], [bvb])
                                    cp("dve", Vg[gp][:, b4 * 4:(b4 + 1) * 4, :], bv[:, :].rearrange("p (a b) -> p a b", a=4),
                                       [bvb], [b_Vg[gp][b4]])
                                    yield
                                return
                            for b2 in range(8):
                                bv, bvb = r3.next()
                                for q in range(2):
                                    blk = 16 * G + b2 * 2 + q
                                    for kc in range(2):
                                        mm(bv[:, q * 256:(q + 1) * 256], ckvnT[:, kc, blk * 128:(blk + 1) * 128],
                                           wkvb[hp][:, kc, 0:256], kc == 0, kc == 1, [b_wkvb[hp], b_ckvn[blk // 4]], [bvb])
                                for q in range(2):
                                    c = b2 * 2 + q
                                    act(junk3[:, :], bv[:, q * 256:q * 256 + 128], AF.Square, [bvb], [b_rkt[gp]],
                                        accum_out=rkt[gp][:, c:c + 1])
                                cp("dve", Vg[gp][:, b2 * 2:(b2 + 1) * 2, :],
                                   bv[:, :].rearrange("p (a b) -> p a b", a=2)[:, :, 128:256], [bvb], [b_Vg[gp][b2 // 2]])
                                yield
                            tt("dve", rkt[gp][:, :], rkt[gp][:, :], ssr[:, G * 16:(G + 1) * 16], ALU.add,
                               [b_rkt[gp]] + [b_ssr[4 * G + i] for i in range(4)], [b_rkt[gp]])
                            act(rkt[gp][:, :], rkt[gp][:, :], AF.Ln, [b_rkt[gp], b_const], [b_rkt[gp]], scale=1.0, bias=cst[:, 1:2])
                            act(rk_all[:, h, G * 16:(G + 1) * 16], rkt[gp][:, :], AF.Exp, [b_rkt[gp]], [b_rkall[h][G]], scale=-0.5)
                            if DBG and h == 0 and hf == 0 and G == 0:
                                S.dma("sp", dbg_k, KTg[gp][:], reads=b_KTg[gp])
                                S.dma("sp", dbg_v, Vg[gp][:], reads=b_Vg[gp])
                                S.dma("sp", dbg_rk, rk_all[:, h, 0:16], reads=[b_rkall[h][G]])

                        def item_A(h, G, filler=None):
                            hp = h % 2
                            gp = (h * NG + G) % 2
                            steps = []
                            for ot in range(2):
                                s = s_lo + ot
                                if s >= G:
                                    for kb in range(16):
                                        steps.append((ot, s, kb))
                            LAG = 2
                            pend = []
                            for i in range(len(steps) + LAG):
                                if i < len(steps):
                                    ot, s, kb = steps[i]
                                    cs = slice(ot * 512, (ot + 1) * 512)
                                    gblk = 16 * G + kb
                                    bs_, bsb = r3.next()
                                    mm(bs_[:, :], KTg[gp][:, kb * 128:(kb + 1) * 128], QTn[hp][:, cs], True, False,
                                       [b_KTg[gp][kb // 4], b_QT[hp][ot]], [bsb])
                                    mm(bs_[:, :], krT[0:64, gblk * 128:(gblk + 1) * 128], QTr[hp][:, cs], False, True,
                                       [b_kr[gblk // 4], b_QT[hp][ot]], [bsb])
                                    Pt, Ptb = pring.next()
                                    act(Pt[:, :], bs_[:, :], AF.Exp, [bsb, b_rkall[h][G]], [Ptb], scale=rk_all[:, h, gblk:gblk + 1])
                                    if G == s:
                                        tt("dve", Pt[:, :], Pt[:, :], masks[:, kb, :], ALU.mult, [b_masks, Ptb], [Ptb])
                                    pend.append((ot, s, kb, Pt, Ptb))
                                if i >= LAG:
                                    ot, s, kb, Pt, Ptb = pend[i - LAG]
                                    first = (G == 0 and kb == 0)
                                    last = (G == s and kb == 15)
                                    mm(OT[ot][0][:, :], Vg[gp][:, kb, :], Pt[:, :], first, last, [b_Vg[gp][kb // 4], Ptb],
                                       [OT[ot][1]])
                                    if G == s:
                                        mm(SM[ot][0][:, :], ones_bf[:, :], Pt[:, :], kb == 0, kb == 15 and s == 0,
                                           [b_const, Ptb], [SM[ot][1]])
                                        if kb == 15 and s > 0:
                                            mm(SM[ot][0][:, :], ones_f[:, :], accs[hp][ot][:, :], False, True,
                                               [b_const, b_accs[hp][ot]], [SM[ot][1]])
                                    else:
                                        stt("dve", accs[hp][ot][:, :], Pt[:, :], 1.0, accs[hp][ot][:, :], ALU.mult, ALU.add,
                                            [b_accs[hp][ot], Ptb], [b_accs[hp][ot]])
                                if filler is not None:
                                    next(filler, None)
                            if filler is not None:
                                for _ in filler:
                                    pass

                        def item_F(h):
                            hp = h % 2
                            for ot in range(2):
                                cp("act", ssb[ot][:, :], SM[ot][0][:, :], [SM[ot][1]], [b_ssb[ot]])
                                cp("dve", osb[ot][:, :], OT[ot][0][:, :], [OT[ot][1]], [b_osb[ot]])
                            for ot in range(2):
                                cs = slice(ot * 512, (ot + 1) * 512)
                                op("dve", lambda e: e.reciprocal(out=ssb[ot][:, :], in_=ssb[ot][:, :]), [b_ssb[ot]], [b_ssb[ot]])
                                tt("pool", osb[ot][:, :], osb[ot][:, :], ssb[ot][:, :], ALU.mult, [b_osb[ot], b_ssb[ot]], [b_osb[ot]])
                                tt("pool", yTa[:, h, cs], osb[ot][:, :], szaT[:, h, cs], ALU.mult, [b_osb[ot], b_sza[h][ot]],
                                   [b_yTa[h][ot]])

                        import itertools
                        for _ in item_Q(0):
                            pass
                        for _ in item_G(0, 0):
                            pass
                        for h in range(8):
                            for G in range(NG):
                                if G + 1 < NG:
                                    filler = item_G(h, G + 1)
                                elif h + 1 < 8:
                                    filler = itertools.chain(item_Q(h + 1), item_G(h + 1, 0))
                                else:
                                    filler = None
                                item_A(h, G, filler)
                            item_F(h)

                        S.barrier()
                        ck("3")

                    if DBG:
                        S.dma("sp", dbg_yc[hf], yTc[:], reads=[b for l in b_yTc for b in l])
                        S.dma("sp", dbg_ya[hf], yTa[:], reads=[b for l in b_yTa for b in l])
                    with ExitStack() as es:
                        wo = [sb(es, f"wo{i}", [128, KC, 512], BF16) for i in range(2)]
                        b_wo = bufs("wo", 2, 4)
                        NX = 4
                        xres = [sb(es, f"xres{i}", [128, 512], F32) for i in range(NX)]
                        b_xres = bufs("xres", NX)
                        o1 = [sb(es, f"o1{i}", [128, 512], F32) for i in range(2)]
                        b_o1 = bufs("o1", 2)
                        o2 = [sb(es, f"o2{i}", [128, 512], F32) for i in range(3)]
                        b_o2 = bufs("o2", 3)
                        items = [(ct, tb) for ct in range(4) for tb in range(8)]

                        def load_x(n):
                            ct, tb = items[n]
                            r0 = hf * 1024 + tb * 128
                            S.dma("sp", xres[n % NX][:], x_own[r0:r0 + 128, ct * 512:(ct + 1) * 512], writes=[b_xres[n % NX]])

                        def load_wo(ct):
                            for qd in range(4):
                                S.dma("pool", wo[ct % 2][:, qd * 4:(qd + 1) * 4, :], w_out_l[ct][:, qd * 4:(qd + 1) * 4, :],
                                      writes=[b_wo[ct % 2][qd]])

                        load_wo(0)
                        load_x(0)
                        load_x(1)
                        for n, (ct, tb) in enumerate(items):
                            w, wb = wo[ct % 2], b_wo[ct % 2]
                            if tb == 0 and ct + 1 < 4:
                                load_wo(ct + 1)
                            if n + 2 < len(items):
                                load_x(n + 2)
                            r0 = hf * 1024 + tb * 128
                            ot = tb // 4
                            ts_ = slice(tb * 128, (tb + 1) * 128)
                            xi, oi, pi = n % NX, n % 3, n % 2
                            bo, bob = bank_ring.next()
                            for kc in range(KC):
                                if kc < 8:
                                    l, lb = yTc[:, kc, ts_], b_yTc[kc][ot]
                                else:
                                    l, lb = yTa[:, kc - 8, ts_], b_yTa[kc - 8][ot]
                                mm(bo[:, :], l, w[:, kc, :], kc == 0, kc == KC - 1, [lb, wb[kc // 4]], [bob])
                            tt("dve", o1[pi][:, :], bo[:, :], gate_bc[:, ct * 512:(ct + 1) * 512], ALU.mult, [bob, b_gate],
                               [b_o1[pi]])
                            tt("pool", o2[oi][:, :], o1[pi][:, :], xres[xi][:, :], ALU.add, [b_o1[pi], b_xres[xi]],
                               [b_o2[oi]])
                            S.dma("sp", out_d[r0:r0 + 128, ct * 512:(ct + 1) * 512], o2[oi][:, :], reads=[b_o2[oi]])
                        S.barrier()
                        ck("4")

    except _Stop:
        pass
    S.off = False
    S.barrier()
    top.close()
    return nc


def _prep_inputs(x, c, positions, ada_w, ada_b, norm_g, w_in, conv_w, q_a_g, w_q_b, kv_a_g, w_kv_b, q_g, k_g, w_out):
    f = np.float32
    x = np.asarray(x, f)
    B, SEQ, _ = x.shape
    NT = SEQ // 512
    NSLOT = NT // 4
    NBLK = SEQ // 128
    positions = np.asarray(positions, np.int32)
    ada_w = np.asarray(ada_w, f)[0]
    w_in = np.asarray(w_in, f)[0]
    w_q_b = np.asarray(w_q_b, f)[0]
    w_kv_b = np.asarray(w_kv_b, f)[0]
    w_out = np.asarray(w_out, f)[0]

    def cols(v, n):
        return np.ascontiguousarray(np.asarray(v, f).reshape(n, 128).T)

    def wl(w):
        n = w.shape[1] // 128
        return w.reshape(KC, 128, n, 128).transpose(2, 1, 0, 3)

    perm = (np.arange(64) + 32) % 64
    wr = w_in[:, 4864:4928]
    chunks = [wl(w_in[:, 4608:4864]), wl(np.concatenate([wr, wr[:, perm]], axis=1))]
    conv = np.stack([w_in[:, 0:1024], w_in[:, 2048:3072], w_in[:, 1024:2048], w_in[:, 3072:4096]], 0)
    conv = conv.reshape(4, D, 8, 128).transpose(2, 0, 1, 3).reshape(32, D, 128)
    chunks.append(conv.reshape(32, KC, 128, 128).transpose(0, 2, 1, 3))
    chunks.append(wl(w_in[:, 4096:4608]))
    chunks.append(wl(w_in[:, 4928:5952]))
    w_in_l = np.ascontiguousarray(np.concatenate(chunks, 0))
    assert w_in_l.shape == (47, 128, KC, 128)
    ada_w_l = np.ascontiguousarray(ada_w.reshape(KC, 128, 12, 512).transpose(2, 1, 0, 3))
    w_out_l = np.ascontiguousarray(w_out.reshape(KC, 128, 4, 512).transpose(2, 1, 0, 3))
    wq = w_q_b.reshape(4, 128, 8, 192).transpose(2, 1, 0, 3)
    wqb_l = np.ascontiguousarray(np.concatenate([wq, wq[..., 128:192][..., perm]], -1))
    wkvb_l = np.ascontiguousarray(w_kv_b.reshape(2, 128, 8, 256).transpose(2, 1, 0, 3))

    def g3(g):
        g = np.asarray(g, f)[0]
        o = np.zeros((128, 3), f)
        o[:, 0] = g[0:128]
        o[0:64, 1] = g[128:192]
        o[0:64, 2] = g[128:192][perm]
        return o

    inv_freq = (10000.0 ** (-(np.arange(0, 64, 2, dtype=np.float64)) / 64.0)).astype(f)
    rope_c = np.zeros((64, 4), f)
    rope_c[:, 0] = np.concatenate([inv_freq, inv_freq])
    rope_c[:, 1] = np.concatenate([-np.ones(32), np.ones(32)]) * SHRINK
    rope_c[:, 2] = SHRINK
    kidx = (np.arange(NBLK)[None, :] * 128 + np.arange(128)[:, None]).astype(f)
    shared = dict(
        ada_w_l=ada_w_l, ada_b=np.asarray(ada_b, f).reshape(1, 6144), ng_col=cols(np.asarray(norm_g)[0], KC),
        w_in_l=w_in_l, convw_col=np.ascontiguousarray(np.asarray(conv_w, f)[0].reshape(3, 8, 128).transpose(2, 1, 0)),
        qag_col=cols(np.asarray(q_a_g)[0], 4), kvag_col=cols(np.asarray(kv_a_g)[0], 2), wqb_l=wqb_l, wkvb_l=wkvb_l,
        qg_col=g3(q_g), kg_col=g3(k_g), w_out_l=w_out_l, rope_c=rope_c, kidx=kidx)
    in_maps = []
    meta = []
    for core in range(NCORES):
        b, j = core // 4, core % 4
        tiles = [4 * s + j for s in range(NSLOT)]
        rows = np.concatenate([np.arange(T * 512, (T + 1) * 512) for T in tiles])
        halo = np.zeros((NSLOT * 2, D), f)
        hv = np.zeros((128, NSLOT * 2), f)
        for s, T in enumerate(tiles):
            if T > 0:
                halo[2 * s:2 * s + 2] = x[b, T * 512 - 2:T * 512]
                hv[:, 2 * s:2 * s + 2] = 1.0
        m = dict(shared)
        m.update(x_all=x[b], x_own=np.ascontiguousarray(x[b, rows]), x_halo=halo, halo_valid=hv,
                 pos_all=np.ascontiguousarray(positions[b][None, :]), pos_own=np.ascontiguousarray(positions[b, rows][None, :]),
                 qidx=rows.astype(f)[None, :], c_col=cols(np.asarray(c, f)[b], KC))
        in_maps.append(m)
        meta.append((b, rows))
    return in_maps, meta, (B, SEQ)


_NC_CACHE = {}


def kernel(**inputs):
    in_maps, meta, (B, SEQ) = _prep_inputs(**inputs)
    if SEQ not in _NC_CACHE:
        _NC_CACHE[SEQ] = build(SEQ)
    nc = _NC_CACHE[SEQ]
    res = run_bass_kernel_spmd(nc, in_maps, core_ids=list(range(NCORES)))
    out = np.empty((B, SEQ, D), np.float32)
    for core, (b, rows) in enumerate(meta):
        out[b, rows] = res.results[core]["out"]
    if DBG_HOOK is not None:
        DBG_HOOK(res)
    return out
```
